# Optimizing a Trainium2 kernel written in Bass

```python
import jax, jax.numpy as jnp
from jax import lax
import numpy as np

D_MODEL = 1024
BATCH = 16
SEQ = 2048
DEPTH = 1
DEC_BATCH = 128
DEC_SEQ = 8
PAST_LEN = 16384
PAGE_SIZE = 128

PLE_DIM = 256
D_FF = 2816
NORM_EPS = 1e-6
GDN_HEADS = 4
GDN_DK = 128
GDN_DV = 128
GDN_CONV = 4
GDN_CHUNK = 64
SWA_HEADS = 8
SWA_KV_HEADS = 2
SWA_HD = 64
WINDOW = 128
ROT_DIM = SWA_HD // 4
ROPE_THETA = 500000.0

GDN_QK_W = GDN_HEADS * GDN_DK
GDN_V_W = GDN_HEADS * GDN_DV
GDN_CONV_W = 2 * GDN_QK_W + GDN_V_W
SWA_Q_W = SWA_HEADS * SWA_HD
SWA_KV_W = SWA_KV_HEADS * SWA_HD
MIX_W = GDN_V_W + SWA_Q_W
IN_SIZES = (GDN_CONV_W, GDN_V_W, GDN_HEADS, GDN_HEADS, SWA_Q_W, SWA_KV_W, SWA_KV_W)
IN_W = sum(IN_SIZES)

kernel_name = 'hymba_gdn_swa_macaron_decode_step'


def rmsnorm(x, g):
    xf = x.astype(jnp.float32)
    xf = xf * lax.rsqrt(jnp.mean(xf * xf, axis=-1, keepdims=True) + NORM_EPS)
    return (xf * g.astype(jnp.float32)).astype(x.dtype)


def swiglu(x, w_gu, w_down):
    gate, up = jnp.split(x @ w_gu, 2, axis=-1)
    return (jax.nn.silu(gate) * up) @ w_down


def l2norm(x):
    return x * lax.rsqrt(jnp.sum(x * x, axis=-1, keepdims=True) + 1e-6)


def causal_conv(x, buf, w):
    T = x.shape[1]
    full = jnp.concatenate([buf.astype(x.dtype), x], axis=1)
    out = full[:, 0:T] * w[0]
    for j in range(1, GDN_CONV):
        out = out + full[:, j:j + T] * w[j]
    return jax.nn.silu(out), full[:, -(GDN_CONV - 1):]


def rope_partial(x, pos):
    half = ROT_DIM // 2
    inv = ROPE_THETA ** (-jnp.arange(half, dtype=jnp.float32) * 2.0 / ROT_DIM)
    ang = pos.astype(jnp.float32)[:, None] * inv[None, :]
    cos = jnp.cos(ang)[None, :, None, :]
    sin = jnp.sin(ang)[None, :, None, :]
    xf = x.astype(jnp.float32)
    x1 = xf[..., :half]
    x2 = xf[..., half:ROT_DIM]
    out = jnp.concatenate([x1 * cos - x2 * sin, x2 * cos + x1 * sin, xf[..., ROT_DIM:]], axis=-1)
    return out.astype(x.dtype)


def sink_softmax(s, mask, sinks):
    s = jnp.where(mask, s, -jnp.inf)
    sk = sinks.astype(jnp.float32).reshape(SWA_KV_HEADS, SWA_HEADS // SWA_KV_HEADS)[:, :, None, None]
    m = jnp.maximum(jnp.max(s, axis=-1, keepdims=True), sk)
    p = jnp.exp(s - m)
    return p / (jnp.sum(p, axis=-1, keepdims=True) + jnp.exp(sk - m))


def swa_prompt(q, k, v, sinks):
    B, T = q.shape[:2]
    G = SWA_HEADS // SWA_KV_HEADS
    NB = T // WINDOW
    qb = q.reshape(B, NB, WINDOW, SWA_KV_HEADS, G, SWA_HD)
    pad = ((0, 0), (WINDOW, 0), (0, 0), (0, 0))
    kp = jnp.pad(k, pad).reshape(B, NB + 1, WINDOW, SWA_KV_HEADS, SWA_HD)
    vp = jnp.pad(v, pad).reshape(B, NB + 1, WINDOW, SWA_KV_HEADS, SWA_HD)
    kk = jnp.concatenate([kp[:, :-1], kp[:, 1:]], axis=2)
    vv = jnp.concatenate([vp[:, :-1], vp[:, 1:]], axis=2)
    s = jnp.einsum('bnqkgd,bnskd->bnkgqs', qb, kk, preferred_element_type=jnp.float32) * (SWA_HD ** -0.5)
    blk = jnp.arange(NB)[:, None] * WINDOW
    qpos = blk + jnp.arange(WINDOW)[None, :]
    kpos = blk - WINDOW + jnp.arange(2 * WINDOW)[None, :]
    diff = qpos[:, :, None] - kpos[:, None, :]
    mask = (diff >= 0) & (diff < WINDOW) & (kpos[:, None, :] >= 0)
    p = sink_softmax(s, mask[:, None, None], sinks)
    o = jnp.einsum('bnkgqs,bnskd->bnqkgd', p.astype(vv.dtype), vv)
    return o.reshape(B, T, SWA_Q_W)


def swa_sample(q, k, v, k_buf, v_buf, sinks):
    B, T = q.shape[:2]
    G = SWA_HEADS // SWA_KV_HEADS
    wb = k_buf.shape[1]
    kk = jnp.concatenate([k_buf.astype(k.dtype), k], axis=1)
    vv = jnp.concatenate([v_buf.astype(v.dtype), v], axis=1)
    qg = q.reshape(B, T, SWA_KV_HEADS, G, SWA_HD)
    s = jnp.einsum('bqkgd,bskd->bkgqs', qg, kk, preferred_element_type=jnp.float32) * (SWA_HD ** -0.5)
    qpos = PAST_LEN + jnp.arange(T)
    kpos = jnp.concatenate([PAST_LEN - wb + jnp.arange(wb), qpos])
    diff = qpos[:, None] - kpos[None, :]
    mask = (diff >= 0) & (diff < WINDOW)
    p = sink_softmax(s, mask, sinks)
    o = jnp.einsum('bkgqs,bskd->bqkgd', p.astype(vv.dtype), vv)
    return o.reshape(B, T, SWA_Q_W), kk[:, -wb:], vv[:, -wb:]


def gated_delta_chunked(q, k, v, g, beta, S0):
    B, T, H, _ = q.shape
    dv = v.shape[-1]
    C = min(GDN_CHUNK, T)
    N = -(-T // C)
    pad = N * C - T

    def prep(x):
        x = jnp.pad(x, [(0, 0), (0, pad)] + [(0, 0)] * (x.ndim - 2))
        x = x.reshape((B, N, C) + x.shape[2:])
        return jnp.moveaxis(x, (1, 3), (0, 2))

    qc, kc, vc, gcs, bc = prep(q), prep(k), prep(v), prep(g), prep(beta)
    gc = jnp.cumsum(gcs, axis=-1)
    tri = jnp.tril(jnp.ones((C, C), bool))
    strict = jnp.tril(jnp.ones((C, C), bool), -1)
    decay = jnp.exp(jnp.where(tri, gc[..., :, None] - gc[..., None, :], -jnp.inf))
    kb = kc * bc[..., None]
    vb = vc * bc[..., None]
    A = jnp.where(strict, jnp.einsum('nbhid,nbhjd->nbhij', kb, kc) * decay, 0.0)
    Tm = A + jnp.eye(C, dtype=A.dtype)
    u = lax.linalg.triangular_solve(Tm, vb, left_side=True, lower=True)
    w = lax.linalg.triangular_solve(Tm, kb * jnp.exp(gc)[..., None], left_side=True, lower=True)
    qk = jnp.einsum('nbhid,nbhjd->nbhij', qc, kc) * decay
    glast = gc[..., -1]
    kdec = kc * jnp.exp(glast[..., None] - gc)[..., None]
    qdec = qc * jnp.exp(gc)[..., None]

    def step(S, xs):
        qd, kd, u_, w_, qk_, gl = xs
        v_new = u_ - jnp.einsum('bhcd,bhde->bhce', w_, S)
        o = jnp.einsum('bhcd,bhde->bhce', qd, S) + jnp.einsum('bhij,bhje->bhie', qk_, v_new)
        S = S * jnp.exp(gl)[..., None, None] + jnp.einsum('bhcd,bhce->bhde', kd, v_new)
        return S, o

    S, o = lax.scan(step, S0, (qdec, kdec, u, w, qk, glast))
    o = jnp.moveaxis(o, (0, 2), (1, 3)).reshape(B, N * C, H, dv)[:, :T]
    return o, S


def layer(x, p, lp, S0, conv_buf, kv_buf, pos0):
    B, T, _ = x.shape
    f32 = jnp.float32
    h = x + 0.5 * swiglu(rmsnorm(x, lp['norm_ffn1']), lp['ffn1_gu'], lp['ffn1_down'])
    n = rmsnorm(h, lp['norm_mix'])
    proj = n @ lp['w_in']
    qkv_pre, z, a, b, q_s, k_s, v_s = jnp.split(proj, np.cumsum(IN_SIZES)[:-1].tolist(), axis=-1)
    qkv, new_conv = causal_conv(qkv_pre, conv_buf, lp['conv_w'])
    qg, kg, vg = jnp.split(qkv.astype(f32), [GDN_QK_W, 2 * GDN_QK_W], axis=-1)
    qg = l2norm(qg.reshape(B, T, GDN_HEADS, GDN_DK)) * (GDN_DK ** -0.5)
    kg = l2norm(kg.reshape(B, T, GDN_HEADS, GDN_DK))
    vg = vg.reshape(B, T, GDN_HEADS, GDN_DV)
    gdec = -jnp.exp(lp['a_log'].astype(f32)) * jax.nn.softplus(a.astype(f32) + lp['dt_bias'].astype(f32))
    beta = jax.nn.sigmoid(b.astype(f32))
    o_g, S_new = gated_delta_chunked(qg, kg, vg, gdec, beta, S0.astype(f32))
    o_g = rmsnorm(o_g, lp['gdn_norm']) * jax.nn.silu(z.reshape(B, T, GDN_HEADS, GDN_DV).astype(f32))
    o_g = o_g.reshape(B, T, GDN_V_W).astype(x.dtype)
    pos = pos0 + jnp.arange(T)
    q_s = rope_partial(q_s.reshape(B, T, SWA_HEADS, SWA_HD), pos)
    k_s = rope_partial(k_s.reshape(B, T, SWA_KV_HEADS, SWA_HD), pos)
    v_s = v_s.reshape(B, T, SWA_KV_HEADS, SWA_HD)
    if kv_buf is None:
        o_s = swa_prompt(q_s, k_s, v_s, lp['sinks'])
        new_k, new_v = k_s[:, -WINDOW:], v_s[:, -WINDOW:]
    else:
        o_s, new_k, new_v = swa_sample(q_s, k_s, v_s, kv_buf[0], kv_buf[1], lp['sinks'])
    h = h + jnp.concatenate([o_g, o_s], axis=-1) @ lp['w_out']
    h = h + 0.5 * swiglu(rmsnorm(h, lp['norm_ffn2']), lp['ffn2_gu'], lp['ffn2_down'])
    h = h + jax.nn.sigmoid(rmsnorm(h, lp['norm_ple']) @ lp['ple_gate']) * (p @ lp['ple_proj'])
    return h, S_new.astype(S0.dtype), new_conv, new_k, new_v


def setup_inputs(seed: int = 0) -> dict:
    key = jax.random.key(seed)
    ks = jax.random.split(key, 32)
    f32 = jnp.float32

    def nrm(k, shape, scale):
        return jax.random.normal(k, shape, f32) * scale

    def gain(k, dim):
        return 1.0 + 0.02 * jax.random.normal(k, (DEPTH, dim), f32)

    wb = min(WINDOW, PAST_LEN)
    return {
        'x_prompt': nrm(ks[0], (BATCH, SEQ, D_MODEL), 1.0),
        'x_sample': nrm(ks[1], (DEC_BATCH, DEC_SEQ, D_MODEL), 1.0),
        'state_gdn': nrm(ks[2], (DEPTH, DEC_BATCH, GDN_HEADS, GDN_DK, GDN_DV), GDN_DK ** -0.5),
        'state_conv': nrm(ks[3], (DEPTH, DEC_BATCH, GDN_CONV - 1, GDN_CONV_W), 1.0),
        'cache_swa_k': nrm(ks[4], (DEPTH, DEC_BATCH, wb, SWA_KV_HEADS, SWA_HD), 1.0),
        'cache_swa_v': nrm(ks[5], (DEPTH, DEC_BATCH, wb, SWA_KV_HEADS, SWA_HD), 1.0),
        'p_prompt': nrm(ks[6], (DEPTH, BATCH, SEQ, PLE_DIM), 1.0),
        'p_sample': nrm(ks[7], (DEPTH, DEC_BATCH, DEC_SEQ, PLE_DIM), 1.0),
        'norm_ffn1': gain(ks[8], D_MODEL),
        'ffn1_gu': nrm(ks[9], (DEPTH, D_MODEL, 2 * D_FF), D_MODEL ** -0.5),
        'ffn1_down': nrm(ks[10], (DEPTH, D_FF, D_MODEL), D_FF ** -0.5),
        'norm_mix': gain(ks[11], D_MODEL),
        'w_in': nrm(ks[12], (DEPTH, D_MODEL, IN_W), D_MODEL ** -0.5),
        'conv_w': nrm(ks[13], (DEPTH, GDN_CONV, GDN_CONV_W), GDN_CONV ** -0.5),
        'a_log': jnp.log(jax.random.uniform(ks[14], (DEPTH, GDN_HEADS), f32, 1.0, 16.0)),
        'dt_bias': nrm(ks[15], (DEPTH, GDN_HEADS), 0.1),
        'gdn_norm': gain(ks[16], GDN_DV),
        'sinks': nrm(ks[17], (DEPTH, SWA_HEADS), 1.0),
        'w_out': nrm(ks[18], (DEPTH, MIX_W, D_MODEL), MIX_W ** -0.5),
        'norm_ffn2': gain(ks[19], D_MODEL),
        'ffn2_gu': nrm(ks[20], (DEPTH, D_MODEL, 2 * D_FF), D_MODEL ** -0.5),
        'ffn2_down': nrm(ks[21], (DEPTH, D_FF, D_MODEL), D_FF ** -0.5),
        'norm_ple': gain(ks[22], D_MODEL),
        'ple_proj': nrm(ks[23], (DEPTH, PLE_DIM, D_MODEL), PLE_DIM ** -0.5),
        'ple_gate': nrm(ks[24], (DEPTH, D_MODEL, D_MODEL), D_MODEL ** -0.5),
        'norm_final': 1.0 + 0.02 * jax.random.normal(ks[25], (D_MODEL,), f32),
    }


def reference(x_prompt, x_sample, state_gdn, state_conv, cache_swa_k, cache_swa_v, p_prompt, p_sample,
              norm_ffn1, ffn1_gu, ffn1_down, norm_mix, w_in, conv_w, a_log, dt_bias, gdn_norm, sinks,
              w_out, norm_ffn2, ffn2_gu, ffn2_down, norm_ple, ple_proj, ple_gate, norm_final):
    hp, hs = x_prompt, x_sample
    sg_p, sc_p, kk_p, vv_p = [], [], [], []
    sg_s, sc_s, kk_s, vv_s = [], [], [], []
    for i in range(DEPTH):
        lp = {'norm_ffn1': norm_ffn1[i], 'ffn1_gu': ffn1_gu[i], 'ffn1_down': ffn1_down[i],
              'norm_mix': norm_mix[i], 'w_in': w_in[i], 'conv_w': conv_w[i], 'a_log': a_log[i],
              'dt_bias': dt_bias[i], 'gdn_norm': gdn_norm[i], 'sinks': sinks[i], 'w_out': w_out[i],
              'norm_ffn2': norm_ffn2[i], 'ffn2_gu': ffn2_gu[i], 'ffn2_down': ffn2_down[i],
              'norm_ple': norm_ple[i], 'ple_proj': ple_proj[i], 'ple_gate': ple_gate[i]}
        B = hp.shape[0]
        S0 = jnp.zeros((B, GDN_HEADS, GDN_DK, GDN_DV), state_gdn.dtype)
        cb0 = jnp.zeros((B, GDN_CONV - 1, GDN_CONV_W), hp.dtype)
        hp, s1, c1, k1, v1 = layer(hp, p_prompt[i], lp, S0, cb0, None, 0)
        sg_p.append(s1); sc_p.append(c1); kk_p.append(k1); vv_p.append(v1)
        hs, s2, c2, k2, v2 = layer(hs, p_sample[i], lp, state_gdn[i], state_conv[i],
                                   (cache_swa_k[i], cache_swa_v[i]), PAST_LEN)
        sg_s.append(s2); sc_s.append(c2); kk_s.append(k2); vv_s.append(v2)
    y_prompt = rmsnorm(hp, norm_final)
    y_sample = rmsnorm(hs, norm_final)
    return (y_prompt, y_sample,
            jnp.stack(sg_p), jnp.stack(sc_p), jnp.stack(kk_p), jnp.stack(vv_p),
            jnp.stack(sg_s), jnp.stack(sc_s), jnp.stack(kk_s), jnp.stack(vv_s))
```

```python
import contextlib
import numpy as np
import concourse.bass as bass
import concourse.mybir as mybir
from concourse.bass_utils import run_bass_kernel_spmd

F32 = mybir.dt.float32
BF16 = mybir.dt.bfloat16
AF = mybir.ActivationFunctionType
ALU = mybir.AluOpType
AX = mybir.AxisListType

NCORES = 8
D = 1024
DFF = 2816
NFF = 22
INW = 2824
EPS = 1e-6


class Buf:
    def __init__(self, ap, uid, reg=None, psum=False):
        self.ap = ap
        self.uid = uid
        self.reg = reg
        self.psum = psum

    def __getitem__(self, k):
        return Buf(self.ap[k], self.uid, self.reg, self.psum)

    def v(self, fn):
        return Buf(fn(self.ap), self.uid, self.reg, self.psum)


class Op:
    __slots__ = ("eng", "fn", "reads", "writes", "dma", "idx", "pos", "waits", "inc", "slot", "slotval", "pre", "cost")

    def __init__(self, eng, fn, reads, writes, dma):
        self.eng, self.fn, self.reads, self.writes, self.dma = eng, fn, reads, writes, dma
        self.waits = []
        self.inc = None
        self.pre = None


def _ap(x):
    return x.ap if isinstance(x, Buf) else x


class Prog:
    ENGS = ["pe", "act", "dve", "pool", "sp"]
    NSLOT = 12
    WINDOW = 5
    KEEP_STREAM_ORDER = True

    def __init__(self, nc, es):
        self.nc, self.es = nc, es
        self.ops = []
        self.nuid = 0

    def newbuf(self, ap):
        self.nuid += 1
        return Buf(ap, self.nuid)

    def sb(self, name, shape, dt):
        t = self.es.enter_context(self.nc.sbuf_tensor(name, list(shape), dt))
        return self.newbuf(t[:])

    def ps(self, name, shape, dt):
        t = self.es.enter_context(self.nc.psum_tensor(name, list(shape), dt))
        b = self.newbuf(t[:])
        b.psum = True
        return b

    def alias(self, buf):
        self.nuid += 1
        return Buf(buf.ap, self.nuid, buf.reg)

    def make_arena(self, name, nbytes):
        t = self.es.enter_context(self.nc.sbuf_tensor(name, [128, nbytes // 2], BF16))
        self.nuid += 1
        self.arena_ap = t[:]
        self.arena_reg = self.nuid
        self.arena_bytes = nbytes
        self.fence_t = self.sb("fence_t", [128, 2], F32)
        self.regbuf = Buf(self.fence_t.ap, self.arena_reg)
        self.aoff = 0

    def arena_reset(self):
        self.aoff = 0

    def take(self, nelem, dt, parts=128):
        esz = 4 if dt == F32 else 2
        nb = (nelem * esz + 31) // 32 * 32
        assert self.aoff + nb <= self.arena_bytes, ("arena overflow", self.aoff, nb, self.arena_bytes)
        ap = self.arena_ap[0:parts, self.aoff // 2:(self.aoff + nelem * esz) // 2]
        if dt == F32:
            ap = ap.bitcast(F32)
        self.aoff += nb
        self.nuid += 1
        return Buf(ap, self.nuid, self.arena_reg)

    def fence(self):
        o = self.fence_t.ap
        self.add("pool", lambda e: e.memset(o, 0.0), [], [self.regbuf])

    def add(self, eng, fn, reads, writes, dma=False):
        r = [b.uid for b in reads if isinstance(b, Buf)]
        w = [b.uid for b in writes if isinstance(b, Buf)]
        w += [b.uid for b in reads if isinstance(b, Buf) and b.psum and b.uid not in w]
        for b in list(reads) + list(writes):
            if isinstance(b, Buf) and b.reg is not None and b.reg not in r:
                r.append(b.reg)
        op = Op(eng, fn, r, w, dma)
        op.idx = len(self.ops)
        n = 0
        for b in writes:
            if isinstance(b, Buf):
                k = 1
                for d in b.ap.shape[1:]:
                    k *= int(d)
                n = max(n, k)
        if eng == "pe":
            op.cost = 60.0 + 0.47 * n
        elif eng == "act":
            op.cost = 230.0 + 0.85 * n
        elif eng == "dve":
            op.cost = 70.0 + 1.1 * n
        elif eng == "pool":
            op.cost = 120.0 + 2.0 * n
        else:
            op.cost = 100.0
        if dma:
            op.cost = 2000.0
        self.ops.append(op)
        return op

    def mm(self, out, lhsT, rhs, start=True, stop=True, **kw):
        o, l, r = out.ap, lhsT.ap, rhs.ap
        self.add("pe", lambda e: e.matmul(o, l, r, start=start, stop=stop, **kw), [lhsT, rhs], [out])

    def tr(self, out, in_, ident):
        o, i, d = out.ap, in_.ap, ident.ap
        self.add("pe", lambda e: e.transpose(o, i, d), [in_, ident], [out])

    def act(self, out, in_, func, bias=0.0, scale=1.0, accum=None, eng="act"):
        o, i, b, s = out.ap, in_.ap, _ap(bias), _ap(scale)
        a = _ap(accum) if accum is not None else None
        self.add(eng, lambda e: e.activation(o, i, func, bias=b, scale=s, accum_out=a),
                 [in_, bias, scale], [out] + ([accum] if accum is not None else []))

    def ts(self, eng, out, in0, s1, s2=None, op0=ALU.mult, op1=None, accum=None):
        o, i, a1, a2 = out.ap, in0.ap, _ap(s1), _ap(s2)
        ac = _ap(accum) if accum is not None else None
        if op1 is None:
            f = lambda e: e.tensor_scalar(o, i, a1, None, op0)
        elif ac is None:
            f = lambda e: e.tensor_scalar(o, i, a1, a2, op0, op1)
        else:
            f = lambda e: e.tensor_scalar(o, i, a1, a2, op0, op1, accum_out=ac)
        self.add(eng, f, [in0, s1, s2], [out] + ([accum] if accum is not None else []))

    def tt(self, eng, out, in0, in1, op):
        o, a, b = out.ap, in0.ap, in1.ap
        self.add(eng, lambda e: e.tensor_tensor(o, a, b, op), [in0, in1], [out])

    def stt(self, eng, out, in0, scalar, in1, op0, op1):
        o, a, s, b = out.ap, in0.ap, _ap(scalar), in1.ap
        self.add(eng, lambda e: e.scalar_tensor_tensor(o, a, s, b, op0, op1), [in0, scalar, in1], [out])

    def copy(self, eng, out, in_):
        o, i = out.ap, in_.ap
        if eng == "act":
            self.add(eng, lambda e: e.copy(o, i), [in_], [out])
        else:
            self.add(eng, lambda e: e.tensor_copy(o, i), [in_], [out])

    def memset(self, eng, out, val):
        o = out.ap
        self.add(eng, lambda e: e.memset(o, val), [], [out])

    def recip(self, out, in_):
        o, i = out.ap, in_.ap
        self.add("dve", lambda e: e.reciprocal(o, i), [in_], [out])

    def dma(self, q, out, in_):
        o, i = _ap(out), _ap(in_)
        self.add(q, lambda e: e.dma_start(out=o, in_=i), [in_], [out], dma=True)

    def capture(self):
        self._saved = self.ops
        self.ops = []

    def end_capture(self):
        L = self.ops
        self.ops = self._saved
        return L

    def merge(self, LA, LB):
        na, nb = len(LA), len(LB)
        ia = ib = 0
        while ia < na or ib < nb:
            if ib >= nb or (ia < na and ia * nb <= ib * na):
                self.ops.append(LA[ia])
                ia += 1
            else:
                self.ops.append(LB[ib])
                ib += 1

    def merge_n(self, Ls):
        Ls = [L for L in Ls if L]
        pos = [0] * len(Ls)
        total = sum(len(L) for L in Ls)
        for _ in range(total):
            best, bi = None, -1
            for i, L in enumerate(Ls):
                if pos[i] < len(L):
                    f = (pos[i] + 0.5) / len(L)
                    if best is None or f < best:
                        best, bi = f, i
            self.ops.append(Ls[bi][pos[bi]])
            pos[bi] += 1

    SCHED_LAT = 600.0

    def merge_sched(self, Ls, lat=None):
        lat = self.SCHED_LAT if lat is None else lat
        ops = [op for L in Ls for op in L]
        n = len(ops)
        lastw, readers = {}, {}
        deps = [set() for _ in range(n)]
        for i, op in enumerate(ops):
            for u in op.reads:
                if u in lastw:
                    deps[i].add(lastw[u])
            for u in op.writes:
                if u in lastw:
                    deps[i].add(lastw[u])
                for r in readers.get(u, ()):
                    deps[i].add(r)
            deps[i].discard(i)
            for u in op.reads:
                readers.setdefault(u, []).append(i)
            for u in op.writes:
                lastw[u] = i
                readers[u] = []
        sid = []
        for si, L in enumerate(Ls):
            sid += [si] * len(L)
        last_on = {}
        if self.KEEP_STREAM_ORDER:
            for i, op in enumerate(ops):
                key = (sid[i], op.eng)
                if key in last_on:
                    deps[i].add(last_on[key])
                last_on[key] = i
        nsucc_ready = [len(d) for d in deps]
        succ = [[] for _ in range(n)]
        for i, d in enumerate(deps):
            for j in d:
                succ[j].append(i)
        finish = [0.0] * n
        ready_t = [0.0] * n
        teng = {}
        avail = [i for i in range(n) if nsucc_ready[i] == 0]
        order = []
        while avail:
            best, bt = None, None
            for i in avail:
                t = max(ready_t[i], teng.get(ops[i].eng, 0.0))
                if bt is None or t < bt or (t == bt and i < best):
                    best, bt = i, t
            avail.remove(best)
            op = ops[best]
            finish[best] = bt + op.cost
            teng[op.eng] = finish[best] if not op.dma else bt + 100.0
            order.append(best)
            for j in succ[best]:
                rt = finish[best] + (lat if ops[j].eng != op.eng else 0.0)
                if rt > ready_t[j]:
                    ready_t[j] = rt
                nsucc_ready[j] -= 1
                if nsucc_ready[j] == 0:
                    avail.append(j)
        assert len(order) == n
        for i in order:
            self.ops.append(ops[i])

    def finalize(self):
        ops = self.ops
        for i, op in enumerate(ops):
            op.idx = i
        per = {e: [] for e in self.ENGS}
        for op in ops:
            op.pos = len(per[op.eng])
            per[op.eng].append(op)
        lastw, readers = {}, {}
        deps_all = []
        for op in ops:
            deps = set()
            for u in op.reads:
                if u in lastw:
                    deps.add(lastw[u])
            for u in op.writes:
                if u in lastw:
                    deps.add(lastw[u])
                for r in readers.get(u, ()):
                    deps.add(r)
            deps.discard(op.idx)
            for u in op.reads:
                readers.setdefault(u, []).append(op.idx)
            for u in op.writes:
                lastw[u] = op.idx
                readers[u] = []
            deps_all.append(deps)
        nd = {e: 0 for e in self.ENGS}
        for op in ops:
            if op.dma:
                k = nd[op.eng]
                nd[op.eng] += 1
                op.slot = k % self.NSLOT
                op.slotval = 16 * (k // self.NSLOT + 1)
        known = {e: {f: -1 for f in self.ENGS} for e in self.ENGS}
        kdma = {e: {} for e in self.ENGS}
        needmark = set()
        for op in ops:
            X = op.eng
            best = {}
            dmadeps = []
            for di in deps_all[op.idx]:
                d = ops[di]
                if d.dma:
                    dmadeps.append(d)
                    continue
                if d.eng == X and not op.dma:
                    if X == "pe":
                        continue
                    if op.pos - d.pos > self.WINDOW:
                        continue
                if d.pos > best.get(d.eng, -1):
                    best[d.eng] = d.pos
            if op.dma:
                if op.slotval > 16:
                    key = (X, op.slot)
                    if kdma[X].get(key, 0) < op.slotval - 16:
                        op.waits.append(("dma", X, op.slot, op.slotval - 16))
                        kdma[X][key] = op.slotval - 16
            for E, p in best.items():
                if p > known[X][E]:
                    known[X][E] = p
                    op.waits.append(("eng", E, p))
                    needmark.add(per[E][p].idx)
            for d in dmadeps:
                key = (d.eng, d.slot)
                if kdma[X].get(key, 0) < d.slotval:
                    kdma[X][key] = d.slotval
                    op.waits.append(("dma", d.eng, d.slot, d.slotval))
        self.cnt = {}
        for e in self.ENGS:
            c = 0
            for op in per[e]:
                if op.idx in needmark and not op.dma:
                    c += 1
                    op.inc = c
        self.per = per
        self.nd = nd
        print('marks', {e: max([o.inc or 0 for o in per[e]] + [0]) for e in self.ENGS}, 'nops', {e: len(per[e]) for e in self.ENGS}, 'ndma', nd)

    def emit(self):
        nc, es = self.nc, self.es
        self.finalize()
        per = self.per
        esem = {e: es.enter_context(nc.semaphore("s_" + e)) for e in self.ENGS}
        dsem = {}
        for e in self.ENGS:
            if self.nd[e]:
                dsem[e] = [es.enter_context(nc.semaphore("d_%s_%d" % (e, i))) for i in range(min(self.NSLOT, self.nd[e]))]
        ops = self.ops

        def run(ename, eng):
            for op in per[ename]:
                for w in op.waits:
                    if w[0] == "eng":
                        eng.wait_ge(esem[w[1]], per[w[1]][w[2]].inc)
                    else:
                        eng.wait_ge(dsem[w[1]][w[2]], w[3])
                ins = op.fn(eng)
                if op.dma:
                    ins.then_inc(dsem[ename][op.slot], 16)
                elif op.inc is not None:
                    ins.then_inc(esem[ename], 1)
            if self.nd[ename]:
                last = {}
                for op in per[ename]:
                    if op.dma:
                        last[op.slot] = op.slotval
                for s, v in last.items():
                    eng.wait_ge(dsem[ename][s], v)

        with nc.Block() as block:
            @block.tensor
            def _(e):
                run("pe", e)

            @block.scalar
            def _(e):
                run("act", e)

            @block.vector
            def _(e):
                run("dve", e)

            @block.gpsimd
            def _(e):
                run("pool", e)

            @block.sync
            def _(e):
                run("sp", e)


class WStream:
    def __init__(self, P, nslot, cols, scratch, per_block):
        self.P = P
        self.slots = [P.sb("wslot%d" % i, [128, cols], BF16) for i in range(nslot)]
        self.pieces = []
        self.issued = 0
        self.scratch = scratch
        self.per_block = per_block

    def plan(self, parts):
        self.pieces.append(parts)
        return len(self.pieces) - 1

    def _issue(self, k):
        slot = self.slots[k % len(self.slots)]
        pos = k % self.per_block
        first = k < self.per_block
        for (dstfn, src) in self.pieces[k]:
            load_piece(self.P, dstfn(slot), src, dstfn(self.scratch[pos]) if self.scratch is not None else None, first)

    def get(self, k, ahead, lo=None):
        lo = k if lo is None else lo
        lim = min(len(self.pieces), k + 1 + ahead, lo + len(self.slots))
        while self.issued < lim:
            self._issue(self.issued)
            self.issued += 1
        return self.slots[k % len(self.slots)]


def load_piece(P, dst, src, scr, first):
    if scr is None:
        P.dma("pool", dst, src)
    elif first:
        P.dma("pool", dst, src)
        P.dma("sp", scr, dst)
    else:
        P.dma("sp", dst, scr)


OFF_G, OFF_CW, OFF_ALOG, OFF_DTB, OFF_GDN, OFF_SINK, OFF_NF, NSMALL = 0, 32, 80, 84, 88, 600, 608, 1632
C_ID, C_U, C_SU, C_ONE, C_US, C_SUS, C_ONES, C_MC, C_COS, C_SIN, NCONST = 0, 128, 256, 384, 512, 640, 768, 896, 1024, 1160, 1296
NMASK = 8
PAST = 16384
THETA = 500000.0


def make_consts():
    c = np.zeros((128, NCONST), np.float32)
    i = np.arange(128)
    m, n = i[:, None], i[None, :]
    same = (m // 8) == (n // 8)
    c[:, C_ID:C_ID + 128] = (m == n)
    c[:, C_U:C_U + 128] = (m <= n)
    c[:, C_SU:C_SU + 128] = (m > n)
    c[:, C_ONE:C_ONE + 128] = 1.0
    c[:, C_US:C_US + 128] = (m <= n) & same
    c[:, C_SUS:C_SUS + 128] = (m > n) & same
    c[:, C_ONES:C_ONES + 128] = same
    c[:, C_MC:C_MC + 128] = (m > (n % 8))
    inv = (np.float32(THETA) ** (-np.arange(8, dtype=np.float32) * np.float32(2.0) / np.float32(16))).astype(np.float32)
    for ti in range(17):
        pos = (ti * 128 + i) if ti < 16 else (PAST + (i % 8))
        ang = pos.astype(np.float32)[:, None] * inv[None, :]
        c[:, C_COS + ti * 8:C_COS + ti * 8 + 8] = np.cos(ang.astype(np.float32)).astype(np.float32)
        c[:, C_SIN + ti * 8:C_SIN + ti * 8 + 8] = np.sin(ang.astype(np.float32)).astype(np.float32)
    return c


def make_smalls(norm_ffn1, norm_mix, norm_ffn2, norm_ple, conv_w, a_log, dt_bias, gdn_norm, sinks, norm_final):
    s = np.zeros((128, NSMALL), np.float32)
    for k, g in enumerate([norm_ffn1, norm_mix, norm_ffn2, norm_ple]):
        s[:, OFF_G + 8 * k:OFF_G + 8 * k + 8] = np.asarray(g, np.float32).reshape(8, 128).T
    cw = np.asarray(conv_w, np.float32).reshape(4, 12, 128)
    s[:, OFF_CW:OFF_CW + 48] = cw.transpose(2, 0, 1).reshape(128, 48)
    s[:, OFF_ALOG:OFF_ALOG + 4] = np.asarray(a_log, np.float32).reshape(1, 4)
    s[:, OFF_DTB:OFF_DTB + 4] = np.asarray(dt_bias, np.float32).reshape(1, 4)
    s[:, OFF_GDN:OFF_GDN + 512] = np.tile(np.asarray(gdn_norm, np.float32).reshape(1, 128), (1, 4))
    s[:, OFF_SINK:OFF_SINK + 8] = np.asarray(sinks, np.float32).reshape(1, 8)
    s[:, OFF_NF:OFF_NF + 1024] = np.asarray(norm_final, np.float32).reshape(1, 1024)
    return s


def build(nc, cfg):
    es = contextlib.ExitStack()
    P = Prog(nc, es)
    NSEQ = cfg.get("nseq", 2)
    SEQ = cfg.get("seq", 2048)
    SAMPLE = cfg.get("sample", True)
    NT = 4
    T = NT * 128
    AHEAD = cfg.get("ahead", 2)
    DK = 128

    def din(name, shape):
        return nc.dram_tensor(name, list(shape), F32, kind="ExternalInput").ap()

    def dout(name, shape):
        return nc.dram_tensor(name, list(shape), F32, kind="ExternalOutput").ap()

    x_p = din("x_p", [NSEQ * SEQ, D])
    p_p = din("p_p", [NSEQ * SEQ, 256])
    x_s = din("x_s", [128, D])
    p_s = din("p_s", [128, 256])
    st_gdn = din("st_gdn", [16, 4, 128, 128])
    st_conv = din("st_conv", [48, 1536])
    ck_d = din("ck", [16, 128, 128])
    cv_d = din("cv", [16, 128, 128])
    ffn1_gu = din("ffn1_gu", [D, 2 * DFF])
    ffn1_down = din("ffn1_down", [DFF, D])
    w_in = din("w_in", [D, INW])
    w_out = din("w_out", [D, D])
    ffn2_gu = din("ffn2_gu", [D, 2 * DFF])
    ffn2_down = din("ffn2_down", [DFF, D])
    ple_proj = din("ple_proj", [256, D])
    ple_gate = din("ple_gate", [D, D])
    smalls_d = din("smalls", [128, NSMALL])
    consts_d = din("consts", [128, NCONST])
    y_p = dout("y_p", [NSEQ * SEQ, D])
    y_s = dout("y_s", [128, D])
    sg_p = dout("sg_p", [NSEQ, 4, 128, 128])
    sc_p = dout("sc_p", [NSEQ * 3, 1536])
    kk_p = dout("kk_p", [NSEQ, 128, 128])
    vv_p = dout("vv_p", [NSEQ, 128, 128])
    sg_s = dout("sg_s", [16, 4, 128, 128])
    sc_s = dout("sc_s", [48, 1536])
    kk_s = dout("kk_s", [16, 128, 128])
    vv_s = dout("vv_s", [16, 128, 128])

    DBG = cfg.get("debug", False)
    if DBG:
        dbg_og = dout("dbg_og", [128, 4 * 512])
        dbg_os = dout("dbg_os", [64, 8 * 512])
        dbg_h = dout("dbg_h", [128, 4 * D])
        dbg_h1 = dout("dbg_h1", [128, 4 * D])
    cst = P.sb("cst", [128, NCONST], F32)
    sml = P.sb("sml", [128, NSMALL], F32)
    P.dma("sp", cst, consts_d)
    P.dma("sp", sml, smalls_d)
    mb = P.sb("mb", [128, NMASK * 128], BF16)
    P.copy("dve", mb, cst[:, 0:NMASK * 128])

    def c32(off):
        return cst[:, off:off + 128]

    def cb(off):
        return mb[:, off:off + 128]

    ident32, identb, onesb = c32(C_ID), cb(C_ID), cb(C_ONE)
    neghalf = P.sb("neghalf", [128, 16], F32)
    P.memset("dve", neghalf, -0.5)
    negA = P.sb("negA", [128, 4], F32)
    P.act(negA, sml[:, OFF_ALOG:OFF_ALOG + 4], AF.Exp)
    P.ts("dve", negA, negA, -1.0)
    esk = P.sb("esk", [128, 8], F32)
    P.act(esk, sml[:, OFF_SINK:OFF_SINK + 8], AF.Exp)
    gnb = sml[:, OFF_GDN:OFF_GDN + 512]
    nfb = sml[:, OFF_NF:OFF_NF + 1024]
    dtb = sml[:, OFF_DTB:OFF_DTB + 4]

    def gcol(k):
        return sml[:, OFF_G + 8 * k:OFF_G + 8 * k + 8]

    NPB = 11 + 6 + 3 + 11 + 3
    USE_SCR = cfg.get("scratch", True)
    if USE_SCR:
        scr_d = nc.dram_tensor("wscr", [NPB + 12, 128, 8 * 520], BF16, kind="Internal").ap()
        scr = [P.newbuf(scr_d[i]) for i in range(NPB + 12)]
    else:
        scr = None
    ws = WStream(P, cfg.get("wslots", 4), 8 * 520, scr[:NPB] if USE_SCR else None, NPB)
    ffn_ctr = [0]
    P.make_arena("arena", cfg.get("arena", 116 * 1024))
    P.arena_reset()
    wd = P.take(NFF * 1024, BF16)
    wd_parts = [P.alias(wd) for _ in range(6)]
    wd3 = [w.v(lambda a: a.rearrange("p (c n) -> p c n", c=NFF)) for w in wd_parts]

    banks = [P.ps("pb%d" % i, [128, 512], F32) for i in range(8)]
    bctr = [0]

    pinned = set()
    bpool = {"cur": list(range(8))}
    pctr = {}

    def set_pool(idx):
        bpool["cur"] = list(idx)

    def bank(pin=False):
        pool = bpool["cur"]
        key = tuple(pool)
        c = pctr.get(key, 0)
        while pool[c % len(pool)] in pinned:
            c += 1
        i = pool[c % len(pool)]
        pctr[key] = c + 1
        if pin:
            pinned.add(i)
        return banks[i]

    def unpin_all():
        pinned.clear()

    def bf(b):
        return b.v(lambda a: a.bitcast(BF16))

    hT = [P.sb("h%d" % n, [128, D], F32) for n in range(NT)]

    class _H3:
        def __getitem__(self, k):
            _, n, sl = k
            return hT[n][:, sl]
    h3 = _H3()
    xs = [P.sb("xs%d" % i, [128, D], BF16) for i in range(2)]
    junk = P.sb("junk", [128, D], BF16)
    xnT = P.sb("xnT", [128, 8 * T], BF16)
    xnT3 = xnT.v(lambda a: a.rearrange("p (k t) -> p k t", k=8))
    aT = P.take(NFF * T, BF16)
    aT3 = aT.v(lambda a: a.rearrange("p (c t) -> p c t", c=NFF))
    sg = [P.take(T, F32) for i in range(2)]
    ss = P.sb("ss", [128, NT], F32)
    rstd = P.sb("rstd", [128, NT], F32)
    cnt = {"xs": 0, "sg": 0, "yt": 0}

    def rmsnorm_fm(gc_, nt):
        t = nt * 128
        xsl = []
        for n in range(nt):
            P.act(junk, h3[:, n, :], AF.Square, accum=ss[:, n:n + 1])
            P.ts("dve", rstd[:, n:n + 1], ss[:, n:n + 1], 1.0 / D, EPS, ALU.mult, ALU.add)
            P.act(rstd[:, n:n + 1], rstd[:, n:n + 1], AF.Ln)
            P.act(rstd[:, n:n + 1], rstd[:, n:n + 1], AF.Exp, scale=-0.5)
            xb = xs[cnt["xs"] % 2]
            cnt["xs"] += 1
            P.ts("dve", xb, h3[:, n, :], rstd[:, n:n + 1])
            xsl.append(xb)
            if n % 2 == 1 or n == nt - 1:
                n0 = n - (1 if n % 2 == 1 else 0)
                for kc in range(8):
                    pt = bf(bank())
                    for m in range(n0, n + 1):
                        P.tr(pt[:, (m - n0) * 128:(m - n0 + 1) * 128], xsl[m][:, kc * 128:(kc + 1) * 128], identb)
                    w = (n + 1 - n0) * 128
                    P.ts("dve", xnT3[:, kc, n0 * 128:n0 * 128 + w], pt[:, :w], gc_[:, kc:kc + 1])

    def ffn_gu(gc_, plan_gu, down, nt, do_norm=True):
        t = nt * 128
        if do_norm:
            rmsnorm_fm(gc_, nt)
        dv = down.rearrange("(c p) n -> p c n", p=128)
        fidx = ffn_ctr[0]
        ffn_ctr[0] += 1
        for i in range(6):
            c0, c1 = i * 4, min(NFF, i * 4 + 4)
            sc_ = None
            if USE_SCR:
                sc_ = scr[NPB + (fidx % 2) * 6 + i].v(lambda a: a[:, :(c1 - c0) * 1024].rearrange("p (c n) -> p c n", c=c1 - c0))
            load_piece(P, wd3[i][:, c0:c1, :], dv[:, c0:c1, :], sc_, fidx < 2)
        for cg in range(NFF // 2):
            W = ws.get(plan_gu[cg], AHEAD)
            W3 = W.v(lambda a: a[:, :8 * 512].rearrange("p (k n) -> p k n", k=8))
            for c in range(2):
                g_ps, u_ps = bank(), bank()
                sgb = sg[cnt["sg"] % 2]
                cnt["sg"] += 1
                for kc in range(8):
                    P.mm(g_ps[:, :t], W3[:, kc, c * 128:(c + 1) * 128], xnT3[:, kc, :t], start=(kc == 0), stop=(kc == 7))
                for kc in range(8):
                    P.mm(u_ps[:, :t], W3[:, kc, 256 + c * 128:256 + (c + 1) * 128], xnT3[:, kc, :t], start=(kc == 0), stop=(kc == 7))
                P.act(sgb[:, :t], g_ps[:, :t], AF.Silu)
                P.tt("dve", aT3[:, cg * 2 + c, :t], u_ps[:, :t], sgb[:, :t], ALU.mult)

    def ffn_down(nt):
        for n in range(nt):
            for half in range(2):
                d_ps = bank()
                for c in range(NFF):
                    P.mm(d_ps, aT3[:, c, n * 128:(n + 1) * 128], wd3[c // 4][:, c, half * 512:(half + 1) * 512],
                         start=(c == 0), stop=(c == NFF - 1))
                hv = h3[:, n, half * 512:(half + 1) * 512]
                P.stt("dve", hv, d_ps, 0.5, hv, ALU.mult, ALU.add)

    def cap(fn, pool):
        P.capture()
        set_pool(pool)
        fn()
        L = P.end_capture()
        set_pool(range(8))
        return L

    def par2(fa, fb):
        if not cfg.get("par2", True):
            fa()
            fb()
            return
        LA = cap(fa, [0, 1, 2, 3])
        LB = cap(fb, [4, 5, 6, 7])
        P.merge_sched([LA, LB])

    def k8(sl, w):
        return sl.v(lambda a: a[:, :8 * w].rearrange("p (k n) -> p k n", k=8))

    def plan_gu(gu):
        guv = gu.rearrange("(k p) n -> p k n", p=128)
        ids = []
        for cg in range(NFF // 2):
            ids.append(ws.plan([(lambda sl: k8(sl, 512)[:, :, 0:256], guv[:, :, cg * 256:(cg + 1) * 256]),
                                (lambda sl: k8(sl, 512)[:, :, 256:512], guv[:, :, DFF + cg * 256:DFF + (cg + 1) * 256])]))
        return ids

    WIN_COLS = [(0, 512), (512, 1024), (1024, 1536), (1536, 2056), (2056, 2568), (2568, 2824)]

    def plan_win():
        wv = w_in.rearrange("(k p) n -> p k n", p=128)
        ids = []
        for (a, b) in WIN_COLS:
            ids.append(ws.plan([((lambda w: (lambda sl: k8(sl, w)))(b - a), wv[:, :, a:b])]))
        return ids

    def plan_wout():
        ids = [ws.plan([(lambda sl: sl.v(lambda a: a[:, :4096].rearrange("p (c n) -> p c n", c=4)),
                         w_out[0:512, :].rearrange("(c p) n -> p c n", p=128))])]
        for i in range(2):
            ids.append(ws.plan([(lambda sl: sl.v(lambda a: a[0:64, :4096].rearrange("p (c n) -> p c n", c=4)),
                                 w_out[512 + i * 256:512 + (i + 1) * 256, :].rearrange("(c p) n -> p c n", p=64))]))
        return ids

    def plan_ple():
        gv = ple_gate.rearrange("(k p) n -> p k n", p=128)
        ids = [ws.plan([(lambda sl: k8(sl, 512), gv[:, :, hf * 512:(hf + 1) * 512])]) for hf in range(2)]
        ids.append(ws.plan([(lambda sl: sl.v(lambda a: a[:, :2048].rearrange("p (c n) -> p c n", c=2)),
                             ple_proj.rearrange("(c p) n -> p c n", p=128))]))
        return ids

    blocks = []
    for s in range(NSEQ):
        nb = SEQ // T
        for b in range(nb):
            blocks.append(dict(kind="p", seq=s, bi=b, t0=s * SEQ + b * T, nt=NT, first=(b == 0), last=(b == nb - 1)))
    if SAMPLE:
        blocks.append(dict(kind="s", seq=0, bi=0, t0=0, nt=1, first=True, last=True))
    plans = []
    for blk in blocks:
        plans.append(dict(f1=plan_gu(ffn1_gu), win=plan_win(), wout=plan_wout(), f2=plan_gu(ffn2_gu), ple=plan_ple()))

    P.arena_reset()
    EXTW = 3 + T
    ext = P.take(4 * EXTW, F32)
    ext3 = ext.v(lambda a: a.rearrange("p (c w) -> p c w", c=4))
    carry = P.sb("carry", [128, 12 * 48], F32)
    carry3 = carry.v(lambda a: a[:, :36].rearrange("p (c j) -> p c j", c=12))
    carry_s = carry.v(lambda a: a.rearrange("p (c b j) -> p c b j", c=12, b=16))
    acc = [P.take(T, F32) for i in range(2)]
    qkvT = P.take(12 * T, BF16)
    qkvT3 = qkvT.v(lambda a: a.rearrange("p (c t) -> p c t", c=12))
    gz = P.take(NT * 512, BF16)
    gz3 = gz.v(lambda a: a.rearrange("p (n d) -> p n d", n=NT))
    szt = [P.take(512, F32) for i in range(2)]
    ab = P.take(NT * 8, F32)
    ab3 = ab.v(lambda a: a.rearrange("p (n c) -> p n c", n=NT))
    gt = P.take(6 * NT * 4, F32)
    gt4 = gt.v(lambda a: a.rearrange("p (k n c) -> p k n c", k=6, n=NT))
    ogT = P.take(4 * T, BF16)
    ogT3 = ogT.v(lambda a: a.rearrange("p (c t) -> p c t", c=4))
    oTs = P.take(8 * T, BF16, parts=64)
    oTs3 = oTs.v(lambda a: a.rearrange("p (c t) -> p c t", c=8))
    S32 = P.sb("S32", [128, 512], F32)
    Sbf = P.sb("Sbf", [128, 512], BF16)
    S32_3 = S32.v(lambda a: a.rearrange("p (c d) -> p c d", c=4))
    Sbf3 = Sbf.v(lambda a: a.rearrange("p (c d) -> p c d", c=4))

    def t4(name, dt, n=1):
        l = []
        for i in range(n):
            b = P.take(512, dt)
            l.append(b.v(lambda a: a.rearrange("p (c d) -> p c d", c=4)))
        return l

    sqb = P.take(1024, BF16)
    rn = P.take(1024, F32)
    kTn, qTn = t4("kTn", BF16)[0], t4("qTn", BF16)[0]
    qdT2 = t4("qdT", BF16, 2)
    dgb = t4("dgb", BF16)[0]
    Ug = t4("Ug", F32)[0]
    Fb, Fs, Fm, Fsb = t4("Fb", BF16)[0], t4("Fs", BF16)[0], t4("Fm", BF16)[0], t4("Fsb", BF16)[0]
    QKm = t4("QKm", BF16)[0]
    Am2, nAm2, B02, qkT2 = t4("Am", BF16, 2), t4("nAm", BF16, 2), t4("B0", BF16, 2), t4("qkT", BF16, 2)
    Pm, PT, Rbf = t4("Pm", BF16, 2), t4("PT", BF16, 2), t4("Rbf", BF16, 2)
    R32 = t4("R32", F32)[0]
    kbg2, kdec2, vb2 = t4("kbg", BF16, 2), t4("kdec", BF16, 2), t4("vb", BF16, 2)
    negwT, vnew = t4("negwT", BF16)[0], t4("vnew", BF16)[0]
    osq, otmp = t4("osq", F32)[0], t4("otmp", F32)[0]
    og = t4("og", BF16)[0]
    stmp = t4("stmp", F32)[0]
    gsc1 = [P.take(64, F32) for i in range(2)]
    gsc2 = P.take(64, F32)
    qkv_tm = P.take(768, F32)
    rtmp = P.take(6 * 80, F32)
    qkb = P.take(640, BF16)
    Vaug = [P.sb("Vaug%d" % i, [128, 130], BF16) for i in range(2)]
    kTs = [P.sb("kTs%d" % i, [64, 256], BF16) for i in range(2)]
    qTs = P.take(1024, BF16, parts=64)
    PTs = [P.take(512, BF16) for i in range(2)]
    den = P.take(512, F32)
    bcs = P.take(512, F32, parts=64)
    sct = P.take(1536, F32, parts=48)
    print("arena mixing bytes", P.aoff)
    for v_ in Vaug:
        P.memset("pool", v_, 1.0)
    if SAMPLE:
        kcT = P.sb("kcT", [64, 16 * 128], BF16)
        Vc = P.sb("Vc", [128, 8 * 130], BF16)
        PTc = P.sb("PTc", [128, 256], BF16)
        P.memset("pool", Vc, 1.0)
    P.arena_reset()
    pst = P.take(NT * 256, F32)
    pstb = P.take(NT * 256, BF16)
    pT = P.take(2 * T, BF16)
    sig = [P.take(512, F32) for i in range(2)]
    ytile = [P.take(D, F32) for i in range(2)]

    def flat(x):
        return x.v(lambda a: a.rearrange("p c d -> p (c d)"))

    def bc_h(v4):
        return v4.v(lambda a: a.unsqueeze(2).to_broadcast([128, 4, 128]))

    def bc_m(m):
        return m.v(lambda a: a.unsqueeze(1).to_broadcast([128, 4, 128]))

    def mixing(blk, plan):
        nt = blk["nt"]
        t = nt * 128
        samp = blk["kind"] == "s"
        if samp:
            e4 = ext3.v(lambda a: a[:, :, :176].rearrange("p c (b j) -> p c b j", j=11))
            P.dma("sp", sct[0:48, :], st_conv)
            for g3 in range(3):
                ps = bank()
                for c in range(4):
                    cc = g3 * 4 + c
                    P.tr(ps[:, c * 48:(c + 1) * 48], sct[0:48, cc * 128:(cc + 1) * 128], ident32[0:48, 0:48])
                P.copy("dve", carry[:, g3 * 192:(g3 + 1) * 192], ps[:, 0:192])
        for grp in range(3):
            W = ws.get(plan["win"][grp], AHEAD)
            W3 = k8(W, 512)
            if samp:
                P.copy("pool", e4[:, :, :, 0:3], carry_s[:, grp * 4:(grp + 1) * 4, :, :])
            elif blk["first"]:
                P.memset("pool", ext3[:, :, 0:3], 0.0)
            else:
                P.copy("pool", ext3[:, :, 0:3], carry3[:, grp * 4:(grp + 1) * 4, :])
            for c in range(4):
                ps = bank()
                for kc in range(8):
                    P.mm(ps[:, :t], W3[:, kc, c * 128:(c + 1) * 128], xnT3[:, kc, :t], start=(kc == 0), stop=(kc == 7))
                if samp:
                    P.copy("act", e4[:, c, :, 3:11], ps[:, :128].v(lambda a: a.rearrange("p (b j) -> p b j", j=8)))
                else:
                    P.copy("act", ext3[:, c, 3:3 + t], ps[:, :t])
            for c in range(4):
                cc = grp * 4 + c
                a_ = acc[c % 2]
                eng = "dve"
                if samp:
                    av = a_[:, :128].v(lambda a: a.rearrange("p (b j) -> p b j", j=8))
                    src = lambda j: e4[:, c, :, j:j + 8]
                else:
                    av = a_[:, :t]
                    src = lambda j: ext3[:, c, j:j + t]
                P.act(av, src(3), AF.Identity, scale=sml[:, OFF_CW + 3 * 12 + cc:OFF_CW + 3 * 12 + cc + 1])
                for j in (2, 1, 0):
                    P.stt(eng, av, src(j), sml[:, OFF_CW + j * 12 + cc:OFF_CW + j * 12 + cc + 1], av, ALU.mult, ALU.add)
                P.act(qkvT3[:, cc, :t], a_[:, :t], AF.Silu)
            if samp:
                P.copy("pool", carry_s[:, grp * 4:(grp + 1) * 4, :, :], e4[:, :, :, 8:11])
            else:
                P.copy("pool", carry3[:, grp * 4:(grp + 1) * 4, :], ext3[:, :, t:t + 3])
        if blk["last"]:
            ncol = 48 if samp else 3
            for g3 in range(3):
                ps = bank()
                for c in range(4):
                    cc = g3 * 4 + c
                    src = carry[:, cc * 48:(cc + 1) * 48] if samp else carry3[:, cc, :]
                    P.tr(ps[0:ncol, c * 128:(c + 1) * 128], src, ident32)
                P.copy("dve", sct[0:ncol, g3 * 512:(g3 + 1) * 512], ps[0:ncol, :])
            if samp:
                P.dma("sp", sc_s, sct[0:48, :])
            else:
                P.dma("sp", sc_p[blk["seq"] * 3:blk["seq"] * 3 + 3, :], sct[0:3, :])
        W = ws.get(plan["win"][3], AHEAD)
        W3 = k8(W, 520)
        for n in range(nt):
            ps, ps2 = bank(), bank()
            for kc in range(8):
                P.mm(ps, xnT3[:, kc, n * 128:(n + 1) * 128], W3[:, kc, 0:512], start=(kc == 0), stop=(kc == 7))
            for kc in range(8):
                P.mm(ps2[:, 0:8], xnT3[:, kc, n * 128:(n + 1) * 128], W3[:, kc, 512:520], start=(kc == 0), stop=(kc == 7))
            sz = szt[n % 2]
            P.act(sz, ps, AF.Silu)
            P.tt("pool", gz3[:, n, :], sz, gnb, ALU.mult)
            P.copy("dve", ab3[:, n, :], ps2[:, 0:8])
        a_v, b_v = ab3[:, :nt, 0:4], ab3[:, :nt, 4:8]
        G = lambda k: gt4[:, k, :nt, :]
        P.tt("dve", G(0), a_v, dtb.v(lambda a: a.unsqueeze(1).to_broadcast([128, nt, 4])), ALU.add)
        P.act(G(1), G(0), AF.Exp)
        P.act(G(1), G(1), AF.Ln, bias=1.0)
        P.tt("dve", G(2), G(1), negA.v(lambda a: a.unsqueeze(1).to_broadcast([128, nt, 4])), ALU.mult)
        P.act(G(3), b_v, AF.Exp, scale=-1.0)
        P.ts("dve", G(3), G(3), 1.0, None, ALU.add)
        P.recip(G(4), G(3))
        MP = cfg.get("mparts", "sdo")
        W4 = k8(ws.get(plan["win"][4], AHEAD), 512)
        W5 = k8(ws.get(plan["win"][5], AHEAD, lo=plan["win"][4]), 256)
        if samp and cfg.get("merge", True):
            def gd_():
                gdn_p1(blk, 0)
                gdn_p2(blk, 0)
            par2(lambda: swa_tile(blk, 0, W4, W5), gd_)
        elif not cfg.get("merge", True):
            for n in range(nt):
                swa_tile(blk, n, W4, W5)
                gdn_p1(blk, n)
                gdn_p2(blk, n)
        else:
            POOL_S, POOL_1, POOL_2 = cfg.get("pools", ([0, 1, 2], [3, 4], [5, 6, 7]))

            LA = cap(lambda: swa_tile(blk, 0, W4, W5), POOL_S)
            LB = cap(lambda: gdn_p1(blk, 0), POOL_1)
            (P.merge_sched if cfg.get('sched', True) else P.merge_n)([LA, LB])
            for n in range(nt):
                Ls = [cap(lambda: gdn_p2(blk, n), POOL_2)]
                if n + 1 < nt:
                    Ls.append(cap(lambda: gdn_p1(blk, n + 1), POOL_1))
                    Ls.append(cap(lambda: swa_tile(blk, n + 1, W4, W5), POOL_S))
                (P.merge_sched if cfg.get('sched', True) else P.merge_n)(Ls)

    def mix_wout(blk, plan):
        nt = blk["nt"]
        if DBG:
            P.dma("pool", dbg_og, ogT)
            P.dma("pool", dbg_os, oTs)
            for n_ in range(NT):
                P.dma("sp", dbg_h1[:, n_ * D:(n_ + 1) * D], hT[n_])
        Wo = [ws.get(plan["wout"][i], AHEAD, lo=plan["wout"][0]) for i in range(3)]
        Wo1 = Wo[0].v(lambda a: a[:, :4096].rearrange("p (c n) -> p c n", c=4))
        Wo2 = [w.v(lambda a: a[0:64, :4096].rearrange("p (c n) -> p c n", c=4)) for w in Wo[1:]]
        for n in range(nt):
            for half in range(2):
                ps = bank()
                for hh in range(4):
                    P.mm(ps, ogT3[:, hh, n * 128:(n + 1) * 128], Wo1[:, hh, half * 512:(half + 1) * 512], start=(hh == 0), stop=False)
                for hh in range(8):
                    P.mm(ps, oTs3[:, hh, n * 128:(n + 1) * 128], Wo2[hh // 4][:, hh % 4, half * 512:(half + 1) * 512],
                         start=False, stop=(hh == 7))
                hv = h3[:, n, half * 512:(half + 1) * 512]
                P.tt("dve", hv, ps, hv, ALU.add)

    swa_state = {"par": 0}

    def swa_tile(blk, n, W4, W5):
        samp = blk["kind"] == "s"
        ti = 16 if samp else blk["bi"] * NT + n
        has_prev = samp or not (blk["first"] and n == 0)
        cur = swa_state["par"] % 2
        prev = 1 - cur
        swa_state["par"] += 1
        ps_q, ps_kv = bank(), bank()
        for kc in range(8):
            P.mm(ps_q, xnT3[:, kc, n * 128:(n + 1) * 128], W4[:, kc, 0:512], start=(kc == 0), stop=(kc == 7))
        for kc in range(8):
            P.mm(ps_kv[:, 0:256], xnT3[:, kc, n * 128:(n + 1) * 128], W5[:, kc, 0:256], start=(kc == 0), stop=(kc == 7))
        P.copy("act", qkv_tm[:, 0:512], ps_q)
        P.copy("dve", qkv_tm[:, 512:768], ps_kv[:, 0:256])
        X = qkv_tm.v(lambda a: a[:, 0:640].rearrange("p (h d) -> p h d", h=10))
        x1, x2 = X[:, :, 0:8], X[:, :, 8:16]
        cosb = cst[:, C_COS + ti * 8:C_COS + ti * 8 + 8].v(lambda a: a.unsqueeze(1).to_broadcast([128, 10, 8]))
        sinb = cst[:, C_SIN + ti * 8:C_SIN + ti * 8 + 8].v(lambda a: a.unsqueeze(1).to_broadcast([128, 10, 8]))
        R = lambda k: rtmp[:, k * 80:(k + 1) * 80].v(lambda a: a.rearrange("p (h d) -> p h d", h=10))
        P.tt("pool", R(0), x1, cosb, ALU.mult)
        P.tt("pool", R(1), x2, sinb, ALU.mult)
        P.tt("pool", R(2), x2, cosb, ALU.mult)
        P.tt("pool", R(3), x1, sinb, ALU.mult)
        P.tt("pool", x1, R(0), R(1), ALU.subtract)
        P.tt("pool", x2, R(2), R(3), ALU.add)
        P.copy("dve", qkb, qkv_tm[:, 0:640])
        Va = Vaug[cur].v(lambda a: a.rearrange("p (k d) -> p k d", k=2))
        P.copy("pool", Va[:, :, 0:64], qkv_tm[:, 640:768].v(lambda a: a.rearrange("p (k d) -> p k d", k=2)))
        if blk["last"] and not samp and n == blk["nt"] - 1:
            P.dma("sp", kk_p[blk["seq"]], qkv_tm[:, 512:640])
            P.dma("sp", vv_p[blk["seq"]], qkv_tm[:, 640:768])
        if samp:
            for b in range(16):
                P.dma("sp", kk_s[b, 120:128, :], qkv_tm[b * 8:(b + 1) * 8, 512:640])
                P.dma("sp", vv_s[b, 120:128, :], qkv_tm[b * 8:(b + 1) * 8, 640:768])
        pq = bf(bank())
        for hh in range(8):
            P.tr(pq[0:64, hh * 128:(hh + 1) * 128], qkb[:, hh * 64:(hh + 1) * 64], identb)
        pk = bf(bank())
        for hh in range(2):
            P.tr(pk[0:64, hh * 128:(hh + 1) * 128], qkb[:, 512 + hh * 64:512 + (hh + 1) * 64], identb)
        P.copy("act", qTs, pq[0:64, :])
        P.copy("dve", kTs[cur], pk[0:64, 0:256])
        if samp:
            swa_sample(n, cur)
            return
        for kvh in range(2):
            rhs_q = qTs[:, kvh * 512:(kvh + 1) * 512]
            sc = bank()
            P.mm(sc, kTs[cur][:, kvh * 128:(kvh + 1) * 128], rhs_q)
            P.act(PTs[0], sc, AF.Exp, scale=0.125)
            mcur = cb(C_US) if samp else cb(C_U)
            P.tt("pool", PTs[0].v(lambda a: a.rearrange("p (g q) -> p g q", g=4)),
                 PTs[0].v(lambda a: a.rearrange("p (g q) -> p g q", g=4)), bc_m(mcur), ALU.mult)
            o_ps = bank()
            if samp:
                swa_sample_cache(kvh, rhs_q, o_ps, cur)
            else:
                if has_prev:
                    sp_ = bank()
                    P.mm(sp_, kTs[prev][:, kvh * 128:(kvh + 1) * 128], rhs_q)
                    P.act(PTs[1], sp_, AF.Exp, scale=0.125)
                    P.tt("pool", PTs[1].v(lambda a: a.rearrange("p (g q) -> p g q", g=4)),
                         PTs[1].v(lambda a: a.rearrange("p (g q) -> p g q", g=4)), bc_m(cb(C_SU)), ALU.mult)
                P.mm(o_ps[0:65, :], Vaug[cur][:, kvh * 65:(kvh + 1) * 65], PTs[0], start=True, stop=not has_prev)
                if has_prev:
                    P.mm(o_ps[0:65, :], Vaug[prev][:, kvh * 65:(kvh + 1) * 65], PTs[1], start=False, stop=True)
            dv_ = den[64:65, :].v(lambda a: a.rearrange("p (g q) -> p g q", g=4))
            P.tt("dve", dv_, o_ps[64:65, :].v(lambda a: a.rearrange("p (g q) -> p g q", g=4)),
                 esk[64:65, kvh * 4:(kvh + 1) * 4].v(lambda a: a.unsqueeze(2).to_broadcast([1, 4, 128])), ALU.add)
            P.act(den[64:65, :], den[64:65, :], AF.Ln)
            P.act(den[64:65, :], den[64:65, :], AF.Exp, scale=-1.0)
            bcp = bank()
            P.mm(bcp[0:64, :], cst[64:65, C_ONE:C_ONE + 64], den[64:65, :])
            P.copy("act", bcs, bcp[0:64, :])
            P.tt("dve", oTs3[:, kvh * 4:(kvh + 1) * 4, n * 128:(n + 1) * 128],
                 o_ps[0:64, :].v(lambda a: a.rearrange("p (g q) -> p g q", g=4)),
                 bcs.v(lambda a: a.rearrange("p (g q) -> p g q", g=4)), ALU.mult)

    def g4(x, w=128):
        return x.v(lambda a: a.rearrange("p (g q) -> p g q", g=4))

    def swa_sample(n, cur):
        bgt = lambda x: x.v(lambda a: a.rearrange("p (b g t) -> p b g t", g=4, t=8))
        P.dma("sp", kk_s[:, 0:120, :], ck_d[:, 8:128, :])
        P.dma("sp", vv_s[:, 0:120, :], cv_d[:, 8:128, :])
        qre = qkv_tm.v(lambda a: a[0:64, 0:512].bitcast(BF16))
        for kvh in range(2):
            P.copy("pool", bgt(qre[:, kvh * 512:(kvh + 1) * 512]),
                   qTs[:, kvh * 512:(kvh + 1) * 512].v(lambda a: a.rearrange("p (g b t) -> p b g t", g=4, t=8)))
        o_ps = [bank(pin=True), bank(pin=True)]
        us4 = cb(C_US).v(lambda a: a.rearrange("p (b t) -> p b t", t=8).unsqueeze(2).to_broadcast([128, 16, 4, 8]))
        for kvh in range(2):
            rhs_q = qre[:, kvh * 512:(kvh + 1) * 512]
            sc = bank()
            P.mm(sc, kTs[cur][:, kvh * 128:(kvh + 1) * 128], rhs_q)
            P.act(PTs[kvh], sc, AF.Exp, scale=0.125)
            P.tt("pool", bgt(PTs[kvh]), bgt(PTs[kvh]), us4, ALU.mult)
            P.mm(o_ps[kvh][0:65, :], Vaug[cur][:, kvh * 65:(kvh + 1) * 65], PTs[kvh], start=True, stop=False)
        for half in range(2):
            b0 = half * 8
            ckv = xs[0].v(lambda a: a.rearrange("p (b d) -> p b d", b=8))
            P.dma("pool", ckv, ck_d[b0:b0 + 8].rearrange("b j d -> j b d"))
            for kvh in range(2):
                P.dma("pool", Vc.v(lambda a: a.rearrange("p (b d) -> p b d", b=8)[:, :, kvh * 65:kvh * 65 + 64]),
                      cv_d[b0:b0 + 8].rearrange("b j d -> j b d")[:, :, kvh * 64:(kvh + 1) * 64])
            for hb in range(2):
                pk = bf(bank())
                for i in range(8):
                    bl, kvh = (hb * 8 + i) // 2, (hb * 8 + i) % 2
                    P.tr(pk[0:64, i * 128:(i + 1) * 128], ckv[:, bl, kvh * 64:(kvh + 1) * 64], identb)
                P.copy("act" if hb == 0 else "dve", kcT[:, hb * 1024:(hb + 1) * 1024], pk[0:64, :])
            for kvh in range(2):
                scc = bank()
                for bl in range(8):
                    bb = b0 + bl
                    P.mm(scc[:, bl * 32:(bl + 1) * 32], kcT[:, (bl * 2 + kvh) * 128:(bl * 2 + kvh + 1) * 128],
                         qre[:, kvh * 512 + bb * 32:kvh * 512 + (bb + 1) * 32])
                P.act(PTc, scc[:, 0:256], AF.Exp, scale=0.125)
                pc3 = PTc.v(lambda a: a.rearrange("p (c t) -> p c t", t=8))
                P.tt("pool", pc3, pc3, cb(C_MC)[:, 0:8].v(lambda a: a.unsqueeze(1).to_broadcast([128, 32, 8])), ALU.mult)
                for bl in range(8):
                    bb = b0 + bl
                    P.mm(o_ps[kvh][0:65, bb * 32:(bb + 1) * 32], Vc[:, bl * 130 + kvh * 65:bl * 130 + (kvh + 1) * 65],
                         PTc[:, bl * 32:(bl + 1) * 32], start=False, stop=(half == 1 and bl == 7))
        for kvh in range(2):
            op_ = o_ps[kvh]
            P.tt("dve", bgt(den[64:65, :]), bgt(op_[64:65, :]),
                 esk[64:65, kvh * 4:(kvh + 1) * 4].v(lambda a: a.unsqueeze(1).unsqueeze(3).to_broadcast([1, 16, 4, 8])), ALU.add)
            P.act(den[64:65, :], den[64:65, :], AF.Ln)
            P.act(den[64:65, :], den[64:65, :], AF.Exp, scale=-1.0)
            bcp = bank()
            P.mm(bcp[0:64, :], cst[64:65, C_ONE:C_ONE + 64], den[64:65, :])
            P.copy("act", bcs, bcp[0:64, :])
            P.tt("dve", oTs3[:, kvh * 4:(kvh + 1) * 4, n * 128:(n + 1) * 128].v(lambda a: a.rearrange("p g (b t) -> p b g t", t=8)),
                 bgt(op_[0:64, :]), bgt(bcs), ALU.mult)
        unpin_all()

    pp = {"k": 0}

    hand = {}

    def r4(bk):
        return bk.v(lambda a: a.rearrange("p (c d) -> p c d", c=4))

    def trs(src3):
        pb_ = bf(bank())
        p3 = pb_.v(lambda a: a[:, 0:512].rearrange("p (c d) -> p c d", c=4))
        for hh in range(4):
            P.tr(p3[:, hh, :], src3[:, hh, :], identb)
        return p3

    def gdn_p1(blk, n):
        samp = blk["kind"] == "s"
        par = pp["k"] % 2
        pp["k"] += 1
        H = dict(Am=Am2[par], nAm=nAm2[par], B0=B02[par], qkT=qkT2[par], kbg=kbg2[par], kdec=kdec2[par],
                 vb=vb2[par], qdT=qdT2[par], gsc=gsc1[par])
        hand[n] = H
        gsc_ = H["gsc"]
        cs = slice(n * 128, (n + 1) * 128)
        qTr, kTr, vTr = qkvT3[:, 0:4, cs], qkvT3[:, 4:8, cs], qkvT3[:, 8:12, cs]
        U32 = c32(C_US) if samp else c32(C_U)
        SU32 = c32(C_SUS) if samp else c32(C_SU)
        ON32 = c32(C_ONES) if samp else c32(C_ONE)
        STb = cb(C_SUS) if samp else cb(C_SU)
        g_n, beta_n = gt4[:, 2, n, :], gt4[:, 4, n, :]
        sq3 = sqb.v(lambda a: a.rearrange("p (c d) -> p c d", c=8))
        P.tt("pool", sq3, qkvT3[:, 0:8, cs], qkvT3[:, 0:8, cs], ALU.mult)
        ssq, ssk = bank(), bank()
        P.mm(ssq, onesb, sqb[:, 0:512])
        P.mm(ssk, onesb, sqb[:, 512:1024])
        P.ts("dve", rn[:, 0:512], ssq, EPS, None, ALU.add)
        P.ts("dve", rn[:, 512:1024], ssk, EPS, None, ALU.add)
        P.act(rn, rn, AF.Ln)
        P.act(rn, rn, AF.Exp, scale=-0.5)
        rn3 = rn.v(lambda a: a.rearrange("p (c d) -> p c d", c=8))
        P.stt("dve", qTn, qTr, float(DK) ** -0.5, rn3[:, 0:4, :], ALU.mult, ALU.mult)
        P.tt("dve", kTn, kTr, rn3[:, 4:8, :], ALU.mult)
        gs = bank()
        P.mm(gs[:, 0:4], U32, g_n)
        P.mm(gs[:, 4:8], ON32, g_n)
        P.copy("dve", gsc_[:, 0:8], gs[:, 0:8])
        P.tt("dve", gsc_[:, 8:12], gsc_[:, 4:8], gsc_[:, 0:4], ALU.subtract)
        P.act(gsc_[:, 16:28], gsc_[:, 0:12], AF.Exp)
        e_gc, e_last, e_rem = gsc_[:, 16:20], gsc_[:, 20:24], gsc_[:, 24:28]
        H["e_last"] = e_last
        P.tt("dve", gsc_[:, 32:36], beta_n, e_gc, ALU.mult)
        c_kbg = gsc_[:, 32:36]
        P.tt("pool", dgb, bc_m(identb), bc_h(e_gc), ALU.mult)
        rb = bank()
        P.mm(rb, onesb, flat(dgb))
        P.tt("dve", H["qdT"], qTn, r4(rb), ALU.mult)
        P.tt("pool", Ug, bc_m(U32), bc_h(g_n), ALU.mult)
        dm = bank()
        dm3 = r4(dm)
        for hh in range(4):
            P.mm(dm3[:, hh, :], Ug[:, hh, :], SU32)
        P.act(Fb, dm3, AF.Exp)
        P.tt("pool", Fs, Fb, bc_m(STb), ALU.mult)
        P.tt("pool", Fm, Fs, bc_m(identb), ALU.add)
        P.tt("pool", Fsb, Fs, bc_h(beta_n), ALU.mult)
        kk, qk = bank(), bank()
        kk3, qk3 = r4(kk), r4(qk)
        for hh in range(4):
            P.mm(kk3[:, hh, :], kTn[:, hh, :], kTn[:, hh, :])
        for hh in range(4):
            P.mm(qk3[:, hh, :], qTn[:, hh, :], kTn[:, hh, :])
        P.tt("dve", H["Am"], kk3, Fsb, ALU.mult)
        P.tt("dve", QKm, qk3, Fm, ALU.mult)
        P.ts("pool", H["nAm"], H["Am"], -1.0, 0.0, ALU.mult, ALU.add)
        b_ps = trs(H["Am"])
        P.copy("act", flat(H["B0"]), flat(b_ps))
        qkT_ps = trs(QKm)
        P.copy("act", flat(H["qkT"]), flat(qkT_ps))
        ktm = trs(kTn)
        P.tt("dve", H["kbg"], ktm, bc_h(c_kbg), ALU.mult)
        P.tt("dve", H["kdec"], ktm, bc_h(e_rem), ALU.mult)
        vtm = trs(vTr)
        P.tt("dve", H["vb"], vtm, bc_h(beta_n), ALU.mult)

    def gdn_p2(blk, n):
        samp = blk["kind"] == "s"
        H = hand.pop(n)
        R_ps = bank(pin=True)
        R3 = r4(R_ps)
        for hh in range(4):
            P.mm(R3[:, hh, :], H["nAm"][:, hh, :], identb, start=(hh == 0), stop=False, skip_group_check=True)
            P.mm(R3[:, hh, :], identb, identb, start=False, stop=False, skip_group_check=True)
        P.copy("dve", flat(Rbf[0]), R_ps)
        cur_P, cur_PT, cur_R = H["B0"], H["Am"], Rbf[0]
        for k in range(1, 7):
            nP, nPT, nR = Pm[k % 2], PT[k % 2], Rbf[k % 2]
            if k < 6:
                p_ps = bank()
                p3 = r4(p_ps)
                for hh in range(4):
                    P.mm(p3[:, hh, :], cur_PT[:, hh, :], cur_P[:, hh, :])
            pt_ps = bank()
            pt3 = r4(pt_ps)
            for hh in range(4):
                P.mm(pt3[:, hh, :], cur_P[:, hh, :], cur_PT[:, hh, :])
            P.copy("dve", nPT, pt3)
            if k < 6:
                P.copy("act", flat(nP), flat(p3))
            for hh in range(4):
                P.mm(R3[:, hh, :], nPT[:, hh, :], cur_R[:, hh, :], start=False, stop=(k == 6), skip_group_check=True)
            P.copy("act", flat(nR), R_ps)
            cur_P, cur_PT, cur_R = nP, nPT, nR
        Rf = cur_R
        for i_, bk_ in enumerate(banks):
            if bk_.uid == R_ps.uid:
                pinned.discard(i_)
        w_ps = bank()
        w3 = r4(w_ps)
        for hh in range(4):
            P.mm(w3[:, hh, :], H["kbg"][:, hh, :], Rf[:, hh, :])
        P.ts("dve", negwT, w3, -1.0)
        if samp:
            gdn_sample_state(n, Rf, H)
            return
        if blk["first"] and n == 0:
            P.memset("pool", S32, 0.0)
            P.memset("pool", Sbf, 0.0)
        v_ps = bank()
        v3 = r4(v_ps)
        for hh in range(4):
            P.mm(v3[:, hh, :], Rf[:, hh, :], H["vb"][:, hh, :], start=True, stop=False)
            P.mm(v3[:, hh, :], negwT[:, hh, :], Sbf3[:, hh, :], start=False, stop=True)
        P.copy("act", flat(vnew), flat(v3))
        o_ps = bank()
        o3 = r4(o_ps)
        for hh in range(4):
            P.mm(o3[:, hh, :], H["qkT"][:, hh, :], vnew[:, hh, :], start=True, stop=False)
            P.mm(o3[:, hh, :], H["qdT"][:, hh, :], Sbf3[:, hh, :], start=False, stop=True)
        s_ps = bank()
        s3 = r4(s_ps)
        for hh in range(4):
            P.mm(s3[:, hh, :], H["kdec"][:, hh, :], vnew[:, hh, :])
        P.tt("pool", stmp, S32_3, bc_h(H["e_last"]), ALU.mult)
        P.tt("dve", S32_3, stmp, s3, ALU.add)
        P.copy("act", Sbf, S32)
        if blk["last"] and n == blk["nt"] - 1:
            P.dma("sp", sg_p[blk["seq"]].rearrange("h k v -> k h v"), S32_3)
        gdn_out(n, o3)

    def gdn_out(n, o3):
        gsc = gsc2
        P.act(osq, o3, AF.Square)
        P.add("dve", lambda e: e.tensor_reduce(gsc[:, 40:44].ap, osq.ap, AX.X, ALU.add), [osq], [gsc])
        P.ts("dve", gsc[:, 44:48], gsc[:, 40:44], 1.0 / 128, EPS, ALU.mult, ALU.add)
        P.act(gsc[:, 44:48], gsc[:, 44:48], AF.Ln)
        P.act(gsc[:, 44:48], gsc[:, 44:48], AF.Exp, scale=-0.5)
        P.tt("dve", otmp, o3, bc_h(gsc[:, 44:48]), ALU.mult)
        P.tt("pool", og, otmp, gz3[:, n, :].v(lambda a: a.rearrange("p (c d) -> p c d", c=4)), ALU.mult)
        pb_ = bf(bank())
        p3 = pb_.v(lambda a: a[:, 0:512].rearrange("p (c d) -> p c d", c=4))
        for hh in range(4):
            P.tr(p3[:, hh, :], og[:, hh, :], identb)
        P.copy("dve", ogT3[:, :, n * 128:(n + 1) * 128], p3)

    def gdn_sample_state(n, Rf, H):
        e_last = H["e_last"]
        vb, qdT, qkT, kdec = H["vb"], H["qdT"], H["qkT"], H["kdec"]
        P.tt("pool", dgb, bc_m(identb), bc_h(e_last), ALU.mult)
        rbl = bank()
        P.mm(rbl, onesb, flat(dgb))
        elrb = Ug
        P.copy("act", flat(elrb), rbl)
        vT_ps, oT_ps = bank(pin=True), bank(pin=True)
        vT3, oT3 = r4(vT_ps), r4(oT_ps)
        for hh in range(4):
            P.mm(vT3[:, hh, :], vb[:, hh, :], Rf[:, hh, :], start=(hh == 0), stop=False)
        sbufs = [Sbf3, og]
        first = True
        for b_ in range(16):
            Sb = sbufs[b_ % 2]
            P.dma("pool", Sb, st_gdn[b_].rearrange("h k v -> k h v"))
            cs_ = slice(b_ * 8, (b_ + 1) * 8)
            for hh in range(4):
                P.mm(vT3[:, hh, cs_], Sb[:, hh, :], negwT[:, hh, cs_], start=False, stop=(b_ == 15 and hh == 3))
            for hh in range(4):
                P.mm(oT3[:, hh, cs_], Sb[:, hh, :], qdT[:, hh, cs_], start=first, stop=False)
                first = False
        vT_sb = Pm[0]
        P.copy("act", flat(vT_sb), vT_ps)
        pb_ = bf(bank())
        p3 = pb_.v(lambda a: a[:, 0:512].rearrange("p (c d) -> p c d", c=4))
        for hh in range(4):
            P.tr(p3[:, hh, :], vT_sb[:, hh, :], identb)
        P.copy("act", flat(vnew), flat(p3))
        for hh in range(4):
            P.mm(oT3[:, hh, :], vnew[:, hh, :], qkT[:, hh, :], start=False, stop=(hh == 3))
        P.copy("act", flat(otmp), oT_ps)
        unpin_all()
        o_ps = bank()
        o3 = r4(o_ps)
        for hh in range(4):
            P.tr(o3[:, hh, :], otmp[:, hh, :], ident32)
        gdn_out(n, o3)
        vms = [Pm[1], PT[0]]
        sin = [S32_3, R32]
        sout = [stmp, osq]
        for b_ in range(16):
            vm = vms[b_ % 2]
            P.ts("dve", vm, vnew, c32(C_ONES)[:, b_ * 8:b_ * 8 + 1])
            s_ps = bank()
            s3 = r4(s_ps)
            for hh in range(4):
                P.mm(s3[:, hh, :], kdec[:, hh, :], vm[:, hh, :])
            si, so = sin[b_ % 2], sout[b_ % 2]
            P.dma("sp", si, st_gdn[b_].rearrange("h k v -> k h v"))
            P.tt("pool", so, si, elrb[:, :, b_ * 8:b_ * 8 + 1].v(lambda a: a.to_broadcast([128, 4, 128])), ALU.mult)
            P.tt("dve", so, so, s3, ALU.add)
            P.dma("sp", sg_s[b_].rearrange("h k v -> k h v"), so)

    def ple_final(blk, plan):
        nt = blk["nt"]
        t = nt * 128
        samp = blk["kind"] == "s"
        psrc = p_s if samp else p_p
        ydst = y_s if samp else y_p
        t0 = blk["t0"]
        pst3 = pst.v(lambda a: a.rearrange("p (n d) -> p n d", n=NT))
        pstb3 = pstb.v(lambda a: a.rearrange("p (n d) -> p n d", n=NT))
        pT3 = pT.v(lambda a: a.rearrange("p (c t) -> p c t", c=2))
        P.dma("sp", pst3[:, :nt, :], psrc[t0:t0 + t, :].rearrange("(n p) d -> p n d", p=128))
        P.copy("pool", pstb3[:, :nt, :], pst3[:, :nt, :])
        for c in range(2):
            pb_ = bf(bank())
            for n in range(nt):
                P.tr(pb_[:, n * 128:(n + 1) * 128], pstb3[:, n, c * 128:(c + 1) * 128], identb)
            P.copy("act", pT3[:, c, :t], pb_[:, :t])
        Wg = [k8(ws.get(plan["ple"][i], AHEAD, lo=plan["ple"][0]), 512) for i in range(2)]
        Wp = ws.get(plan["ple"][2], AHEAD, lo=plan["ple"][0]).v(lambda a: a[:, :2048].rearrange("p (c n) -> p c n", c=2))
        k = 0
        for half in range(2):
            for n in range(nt):
                pg_, pp_ = bank(), bank()
                for kc in range(8):
                    P.mm(pg_, xnT3[:, kc, n * 128:(n + 1) * 128], Wg[half][:, kc, :], start=(kc == 0), stop=(kc == 7))
                for c in range(2):
                    P.mm(pp_, pT3[:, c, n * 128:(n + 1) * 128], Wp[:, c, half * 512:(half + 1) * 512], start=(c == 0), stop=(c == 1))
                sb_ = sig[k % 2]
                k += 1
                P.act(sb_, pg_, AF.Sigmoid)
                P.tt("dve", sb_, sb_, pp_, ALU.mult)
                hv = h3[:, n, half * 512:(half + 1) * 512]
                P.tt("pool", hv, hv, sb_, ALU.add)

    def final_store(blk):
        nt = blk["nt"]
        samp = blk["kind"] == "s"
        ydst = y_s if samp else y_p
        t0 = blk["t0"]
        for n in range(nt):
            P.act(junk, h3[:, n, :], AF.Square, accum=ss[:, n:n + 1])
            P.ts("dve", rstd[:, n:n + 1], ss[:, n:n + 1], 1.0 / D, EPS, ALU.mult, ALU.add)
            P.act(rstd[:, n:n + 1], rstd[:, n:n + 1], AF.Ln)
            P.act(rstd[:, n:n + 1], rstd[:, n:n + 1], AF.Exp, scale=-0.5)
            yt = ytile[cnt["yt"] % 2]
            cnt["yt"] += 1
            P.stt("dve", yt, h3[:, n, :], rstd[:, n:n + 1], nfb, ALU.mult, ALU.mult)
            P.dma("sp", ydst[t0 + n * 128:t0 + (n + 1) * 128, :], yt)

    def load_x(blk):
        xsrc = x_s if blk["kind"] == "s" else x_p
        for n in range(blk["nt"]):
            P.dma("sp", hT[n], xsrc[blk["t0"] + n * 128:blk["t0"] + (n + 1) * 128, :])

    for bi, blk in enumerate(blocks):
        nt = blk["nt"]
        pl = plans[bi]
        if bi == 0:
            load_x(blk)
            rmsnorm_fm(gcol(0), nt)
        ffn_gu(gcol(0), pl["f1"], ffn1_down, nt, do_norm=False)
        par2(lambda: ffn_down(nt), lambda: rmsnorm_fm(gcol(1), nt))
        P.fence()
        mixing(blk, pl)
        par2(lambda: mix_wout(blk, pl), lambda: rmsnorm_fm(gcol(2), nt))
        if DBG:
            for n_ in range(NT):
                P.dma("sp", dbg_h[:, n_ * D:(n_ + 1) * D], hT[n_])
        P.fence()
        ffn_gu(gcol(2), pl["f2"], ffn2_down, nt, do_norm=False)
        par2(lambda: ffn_down(nt), lambda: rmsnorm_fm(gcol(3), nt))
        P.fence()
        ple_final(blk, pl)
        if bi + 1 < len(blocks):
            nb_ = blocks[bi + 1]

            def nxt():
                load_x(nb_)
                rmsnorm_fm(gcol(0), nb_["nt"])
            par2(lambda: final_store(blk), nxt)
        else:
            final_store(blk)
        P.fence()

    P.emit()
    es.close()
    return nc


_NC_CACHE = {}


def kernel(x_prompt, x_sample, state_gdn, state_conv, cache_swa_k, cache_swa_v, p_prompt, p_sample,
           norm_ffn1, ffn1_gu, ffn1_down, norm_mix, w_in, conv_w, a_log, dt_bias, gdn_norm, sinks,
           w_out, norm_ffn2, ffn2_gu, ffn2_down, norm_ple, ple_proj, ple_gate, norm_final):
    f = lambda a: np.ascontiguousarray(np.asarray(a), dtype=np.float32)
    x_prompt, x_sample, state_gdn, state_conv = f(x_prompt), f(x_sample), f(state_gdn), f(state_conv)
    cache_swa_k, cache_swa_v, p_prompt, p_sample = f(cache_swa_k), f(cache_swa_v), f(p_prompt), f(p_sample)
    nc = bass.Bass("TRN2", target_bir_lowering=False)
    build(nc, dict(nseq=2, seq=2048, sample=True))
    smalls = make_smalls(f(norm_ffn1)[0], f(norm_mix)[0], f(norm_ffn2)[0], f(norm_ple)[0], f(conv_w)[0], f(a_log)[0],
                         f(dt_bias)[0], f(gdn_norm)[0], f(sinks)[0], f(norm_final))
    consts = make_consts()
    shared = dict(ffn1_gu=f(ffn1_gu)[0], ffn1_down=f(ffn1_down)[0], w_in=f(w_in)[0], w_out=f(w_out)[0],
                  ffn2_gu=f(ffn2_gu)[0], ffn2_down=f(ffn2_down)[0], ple_proj=f(ple_proj)[0], ple_gate=f(ple_gate)[0],
                  smalls=smalls, consts=consts)
    in_maps = []
    for c in range(NCORES):
        sp, ss_ = slice(2 * c, 2 * c + 2), slice(16 * c, 16 * c + 16)
        m = dict(shared)
        m.update(x_p=f(x_prompt[sp].reshape(4096, 1024)), p_p=f(p_prompt[0, sp].reshape(4096, 256)),
                 x_s=f(x_sample[ss_].reshape(128, 1024)), p_s=f(p_sample[0, ss_].reshape(128, 256)),
                 st_gdn=f(state_gdn[0, ss_]), st_conv=f(state_conv[0, ss_].reshape(48, 1536)),
                 ck=f(cache_swa_k[0, ss_].reshape(16, 128, 128)), cv=f(cache_swa_v[0, ss_].reshape(16, 128, 128)))
        in_maps.append(m)
    res = run_bass_kernel_spmd(nc, in_maps, core_ids=list(range(NCORES)))
    R_ = res.results
    cat = lambda k, shp: np.concatenate([np.asarray(r[k], dtype=np.float32).reshape(shp) for r in R_], axis=0)
    y_prompt = cat("y_p", (2, 2048, 1024))
    y_sample = cat("y_s", (16, 8, 1024))
    sgp = cat("sg_p", (2, 4, 128, 128))[None]
    scp = cat("sc_p", (2, 3, 1536))[None]
    kkp = cat("kk_p", (2, 128, 2, 64))[None]
    vvp = cat("vv_p", (2, 128, 2, 64))[None]
    sgs = cat("sg_s", (16, 4, 128, 128))[None]
    scs = cat("sc_s", (16, 3, 1536))[None]
    kks = cat("kk_s", (16, 128, 2, 64))[None]
    vvs = cat("vv_s", (16, 128, 2, 64))[None]
    return (y_prompt, y_sample, sgp, scp, kkp, vvp, sgs, scs, kks, vvs)
```

```python
import contextlib
import numpy as np
import concourse.bass as bass
import concourse.mybir as mybir
from concourse.bass_utils import run_bass_kernel_spmd

F32 = mybir.dt.float32
BF16 = mybir.dt.bfloat16
AF = mybir.ActivationFunctionType
ALU = mybir.AluOpType
AX = mybir.AxisListType

NCORES = 8
D = 1024
DFF = 2816
NFF = 22
INW = 2824
EPS = 1e-6


class Buf:
    def __init__(self, ap, uid, reg=None, psum=False):
        self.ap = ap
        self.uid = uid
        self.reg = reg
        self.psum = psum

    def __getitem__(self, k):
        return Buf(self.ap[k], self.uid, self.reg, self.psum)

    def v(self, fn):
        return Buf(fn(self.ap), self.uid, self.reg, self.psum)


class Op:
    __slots__ = ("eng", "fn", "reads", "writes", "dma", "idx", "pos", "waits", "inc", "slot", "slotval", "pre", "cost")

    def __init__(self, eng, fn, reads, writes, dma):
        self.eng, self.fn, self.reads, self.writes, self.dma = eng, fn, reads, writes, dma
        self.waits = []
        self.inc = None
        self.pre = None


def _ap(x):
    return x.ap if isinstance(x, Buf) else x


class Prog:
    ENGS = ["pe", "act", "dve", "pool", "sp"]
    NSLOT = 12
    WINDOW = 5
    KEEP_STREAM_ORDER = True

    def __init__(self, nc, es):
        self.nc, self.es = nc, es
        self.ops = []
        self.nuid = 0

    def newbuf(self, ap):
        self.nuid += 1
        return Buf(ap, self.nuid)

    def sb(self, name, shape, dt):
        t = self.es.enter_context(self.nc.sbuf_tensor(name, list(shape), dt))
        return self.newbuf(t[:])

    def ps(self, name, shape, dt):
        t = self.es.enter_context(self.nc.psum_tensor(name, list(shape), dt))
        b = self.newbuf(t[:])
        b.psum = True
        return b

    def alias(self, buf):
        self.nuid += 1
        return Buf(buf.ap, self.nuid, buf.reg)

    def make_arena(self, name, nbytes, bbytes=0):
        self.b_lo = nbytes - bbytes
        self.boff = nbytes - bbytes
        t = self.es.enter_context(self.nc.sbuf_tensor(name, [128, nbytes // 2], BF16))
        self.nuid += 1
        self.arena_ap = t[:]
        self.arena_reg = self.nuid
        self.arena_bytes = nbytes
        self.fence_t = self.sb("fence_t", [128, 2], F32)
        self.regbuf = Buf(self.fence_t.ap, self.arena_reg)
        self.aoff = 0

    def arena_reset(self):
        self.aoff = 0

    def take(self, nelem, dt, parts=128, region="A"):
        esz = 4 if dt == F32 else 2
        nb = (nelem * esz + 31) // 32 * 32
        if region == "B":
            assert self.boff + nb <= self.arena_bytes, ("arena B overflow", self.boff, nb, self.arena_bytes)
            off = self.boff
            self.boff += nb
            reg = None
        else:
            assert self.aoff + nb <= self.b_lo, ("arena A overflow", self.aoff, nb, self.b_lo)
            off = self.aoff
            self.aoff += nb
            reg = self.arena_reg
        ap = self.arena_ap[0:parts, off // 2:(off + nelem * esz) // 2]
        if dt == F32:
            ap = ap.bitcast(F32)
        self.nuid += 1
        return Buf(ap, self.nuid, reg)

    def fence(self):
        o = self.fence_t.ap
        self.add("pool", lambda e: e.memset(o, 0.0), [], [self.regbuf])

    def add(self, eng, fn, reads, writes, dma=False):
        r = [b.uid for b in reads if isinstance(b, Buf)]
        w = [b.uid for b in writes if isinstance(b, Buf)]
        w += [b.uid for b in reads if isinstance(b, Buf) and b.psum and b.uid not in w]
        for b in list(reads) + list(writes):
            if isinstance(b, Buf) and b.reg is not None and b.reg not in r:
                r.append(b.reg)
        op = Op(eng, fn, r, w, dma)
        op.idx = len(self.ops)
        n = 0
        for b in writes:
            if isinstance(b, Buf):
                k = 1
                for d in b.ap.shape[1:]:
                    k *= int(d)
                n = max(n, k)
        if eng == "pe":
            op.cost = 60.0 + 0.47 * n
        elif eng == "act":
            op.cost = 230.0 + 0.85 * n
        elif eng == "dve":
            op.cost = 70.0 + 1.1 * n
        elif eng == "pool":
            op.cost = 120.0 + 2.0 * n
        else:
            op.cost = 100.0
        if dma:
            op.cost = 2000.0
        self.ops.append(op)
        return op

    def mm(self, out, lhsT, rhs, start=True, stop=True, **kw):
        o, l, r = out.ap, lhsT.ap, rhs.ap
        self.add("pe", lambda e: e.matmul(o, l, r, start=start, stop=stop, **kw), [lhsT, rhs], [out])

    def tr(self, out, in_, ident):
        o, i, d = out.ap, in_.ap, ident.ap
        self.add("pe", lambda e: e.transpose(o, i, d), [in_, ident], [out])

    def act(self, out, in_, func, bias=0.0, scale=1.0, accum=None, eng="act"):
        o, i, b, s = out.ap, in_.ap, _ap(bias), _ap(scale)
        a = _ap(accum) if accum is not None else None
        self.add(eng, lambda e: e.activation(o, i, func, bias=b, scale=s, accum_out=a),
                 [in_, bias, scale], [out] + ([accum] if accum is not None else []))

    def ts(self, eng, out, in0, s1, s2=None, op0=ALU.mult, op1=None, accum=None):
        o, i, a1, a2 = out.ap, in0.ap, _ap(s1), _ap(s2)
        ac = _ap(accum) if accum is not None else None
        if op1 is None:
            f = lambda e: e.tensor_scalar(o, i, a1, None, op0)
        elif ac is None:
            f = lambda e: e.tensor_scalar(o, i, a1, a2, op0, op1)
        else:
            f = lambda e: e.tensor_scalar(o, i, a1, a2, op0, op1, accum_out=ac)
        self.add(eng, f, [in0, s1, s2], [out] + ([accum] if accum is not None else []))

    def tt(self, eng, out, in0, in1, op):
        o, a, b = out.ap, in0.ap, in1.ap
        self.add(eng, lambda e: e.tensor_tensor(o, a, b, op), [in0, in1], [out])

    def stt(self, eng, out, in0, scalar, in1, op0, op1):
        o, a, s, b = out.ap, in0.ap, _ap(scalar), in1.ap
        self.add(eng, lambda e: e.scalar_tensor_tensor(o, a, s, b, op0, op1), [in0, scalar, in1], [out])

    def copy(self, eng, out, in_):
        o, i = out.ap, in_.ap
        if eng == "act":
            self.add(eng, lambda e: e.copy(o, i), [in_], [out])
        else:
            self.add(eng, lambda e: e.tensor_copy(o, i), [in_], [out])

    def memset(self, eng, out, val):
        o = out.ap
        self.add(eng, lambda e: e.memset(o, val), [], [out])

    def recip(self, out, in_):
        o, i = out.ap, in_.ap
        self.add("dve", lambda e: e.reciprocal(o, i), [in_], [out])

    def dma(self, q, out, in_):
        o, i = _ap(out), _ap(in_)
        self.add(q, lambda e: e.dma_start(out=o, in_=i), [in_], [out], dma=True)

    def capture(self):
        self._saved = self.ops
        self.ops = []

    def end_capture(self):
        L = self.ops
        self.ops = self._saved
        return L

    def merge(self, LA, LB):
        na, nb = len(LA), len(LB)
        ia = ib = 0
        while ia < na or ib < nb:
            if ib >= nb or (ia < na and ia * nb <= ib * na):
                self.ops.append(LA[ia])
                ia += 1
            else:
                self.ops.append(LB[ib])
                ib += 1

    def merge_n(self, Ls):
        Ls = [L for L in Ls if L]
        pos = [0] * len(Ls)
        total = sum(len(L) for L in Ls)
        for _ in range(total):
            best, bi = None, -1
            for i, L in enumerate(Ls):
                if pos[i] < len(L):
                    f = (pos[i] + 0.5) / len(L)
                    if best is None or f < best:
                        best, bi = f, i
            self.ops.append(Ls[bi][pos[bi]])
            pos[bi] += 1

    SCHED_LAT = 600.0

    def merge_sched(self, Ls, lat=None):
        lat = self.SCHED_LAT if lat is None else lat
        ops = [op for L in Ls for op in L]
        n = len(ops)
        lastw, readers = {}, {}
        deps = [set() for _ in range(n)]
        for i, op in enumerate(ops):
            for u in op.reads:
                if u in lastw:
                    deps[i].add(lastw[u])
            for u in op.writes:
                if u in lastw:
                    deps[i].add(lastw[u])
                for r in readers.get(u, ()):
                    deps[i].add(r)
            deps[i].discard(i)
            for u in op.reads:
                readers.setdefault(u, []).append(i)
            for u in op.writes:
                lastw[u] = i
                readers[u] = []
        sid = []
        for si, L in enumerate(Ls):
            sid += [si] * len(L)
        last_on = {}
        if self.KEEP_STREAM_ORDER:
            for i, op in enumerate(ops):
                key = (sid[i], op.eng)
                if key in last_on:
                    deps[i].add(last_on[key])
                last_on[key] = i
        nsucc_ready = [len(d) for d in deps]
        succ = [[] for _ in range(n)]
        for i, d in enumerate(deps):
            for j in d:
                succ[j].append(i)
        finish = [0.0] * n
        ready_t = [0.0] * n
        teng = {}
        avail = [i for i in range(n) if nsucc_ready[i] == 0]
        order = []
        while avail:
            best, bt = None, None
            for i in avail:
                t = max(ready_t[i], teng.get(ops[i].eng, 0.0))
                if bt is None or t < bt or (t == bt and i < best):
                    best, bt = i, t
            avail.remove(best)
            op = ops[best]
            finish[best] = bt + op.cost
            teng[op.eng] = finish[best] if not op.dma else bt + 100.0
            order.append(best)
            for j in succ[best]:
                rt = finish[best] + (lat if ops[j].eng != op.eng else 0.0)
                if rt > ready_t[j]:
                    ready_t[j] = rt
                nsucc_ready[j] -= 1
                if nsucc_ready[j] == 0:
                    avail.append(j)
        assert len(order) == n
        for i in order:
            self.ops.append(ops[i])

    def finalize(self):
        ops = self.ops
        for i, op in enumerate(ops):
            op.idx = i
        per = {e: [] for e in self.ENGS}
        for op in ops:
            op.pos = len(per[op.eng])
            per[op.eng].append(op)
        lastw, readers = {}, {}
        deps_all = []
        for op in ops:
            deps = set()
            for u in op.reads:
                if u in lastw:
                    deps.add(lastw[u])
            for u in op.writes:
                if u in lastw:
                    deps.add(lastw[u])
                for r in readers.get(u, ()):
                    deps.add(r)
            deps.discard(op.idx)
            for u in op.reads:
                readers.setdefault(u, []).append(op.idx)
            for u in op.writes:
                lastw[u] = op.idx
                readers[u] = []
            deps_all.append(deps)
        nd = {e: 0 for e in self.ENGS}
        for op in ops:
            if op.dma:
                k = nd[op.eng]
                nd[op.eng] += 1
                op.slot = k % self.NSLOT
                op.slotval = 16 * (k // self.NSLOT + 1)
        known = {e: {f: -1 for f in self.ENGS} for e in self.ENGS}
        kdma = {e: {} for e in self.ENGS}
        needmark = set()
        for op in ops:
            X = op.eng
            best = {}
            dmadeps = []
            for di in deps_all[op.idx]:
                d = ops[di]
                if d.dma:
                    dmadeps.append(d)
                    continue
                if d.eng == X and not op.dma:
                    if X == "pe":
                        continue
                    if op.pos - d.pos > self.WINDOW:
                        continue
                if d.pos > best.get(d.eng, -1):
                    best[d.eng] = d.pos
            if op.dma:
                if op.slotval > 16:
                    key = (X, op.slot)
                    if kdma[X].get(key, 0) < op.slotval - 16:
                        op.waits.append(("dma", X, op.slot, op.slotval - 16))
                        kdma[X][key] = op.slotval - 16
            for E, p in best.items():
                if p > known[X][E]:
                    known[X][E] = p
                    op.waits.append(("eng", E, p))
                    needmark.add(per[E][p].idx)
            for d in dmadeps:
                key = (d.eng, d.slot)
                if kdma[X].get(key, 0) < d.slotval:
                    kdma[X][key] = d.slotval
                    op.waits.append(("dma", d.eng, d.slot, d.slotval))
        self.cnt = {}
        for e in self.ENGS:
            c = 0
            for op in per[e]:
                if op.idx in needmark and not op.dma:
                    c += 1
                    op.inc = c
        self.per = per
        self.nd = nd
        print('marks', {e: max([o.inc or 0 for o in per[e]] + [0]) for e in self.ENGS}, 'nops', {e: len(per[e]) for e in self.ENGS}, 'ndma', nd)

    def emit(self):
        nc, es = self.nc, self.es
        self.finalize()
        per = self.per
        esem = {e: es.enter_context(nc.semaphore("s_" + e)) for e in self.ENGS}
        dsem = {}
        for e in self.ENGS:
            if self.nd[e]:
                dsem[e] = [es.enter_context(nc.semaphore("d_%s_%d" % (e, i))) for i in range(min(self.NSLOT, self.nd[e]))]
        ops = self.ops

        def run(ename, eng):
            for op in per[ename]:
                for w in op.waits:
                    if w[0] == "eng":
                        eng.wait_ge(esem[w[1]], per[w[1]][w[2]].inc)
                    else:
                        eng.wait_ge(dsem[w[1]][w[2]], w[3])
                ins = op.fn(eng)
                if op.dma:
                    ins.then_inc(dsem[ename][op.slot], 16)
                elif op.inc is not None:
                    ins.then_inc(esem[ename], 1)
            if self.nd[ename]:
                last = {}
                for op in per[ename]:
                    if op.dma:
                        last[op.slot] = op.slotval
                for s, v in last.items():
                    eng.wait_ge(dsem[ename][s], v)

        with nc.Block() as block:
            @block.tensor
            def _(e):
                run("pe", e)

            @block.scalar
            def _(e):
                run("act", e)

            @block.vector
            def _(e):
                run("dve", e)

            @block.gpsimd
            def _(e):
                run("pool", e)

            @block.sync
            def _(e):
                run("sp", e)


class WStream:
    def __init__(self, P, nslot, cols, scratch, per_block):
        self.P = P
        self.slots = [P.sb("wslot%d" % i, [128, cols], BF16) for i in range(nslot)]
        self.pieces = []
        self.issued = 0
        self.scratch = scratch
        self.per_block = per_block

    def plan(self, parts):
        self.pieces.append(parts)
        return len(self.pieces) - 1

    def _issue(self, k):
        slot = self.slots[k % len(self.slots)]
        pos = k % self.per_block
        first = k < self.per_block
        for (dstfn, src) in self.pieces[k]:
            load_piece(self.P, dstfn(slot), src, dstfn(self.scratch[pos]) if self.scratch is not None else None, first)

    def get(self, k, ahead, lo=None):
        lo = k if lo is None else lo
        lim = min(len(self.pieces), k + 1 + ahead, lo + len(self.slots))
        while self.issued < lim:
            self._issue(self.issued)
            self.issued += 1
        return self.slots[k % len(self.slots)]


def load_piece(P, dst, src, scr, first):
    if scr is None:
        P.dma("pool", dst, src)
    elif first:
        P.dma("pool", dst, src)
        P.dma("sp", scr, dst)
    else:
        P.dma("sp", dst, scr)


OFF_G, OFF_CW, OFF_ALOG, OFF_DTB, OFF_GDN, OFF_SINK, OFF_NF, NSMALL = 0, 32, 80, 84, 88, 600, 608, 1632
C_ID, C_U, C_SU, C_ONE, C_US, C_SUS, C_ONES, C_MC, C_COS, C_SIN, NCONST = 0, 128, 256, 384, 512, 640, 768, 896, 1024, 1160, 1296
NMASK = 8
PAST = 16384
THETA = 500000.0


def make_consts():
    c = np.zeros((128, NCONST), np.float32)
    i = np.arange(128)
    m, n = i[:, None], i[None, :]
    same = (m // 8) == (n // 8)
    c[:, C_ID:C_ID + 128] = (m == n)
    c[:, C_U:C_U + 128] = (m <= n)
    c[:, C_SU:C_SU + 128] = (m > n)
    c[:, C_ONE:C_ONE + 128] = 1.0
    c[:, C_US:C_US + 128] = (m <= n) & same
    c[:, C_SUS:C_SUS + 128] = (m > n) & same
    c[:, C_ONES:C_ONES + 128] = same
    c[:, C_MC:C_MC + 128] = (m > (n % 8))
    inv = (np.float32(THETA) ** (-np.arange(8, dtype=np.float32) * np.float32(2.0) / np.float32(16))).astype(np.float32)
    for ti in range(17):
        pos = (ti * 128 + i) if ti < 16 else (PAST + (i % 8))
        ang = pos.astype(np.float32)[:, None] * inv[None, :]
        c[:, C_COS + ti * 8:C_COS + ti * 8 + 8] = np.cos(ang.astype(np.float32)).astype(np.float32)
        c[:, C_SIN + ti * 8:C_SIN + ti * 8 + 8] = np.sin(ang.astype(np.float32)).astype(np.float32)
    return c


def make_smalls(norm_ffn1, norm_mix, norm_ffn2, norm_ple, conv_w, a_log, dt_bias, gdn_norm, sinks, norm_final):
    s = np.zeros((128, NSMALL), np.float32)
    for k, g in enumerate([norm_ffn1, norm_mix, norm_ffn2, norm_ple]):
        s[:, OFF_G + 8 * k:OFF_G + 8 * k + 8] = np.asarray(g, np.float32).reshape(8, 128).T
    cw = np.asarray(conv_w, np.float32).reshape(4, 12, 128)
    s[:, OFF_CW:OFF_CW + 48] = cw.transpose(2, 0, 1).reshape(128, 48)
    s[:, OFF_ALOG:OFF_ALOG + 4] = np.asarray(a_log, np.float32).reshape(1, 4)
    s[:, OFF_DTB:OFF_DTB + 4] = np.asarray(dt_bias, np.float32).reshape(1, 4)
    s[:, OFF_GDN:OFF_GDN + 512] = np.tile(np.asarray(gdn_norm, np.float32).reshape(1, 128), (1, 4))
    s[:, OFF_SINK:OFF_SINK + 8] = np.asarray(sinks, np.float32).reshape(1, 8)
    s[:, OFF_NF:OFF_NF + 1024] = np.asarray(norm_final, np.float32).reshape(1, 1024)
    return s


def build(nc, cfg):
    es = contextlib.ExitStack()
    P = Prog(nc, es)
    NSEQ = cfg.get("nseq", 2)
    SEQ = cfg.get("seq", 2048)
    SAMPLE = cfg.get("sample", True)
    NT = 4
    T = NT * 128
    AHEAD = cfg.get("ahead", 2)
    DK = 128

    def din(name, shape):
        return nc.dram_tensor(name, list(shape), F32, kind="ExternalInput").ap()

    def dout(name, shape):
        return nc.dram_tensor(name, list(shape), F32, kind="ExternalOutput").ap()

    x_p = din("x_p", [NSEQ * SEQ, D])
    p_p = din("p_p", [NSEQ * SEQ, 256])
    x_s = din("x_s", [128, D])
    p_s = din("p_s", [128, 256])
    st_gdn = din("st_gdn", [16, 4, 128, 128])
    st_conv = din("st_conv", [48, 1536])
    ck_d = din("ck", [16, 128, 128])
    cv_d = din("cv", [16, 128, 128])
    ffn1_gu = din("ffn1_gu", [D, 2 * DFF])
    ffn1_down = din("ffn1_down", [DFF, D])
    w_in = din("w_in", [D, INW])
    w_out = din("w_out", [D, D])
    ffn2_gu = din("ffn2_gu", [D, 2 * DFF])
    ffn2_down = din("ffn2_down", [DFF, D])
    ple_proj = din("ple_proj", [256, D])
    ple_gate = din("ple_gate", [D, D])
    smalls_d = din("smalls", [128, NSMALL])
    consts_d = din("consts", [128, NCONST])
    y_p = dout("y_p", [NSEQ * SEQ, D])
    y_s = dout("y_s", [128, D])
    sg_p = dout("sg_p", [NSEQ, 4, 128, 128])
    sc_p = dout("sc_p", [NSEQ * 3, 1536])
    kk_p = dout("kk_p", [NSEQ, 128, 128])
    vv_p = dout("vv_p", [NSEQ, 128, 128])
    sg_s = dout("sg_s", [16, 4, 128, 128])
    sc_s = dout("sc_s", [48, 1536])
    kk_s = dout("kk_s", [16, 128, 128])
    vv_s = dout("vv_s", [16, 128, 128])

    DBG = cfg.get("debug", False)
    if DBG:
        dbg_og = dout("dbg_og", [128, 4 * 512])
        dbg_os = dout("dbg_os", [64, 8 * 512])
        dbg_h = dout("dbg_h", [128, 4 * D])
        dbg_h1 = dout("dbg_h1", [128, 4 * D])
    cst = P.sb("cst", [128, NCONST], F32)
    sml = P.sb("sml", [128, NSMALL], F32)
    P.dma("sp", cst, consts_d)
    P.dma("sp", sml, smalls_d)
    mb = P.sb("mb", [128, NMASK * 128], BF16)
    P.copy("dve", mb, cst[:, 0:NMASK * 128])

    def c32(off):
        return cst[:, off:off + 128]

    def cb(off):
        return mb[:, off:off + 128]

    ident32, identb, onesb = c32(C_ID), cb(C_ID), cb(C_ONE)
    neghalf = P.sb("neghalf", [128, 16], F32)
    P.memset("dve", neghalf, -0.5)
    negA = P.sb("negA", [128, 4], F32)
    P.act(negA, sml[:, OFF_ALOG:OFF_ALOG + 4], AF.Exp)
    P.ts("dve", negA, negA, -1.0)
    esk = P.sb("esk", [128, 8], F32)
    P.act(esk, sml[:, OFF_SINK:OFF_SINK + 8], AF.Exp)
    gnb = sml[:, OFF_GDN:OFF_GDN + 512]
    nfb = sml[:, OFF_NF:OFF_NF + 1024]
    dtb = sml[:, OFF_DTB:OFF_DTB + 4]

    def gcol(k):
        return sml[:, OFF_G + 8 * k:OFF_G + 8 * k + 8]

    NPB = 11 + 6 + 3 + 11 + 3
    USE_SCR = cfg.get("scratch", True)
    if USE_SCR:
        scr_d = nc.dram_tensor("wscr", [NPB + 12, 128, 8 * 520], BF16, kind="Internal").ap()
        scr = [P.newbuf(scr_d[i]) for i in range(NPB + 12)]
    else:
        scr = None
    ws = WStream(P, cfg.get("wslots", 4), 8 * 520, scr[:NPB] if USE_SCR else None, NPB)
    ffn_ctr = [0]
    P.make_arena("arena", 80 * 1024 + 34304, 34304)
    P.arena_reset()
    wd = P.take(NFF * 1024, BF16)
    wd_parts = [P.alias(wd) for _ in range(6)]
    wd3 = [w.v(lambda a: a.rearrange("p (c n) -> p c n", c=NFF)) for w in wd_parts]

    banks = [P.ps("pb%d" % i, [128, 512], F32) for i in range(8)]
    bctr = [0]

    pinned = set()
    bpool = {"cur": list(range(8))}
    pctr = {}

    def set_pool(idx):
        bpool["cur"] = list(idx)

    def bank(pin=False):
        pool = bpool["cur"]
        key = tuple(pool)
        c = pctr.get(key, 0)
        while pool[c % len(pool)] in pinned:
            c += 1
        i = pool[c % len(pool)]
        pctr[key] = c + 1
        if pin:
            pinned.add(i)
        return banks[i]

    def unpin_all():
        pinned.clear()

    def bf(b):
        return b.v(lambda a: a.bitcast(BF16))

    hT = [P.sb("h%d" % n, [128, D], F32) for n in range(NT)]

    class _H3:
        def __getitem__(self, k):
            _, n, sl = k
            return hT[n][:, sl]
    h3 = _H3()
    xs = [P.sb("xs%d" % i, [128, D], BF16) for i in range(2)]
    junk = P.sb("junk", [128, D], BF16)
    xnT = P.sb("xnT", [128, 8 * T], BF16)
    xnT3 = xnT.v(lambda a: a.rearrange("p (k t) -> p k t", k=8))
    aT = P.take(NFF * T, BF16)
    aT3 = aT.v(lambda a: a.rearrange("p (c t) -> p c t", c=NFF))
    sg = [P.take(T, F32) for i in range(2)]
    ss = P.sb("ss", [128, NT], F32)
    rstd = P.sb("rstd", [128, NT], F32)
    cnt = {"xs": 0, "sg": 0, "yt": 0}

    def rmsnorm_fm(gc_, nt):
        t = nt * 128
        xsl = []
        for n in range(nt):
            P.act(junk, h3[:, n, :], AF.Square, accum=ss[:, n:n + 1])
            P.ts("dve", rstd[:, n:n + 1], ss[:, n:n + 1], 1.0 / D, EPS, ALU.mult, ALU.add)
            P.act(rstd[:, n:n + 1], rstd[:, n:n + 1], AF.Ln)
            P.act(rstd[:, n:n + 1], rstd[:, n:n + 1], AF.Exp, scale=-0.5)
            xb = xs[cnt["xs"] % 2]
            cnt["xs"] += 1
            P.ts("dve", xb, h3[:, n, :], rstd[:, n:n + 1])
            xsl.append(xb)
            if n % 2 == 1 or n == nt - 1:
                n0 = n - (1 if n % 2 == 1 else 0)
                for kc in range(8):
                    pt = bf(bank())
                    for m in range(n0, n + 1):
                        P.tr(pt[:, (m - n0) * 128:(m - n0 + 1) * 128], xsl[m][:, kc * 128:(kc + 1) * 128], identb)
                    w = (n + 1 - n0) * 128
                    P.ts("dve", xnT3[:, kc, n0 * 128:n0 * 128 + w], pt[:, :w], gc_[:, kc:kc + 1])

    def ffn_gu(gc_, plan_gu, down, nt, do_norm=True):
        t = nt * 128
        if do_norm:
            rmsnorm_fm(gc_, nt)
        dv = down.rearrange("(c p) n -> p c n", p=128)
        fidx = ffn_ctr[0]
        ffn_ctr[0] += 1
        for i in range(6):
            c0, c1 = i * 4, min(NFF, i * 4 + 4)
            sc_ = None
            if USE_SCR:
                sc_ = scr[NPB + (fidx % 2) * 6 + i].v(lambda a: a[:, :(c1 - c0) * 1024].rearrange("p (c n) -> p c n", c=c1 - c0))
            load_piece(P, wd3[i][:, c0:c1, :], dv[:, c0:c1, :], sc_, fidx < 2)
        for cg in range(NFF // 2):
            W = ws.get(plan_gu[cg], AHEAD)
            W3 = W.v(lambda a: a[:, :8 * 512].rearrange("p (k n) -> p k n", k=8))
            for c in range(2):
                g_ps, u_ps = bank(), bank()
                sgb = sg[cnt["sg"] % 2]
                cnt["sg"] += 1
                for kc in range(8):
                    P.mm(g_ps[:, :t], W3[:, kc, c * 128:(c + 1) * 128], xnT3[:, kc, :t], start=(kc == 0), stop=(kc == 7))
                for kc in range(8):
                    P.mm(u_ps[:, :t], W3[:, kc, 256 + c * 128:256 + (c + 1) * 128], xnT3[:, kc, :t], start=(kc == 0), stop=(kc == 7))
                P.act(sgb[:, :t], g_ps[:, :t], AF.Silu)
                P.tt("dve", aT3[:, cg * 2 + c, :t], u_ps[:, :t], sgb[:, :t], ALU.mult)

    def ffn_down(nt):
        for n in range(nt):
            for half in range(2):
                d_ps = bank()
                for c in range(NFF):
                    P.mm(d_ps, aT3[:, c, n * 128:(n + 1) * 128], wd3[c // 4][:, c, half * 512:(half + 1) * 512],
                         start=(c == 0), stop=(c == NFF - 1))
                hv = h3[:, n, half * 512:(half + 1) * 512]
                P.stt("dve", hv, d_ps, 0.5, hv, ALU.mult, ALU.add)

    def cap(fn, pool):
        P.capture()
        set_pool(pool)
        fn()
        L = P.end_capture()
        set_pool(range(8))
        return L

    def par2(fa, fb):
        if not cfg.get("par2", True):
            fa()
            fb()
            return
        LA = cap(fa, [0, 1, 2, 3])
        LB = cap(fb, [4, 5, 6, 7])
        P.merge_sched([LA, LB])

    def k8(sl, w):
        return sl.v(lambda a: a[:, :8 * w].rearrange("p (k n) -> p k n", k=8))

    def plan_gu(gu):
        guv = gu.rearrange("(k p) n -> p k n", p=128)
        ids = []
        for cg in range(NFF // 2):
            ids.append(ws.plan([(lambda sl: k8(sl, 512)[:, :, 0:256], guv[:, :, cg * 256:(cg + 1) * 256]),
                                (lambda sl: k8(sl, 512)[:, :, 256:512], guv[:, :, DFF + cg * 256:DFF + (cg + 1) * 256])]))
        return ids

    WIN_COLS = [(0, 512), (512, 1024), (1024, 1536), (1536, 2056), (2056, 2568), (2568, 2824)]

    def plan_win():
        wv = w_in.rearrange("(k p) n -> p k n", p=128)
        ids = []
        for (a, b) in WIN_COLS:
            ids.append(ws.plan([((lambda w: (lambda sl: k8(sl, w)))(b - a), wv[:, :, a:b])]))
        return ids

    def plan_wout():
        ids = [ws.plan([(lambda sl: sl.v(lambda a: a[:, :4096].rearrange("p (c n) -> p c n", c=4)),
                         w_out[0:512, :].rearrange("(c p) n -> p c n", p=128))])]
        for i in range(2):
            ids.append(ws.plan([(lambda sl: sl.v(lambda a: a[0:64, :4096].rearrange("p (c n) -> p c n", c=4)),
                                 w_out[512 + i * 256:512 + (i + 1) * 256, :].rearrange("(c p) n -> p c n", p=64))]))
        return ids

    def plan_ple():
        gv = ple_gate.rearrange("(k p) n -> p k n", p=128)
        ids = [ws.plan([(lambda sl: k8(sl, 512), gv[:, :, hf * 512:(hf + 1) * 512])]) for hf in range(2)]
        ids.append(ws.plan([(lambda sl: sl.v(lambda a: a[:, :2048].rearrange("p (c n) -> p c n", c=2)),
                             ple_proj.rearrange("(c p) n -> p c n", p=128))]))
        return ids

    blocks = []
    for s in range(NSEQ):
        nb = SEQ // T
        for b in range(nb):
            blocks.append(dict(kind="p", seq=s, bi=b, t0=s * SEQ + b * T, nt=NT, first=(b == 0), last=(b == nb - 1)))
    if SAMPLE:
        blocks.append(dict(kind="s", seq=0, bi=0, t0=0, nt=1, first=True, last=True))
    plans = []
    for blk in blocks:
        plans.append(dict(f1=plan_gu(ffn1_gu), win=plan_win(), wout=plan_wout(), f2=plan_gu(ffn2_gu), ple=plan_ple()))

    P.arena_reset()
    EXTW = 3 + T
    ext = P.take(4 * EXTW, F32, region="B")
    ext3 = ext.v(lambda a: a.rearrange("p (c w) -> p c w", c=4))
    carry = P.sb("carry", [128, 12 * 48], F32)
    carry3 = carry.v(lambda a: a[:, :36].rearrange("p (c j) -> p c j", c=12))
    carry_s = carry.v(lambda a: a.rearrange("p (c b j) -> p c b j", c=12, b=16))
    acc = [P.take(T, F32, region="B") for i in range(2)]
    qkvT = P.take(12 * T, BF16, region="B")
    qkvT3 = qkvT.v(lambda a: a.rearrange("p (c t) -> p c t", c=12))
    gz = P.take(NT * 512, BF16, region="B")
    gz3 = gz.v(lambda a: a.rearrange("p (n d) -> p n d", n=NT))
    szt = [P.take(512, F32, region="B") for i in range(2)]
    ab = P.take(NT * 8, F32, region="B")
    ab3 = ab.v(lambda a: a.rearrange("p (n c) -> p n c", n=NT))
    gt = P.take(6 * NT * 4, F32, region="B")
    gt4 = gt.v(lambda a: a.rearrange("p (k n c) -> p k n c", k=6, n=NT))
    ogT = P.take(4 * T, BF16)
    ogT3 = ogT.v(lambda a: a.rearrange("p (c t) -> p c t", c=4))
    oTs = P.take(8 * T, BF16, parts=64)
    oTs3 = oTs.v(lambda a: a.rearrange("p (c t) -> p c t", c=8))
    S32 = P.sb("S32", [128, 512], F32)
    Sbf = P.sb("Sbf", [128, 512], BF16)
    S32_3 = S32.v(lambda a: a.rearrange("p (c d) -> p c d", c=4))
    Sbf3 = Sbf.v(lambda a: a.rearrange("p (c d) -> p c d", c=4))

    def t4(name, dt, n=1):
        l = []
        for i in range(n):
            b = P.take(512, dt)
            l.append(b.v(lambda a: a.rearrange("p (c d) -> p c d", c=4)))
        return l

    sqb = P.take(1024, BF16)
    rn = P.take(1024, F32)
    kTn, qTn = t4("kTn", BF16)[0], t4("qTn", BF16)[0]
    qdT2 = t4("qdT", BF16, 2)
    dgb = t4("dgb", BF16)[0]
    Ug = t4("Ug", F32)[0]
    Fb, Fs, Fm, Fsb = t4("Fb", BF16)[0], t4("Fs", BF16)[0], t4("Fm", BF16)[0], t4("Fsb", BF16)[0]
    QKm = t4("QKm", BF16)[0]
    Am2, nAm2, B02, qkT2 = t4("Am", BF16, 2), t4("nAm", BF16, 2), t4("B0", BF16, 2), t4("qkT", BF16, 2)
    Pm, PT, Rbf = t4("Pm", BF16, 2), t4("PT", BF16, 2), t4("Rbf", BF16, 2)
    R32 = t4("R32", F32)[0]
    kbg2, kdec2, vb2 = t4("kbg", BF16, 2), t4("kdec", BF16, 2), t4("vb", BF16, 2)
    negwT, vnew = t4("negwT", BF16)[0], t4("vnew", BF16)[0]
    osq, otmp = t4("osq", F32)[0], t4("otmp", F32)[0]
    og = t4("og", BF16)[0]
    stmp = t4("stmp", F32)[0]
    gsc1 = [P.take(64, F32) for i in range(2)]
    gsc2 = P.take(64, F32)
    qkv_tm = P.take(768, F32)
    rtmp = P.take(6 * 80, F32)
    qkb = P.take(640, BF16)
    Vaug = [P.sb("Vaug%d" % i, [128, 130], BF16) for i in range(2)]
    kTs = [P.sb("kTs%d" % i, [64, 256], BF16) for i in range(2)]
    qTs = P.take(1024, BF16, parts=64)
    PTs = [P.take(512, BF16) for i in range(2)]
    den = P.take(512, F32)
    bcs = P.take(512, F32, parts=64)
    sct = P.take(512, F32, parts=48)
    print("arena mixing bytes", P.aoff)
    for v_ in Vaug:
        P.memset("pool", v_, 1.0)
    if SAMPLE:
        kcT = P.sb("kcT", [64, 16 * 128], BF16)
        Vc = P.sb("Vc", [128, 8 * 130], BF16)
        PTc = P.sb("PTc", [128, 256], BF16)
        P.memset("pool", Vc, 1.0)
    P.arena_reset()
    pst = P.take(NT * 256, F32)
    pstb = P.take(NT * 256, BF16)
    pT = P.take(2 * T, BF16)
    sig = [P.take(512, F32) for i in range(2)]
    ytile = [P.take(D, F32) for i in range(2)]

    def flat(x):
        return x.v(lambda a: a.rearrange("p c d -> p (c d)"))

    def bc_h(v4):
        return v4.v(lambda a: a.unsqueeze(2).to_broadcast([128, 4, 128]))

    def bc_m(m):
        return m.v(lambda a: a.unsqueeze(1).to_broadcast([128, 4, 128]))

    def conv_state_out(blk):
        samp = blk["kind"] == "s"
        if not blk["last"]:
            return
        ncol = 48 if samp else 3
        for g3 in range(3):
            ps = bank()
            for c in range(4):
                cc = g3 * 4 + c
                src = carry[:, cc * 48:(cc + 1) * 48] if samp else carry3[:, cc, :]
                P.tr(ps[0:ncol, c * 128:(c + 1) * 128], src, ident32)
            P.copy("dve", sct[0:ncol, :], ps[0:ncol, :])
            if samp:
                P.dma("sp", sc_s[:, g3 * 512:(g3 + 1) * 512], sct[0:48, :])
            else:
                P.dma("sp", sc_p[blk["seq"] * 3:blk["seq"] * 3 + 3, g3 * 512:(g3 + 1) * 512], sct[0:3, :])

    def mix_prologue(blk, plan):
        nt = blk["nt"]
        t = nt * 128
        samp = blk["kind"] == "s"
        if samp:
            e4 = ext3.v(lambda a: a[:, :, :176].rearrange("p c (b j) -> p c b j", j=11))
            for g3 in range(3):
                P.dma("sp", sct[0:48, :], st_conv[:, g3 * 512:(g3 + 1) * 512])
                ps = bank()
                for c in range(4):
                    P.tr(ps[:, c * 48:(c + 1) * 48], sct[0:48, c * 128:(c + 1) * 128], ident32[0:48, 0:48])
                P.copy("dve", carry[:, g3 * 192:(g3 + 1) * 192], ps[:, 0:192])
        for grp in range(3):
            W = ws.get(plan["win"][grp], AHEAD)
            W3 = k8(W, 512)
            if samp:
                P.copy("pool", e4[:, :, :, 0:3], carry_s[:, grp * 4:(grp + 1) * 4, :, :])
            elif blk["first"]:
                P.memset("pool", ext3[:, :, 0:3], 0.0)
            else:
                P.copy("pool", ext3[:, :, 0:3], carry3[:, grp * 4:(grp + 1) * 4, :])
            for c in range(4):
                ps = bank()
                for kc in range(8):
                    P.mm(ps[:, :t], W3[:, kc, c * 128:(c + 1) * 128], xnT3[:, kc, :t], start=(kc == 0), stop=(kc == 7))
                if samp:
                    P.copy("act", e4[:, c, :, 3:11], ps[:, :128].v(lambda a: a.rearrange("p (b j) -> p b j", j=8)))
                else:
                    P.copy("act", ext3[:, c, 3:3 + t], ps[:, :t])
            for c in range(4):
                cc = grp * 4 + c
                a_ = acc[c % 2]
                eng = "dve"
                if samp:
                    av = a_[:, :128].v(lambda a: a.rearrange("p (b j) -> p b j", j=8))
                    src = lambda j: e4[:, c, :, j:j + 8]
                else:
                    av = a_[:, :t]
                    src = lambda j: ext3[:, c, j:j + t]
                P.ts(eng, av, src(3), sml[:, OFF_CW + 3 * 12 + cc:OFF_CW + 3 * 12 + cc + 1])
                for j in (2, 1, 0):
                    P.stt(eng, av, src(j), sml[:, OFF_CW + j * 12 + cc:OFF_CW + j * 12 + cc + 1], av, ALU.mult, ALU.add)
                P.act(qkvT3[:, cc, :t], a_[:, :t], AF.Silu)
            if samp:
                P.copy("pool", carry_s[:, grp * 4:(grp + 1) * 4, :, :], e4[:, :, :, 8:11])
            else:
                P.copy("pool", carry3[:, grp * 4:(grp + 1) * 4, :], ext3[:, :, t:t + 3])
        W = ws.get(plan["win"][3], AHEAD)
        W3 = k8(W, 520)
        for n in range(nt):
            ps, ps2 = bank(), bank()
            for kc in range(8):
                P.mm(ps, xnT3[:, kc, n * 128:(n + 1) * 128], W3[:, kc, 0:512], start=(kc == 0), stop=(kc == 7))
            for kc in range(8):
                P.mm(ps2[:, 0:8], xnT3[:, kc, n * 128:(n + 1) * 128], W3[:, kc, 512:520], start=(kc == 0), stop=(kc == 7))
            sz = szt[n % 2]
            P.act(sz, ps, AF.Silu)
            P.tt("pool", gz3[:, n, :], sz, gnb, ALU.mult)
            P.copy("dve", ab3[:, n, :], ps2[:, 0:8])
        a_v, b_v = ab3[:, :nt, 0:4], ab3[:, :nt, 4:8]
        G = lambda k: gt4[:, k, :nt, :]
        P.tt("dve", G(0), a_v, dtb.v(lambda a: a.unsqueeze(1).to_broadcast([128, nt, 4])), ALU.add)
        P.act(G(1), G(0), AF.Exp)
        P.act(G(1), G(1), AF.Ln, bias=1.0)
        P.tt("dve", G(2), G(1), negA.v(lambda a: a.unsqueeze(1).to_broadcast([128, nt, 4])), ALU.mult)
        P.act(G(3), b_v, AF.Exp, scale=-1.0)
        P.ts("dve", G(3), G(3), 1.0, None, ALU.add)
        P.recip(G(4), G(3))

    def mix_loop(blk, plan):
        nt = blk["nt"]
        t = nt * 128
        samp = blk["kind"] == "s"
        conv_state_out(blk)
        MP = cfg.get("mparts", "sdo")
        W4 = k8(ws.get(plan["win"][4], AHEAD), 512)
        W5 = k8(ws.get(plan["win"][5], AHEAD, lo=plan["win"][4]), 256)
        if samp and cfg.get("merge", True):
            def gd_():
                gdn_p1(blk, 0)
                gdn_p2(blk, 0)
            par2(lambda: swa_tile(blk, 0, W4, W5), gd_)
        elif not cfg.get("merge", True):
            for n in range(nt):
                swa_tile(blk, n, W4, W5)
                gdn_p1(blk, n)
                gdn_p2(blk, n)
        else:
            POOL_S, POOL_1, POOL_2 = cfg.get("pools", ([0, 1, 2], [3, 4], [5, 6, 7]))

            LA = cap(lambda: swa_tile(blk, 0, W4, W5), POOL_S)
            LB = cap(lambda: gdn_p1(blk, 0), POOL_1)
            (P.merge_sched if cfg.get('sched', True) else P.merge_n)([LA, LB])
            for n in range(nt):
                Ls = [cap(lambda: gdn_p2(blk, n), POOL_2)]
                if n + 1 < nt:
                    Ls.append(cap(lambda: gdn_p1(blk, n + 1), POOL_1))
                    Ls.append(cap(lambda: swa_tile(blk, n + 1, W4, W5), POOL_S))
                (P.merge_sched if cfg.get('sched', True) else P.merge_n)(Ls)

    def mix_wout(blk, plan):
        nt = blk["nt"]
        if DBG:
            P.dma("pool", dbg_og, ogT)
            P.dma("pool", dbg_os, oTs)
            for n_ in range(NT):
                P.dma("sp", dbg_h1[:, n_ * D:(n_ + 1) * D], hT[n_])
        Wo = [ws.get(plan["wout"][i], AHEAD, lo=plan["wout"][0]) for i in range(3)]
        Wo1 = Wo[0].v(lambda a: a[:, :4096].rearrange("p (c n) -> p c n", c=4))
        Wo2 = [w.v(lambda a: a[0:64, :4096].rearrange("p (c n) -> p c n", c=4)) for w in Wo[1:]]
        for n in range(nt):
            for half in range(2):
                ps = bank()
                for hh in range(4):
                    P.mm(ps, ogT3[:, hh, n * 128:(n + 1) * 128], Wo1[:, hh, half * 512:(half + 1) * 512], start=(hh == 0), stop=False)
                for hh in range(8):
                    P.mm(ps, oTs3[:, hh, n * 128:(n + 1) * 128], Wo2[hh // 4][:, hh % 4, half * 512:(half + 1) * 512],
                         start=False, stop=(hh == 7))
                hv = h3[:, n, half * 512:(half + 1) * 512]
                P.tt("dve", hv, ps, hv, ALU.add)

    swa_state = {"par": 0}

    def swa_tile(blk, n, W4, W5):
        samp = blk["kind"] == "s"
        ti = 16 if samp else blk["bi"] * NT + n
        has_prev = samp or not (blk["first"] and n == 0)
        cur = swa_state["par"] % 2
        prev = 1 - cur
        swa_state["par"] += 1
        ps_q, ps_kv = bank(), bank()
        for kc in range(8):
            P.mm(ps_q, xnT3[:, kc, n * 128:(n + 1) * 128], W4[:, kc, 0:512], start=(kc == 0), stop=(kc == 7))
        for kc in range(8):
            P.mm(ps_kv[:, 0:256], xnT3[:, kc, n * 128:(n + 1) * 128], W5[:, kc, 0:256], start=(kc == 0), stop=(kc == 7))
        P.copy("act", qkv_tm[:, 0:512], ps_q)
        P.copy("dve", qkv_tm[:, 512:768], ps_kv[:, 0:256])
        X = qkv_tm.v(lambda a: a[:, 0:640].rearrange("p (h d) -> p h d", h=10))
        x1, x2 = X[:, :, 0:8], X[:, :, 8:16]
        cosb = cst[:, C_COS + ti * 8:C_COS + ti * 8 + 8].v(lambda a: a.unsqueeze(1).to_broadcast([128, 10, 8]))
        sinb = cst[:, C_SIN + ti * 8:C_SIN + ti * 8 + 8].v(lambda a: a.unsqueeze(1).to_broadcast([128, 10, 8]))
        R = lambda k: rtmp[:, k * 80:(k + 1) * 80].v(lambda a: a.rearrange("p (h d) -> p h d", h=10))
        P.tt("pool", R(0), x1, cosb, ALU.mult)
        P.tt("pool", R(1), x2, sinb, ALU.mult)
        P.tt("pool", R(2), x2, cosb, ALU.mult)
        P.tt("pool", R(3), x1, sinb, ALU.mult)
        P.tt("pool", x1, R(0), R(1), ALU.subtract)
        P.tt("pool", x2, R(2), R(3), ALU.add)
        P.copy("dve", qkb, qkv_tm[:, 0:640])
        Va = Vaug[cur].v(lambda a: a.rearrange("p (k d) -> p k d", k=2))
        P.copy("pool", Va[:, :, 0:64], qkv_tm[:, 640:768].v(lambda a: a.rearrange("p (k d) -> p k d", k=2)))
        if blk["last"] and not samp and n == blk["nt"] - 1:
            P.dma("sp", kk_p[blk["seq"]], qkv_tm[:, 512:640])
            P.dma("sp", vv_p[blk["seq"]], qkv_tm[:, 640:768])
        if samp:
            for b in range(16):
                P.dma("sp", kk_s[b, 120:128, :], qkv_tm[b * 8:(b + 1) * 8, 512:640])
                P.dma("sp", vv_s[b, 120:128, :], qkv_tm[b * 8:(b + 1) * 8, 640:768])
        pq = bf(bank())
        for hh in range(8):
            P.tr(pq[0:64, hh * 128:(hh + 1) * 128], qkb[:, hh * 64:(hh + 1) * 64], identb)
        pk = bf(bank())
        for hh in range(2):
            P.tr(pk[0:64, hh * 128:(hh + 1) * 128], qkb[:, 512 + hh * 64:512 + (hh + 1) * 64], identb)
        P.copy("act", qTs, pq[0:64, :])
        P.copy("dve", kTs[cur], pk[0:64, 0:256])
        if samp:
            swa_sample(n, cur)
            return
        for kvh in range(2):
            rhs_q = qTs[:, kvh * 512:(kvh + 1) * 512]
            sc = bank()
            P.mm(sc, kTs[cur][:, kvh * 128:(kvh + 1) * 128], rhs_q)
            P.act(PTs[0], sc, AF.Exp, scale=0.125)
            mcur = cb(C_US) if samp else cb(C_U)
            P.tt("pool", PTs[0].v(lambda a: a.rearrange("p (g q) -> p g q", g=4)),
                 PTs[0].v(lambda a: a.rearrange("p (g q) -> p g q", g=4)), bc_m(mcur), ALU.mult)
            o_ps = bank()
            if samp:
                swa_sample_cache(kvh, rhs_q, o_ps, cur)
            else:
                if has_prev:
                    sp_ = bank()
                    P.mm(sp_, kTs[prev][:, kvh * 128:(kvh + 1) * 128], rhs_q)
                    P.act(PTs[1], sp_, AF.Exp, scale=0.125)
                    P.tt("pool", PTs[1].v(lambda a: a.rearrange("p (g q) -> p g q", g=4)),
                         PTs[1].v(lambda a: a.rearrange("p (g q) -> p g q", g=4)), bc_m(cb(C_SU)), ALU.mult)
                P.mm(o_ps[0:65, :], Vaug[cur][:, kvh * 65:(kvh + 1) * 65], PTs[0], start=True, stop=not has_prev)
                if has_prev:
                    P.mm(o_ps[0:65, :], Vaug[prev][:, kvh * 65:(kvh + 1) * 65], PTs[1], start=False, stop=True)
            dv_ = den[64:65, :].v(lambda a: a.rearrange("p (g q) -> p g q", g=4))
            P.tt("dve", dv_, o_ps[64:65, :].v(lambda a: a.rearrange("p (g q) -> p g q", g=4)),
                 esk[64:65, kvh * 4:(kvh + 1) * 4].v(lambda a: a.unsqueeze(2).to_broadcast([1, 4, 128])), ALU.add)
            P.act(den[64:65, :], den[64:65, :], AF.Ln)
            P.act(den[64:65, :], den[64:65, :], AF.Exp, scale=-1.0)
            bcp = bank()
            P.mm(bcp[0:64, :], cst[64:65, C_ONE:C_ONE + 64], den[64:65, :])
            P.copy("act", bcs, bcp[0:64, :])
            P.tt("dve", oTs3[:, kvh * 4:(kvh + 1) * 4, n * 128:(n + 1) * 128],
                 o_ps[0:64, :].v(lambda a: a.rearrange("p (g q) -> p g q", g=4)),
                 bcs.v(lambda a: a.rearrange("p (g q) -> p g q", g=4)), ALU.mult)

    def g4(x, w=128):
        return x.v(lambda a: a.rearrange("p (g q) -> p g q", g=4))

    def swa_sample(n, cur):
        bgt = lambda x: x.v(lambda a: a.rearrange("p (b g t) -> p b g t", g=4, t=8))
        P.dma("sp", kk_s[:, 0:120, :], ck_d[:, 8:128, :])
        P.dma("sp", vv_s[:, 0:120, :], cv_d[:, 8:128, :])
        qre = qkv_tm.v(lambda a: a[0:64, 0:512].bitcast(BF16))
        for kvh in range(2):
            P.copy("pool", bgt(qre[:, kvh * 512:(kvh + 1) * 512]),
                   qTs[:, kvh * 512:(kvh + 1) * 512].v(lambda a: a.rearrange("p (g b t) -> p b g t", g=4, t=8)))
        o_ps = [bank(pin=True), bank(pin=True)]
        us4 = cb(C_US).v(lambda a: a.rearrange("p (b t) -> p b t", t=8).unsqueeze(2).to_broadcast([128, 16, 4, 8]))
        for kvh in range(2):
            rhs_q = qre[:, kvh * 512:(kvh + 1) * 512]
            sc = bank()
            P.mm(sc, kTs[cur][:, kvh * 128:(kvh + 1) * 128], rhs_q)
            P.act(PTs[kvh], sc, AF.Exp, scale=0.125)
            P.tt("pool", bgt(PTs[kvh]), bgt(PTs[kvh]), us4, ALU.mult)
            P.mm(o_ps[kvh][0:65, :], Vaug[cur][:, kvh * 65:(kvh + 1) * 65], PTs[kvh], start=True, stop=False)
        for half in range(2):
            b0 = half * 8
            ckv = xs[0].v(lambda a: a.rearrange("p (b d) -> p b d", b=8))
            P.dma("pool", ckv, ck_d[b0:b0 + 8].rearrange("b j d -> j b d"))
            for kvh in range(2):
                P.dma("pool", Vc.v(lambda a: a.rearrange("p (b d) -> p b d", b=8)[:, :, kvh * 65:kvh * 65 + 64]),
                      cv_d[b0:b0 + 8].rearrange("b j d -> j b d")[:, :, kvh * 64:(kvh + 1) * 64])
            for hb in range(2):
                pk = bf(bank())
                for i in range(8):
                    bl, kvh = (hb * 8 + i) // 2, (hb * 8 + i) % 2
                    P.tr(pk[0:64, i * 128:(i + 1) * 128], ckv[:, bl, kvh * 64:(kvh + 1) * 64], identb)
                P.copy("act" if hb == 0 else "dve", kcT[:, hb * 1024:(hb + 1) * 1024], pk[0:64, :])
            for kvh in range(2):
                scc = bank()
                for bl in range(8):
                    bb = b0 + bl
                    P.mm(scc[:, bl * 32:(bl + 1) * 32], kcT[:, (bl * 2 + kvh) * 128:(bl * 2 + kvh + 1) * 128],
                         qre[:, kvh * 512 + bb * 32:kvh * 512 + (bb + 1) * 32])
                P.act(PTc, scc[:, 0:256], AF.Exp, scale=0.125)
                pc3 = PTc.v(lambda a: a.rearrange("p (c t) -> p c t", t=8))
                P.tt("pool", pc3, pc3, cb(C_MC)[:, 0:8].v(lambda a: a.unsqueeze(1).to_broadcast([128, 32, 8])), ALU.mult)
                for bl in range(8):
                    bb = b0 + bl
                    P.mm(o_ps[kvh][0:65, bb * 32:(bb + 1) * 32], Vc[:, bl * 130 + kvh * 65:bl * 130 + (kvh + 1) * 65],
                         PTc[:, bl * 32:(bl + 1) * 32], start=False, stop=(half == 1 and bl == 7))
        for kvh in range(2):
            op_ = o_ps[kvh]
            P.tt("dve", bgt(den[64:65, :]), bgt(op_[64:65, :]),
                 esk[64:65, kvh * 4:(kvh + 1) * 4].v(lambda a: a.unsqueeze(1).unsqueeze(3).to_broadcast([1, 16, 4, 8])), ALU.add)
            P.act(den[64:65, :], den[64:65, :], AF.Ln)
            P.act(den[64:65, :], den[64:65, :], AF.Exp, scale=-1.0)
            bcp = bank()
            P.mm(bcp[0:64, :], cst[64:65, C_ONE:C_ONE + 64], den[64:65, :])
            P.copy("act", bcs, bcp[0:64, :])
            P.tt("dve", oTs3[:, kvh * 4:(kvh + 1) * 4, n * 128:(n + 1) * 128].v(lambda a: a.rearrange("p g (b t) -> p b g t", t=8)),
                 bgt(op_[0:64, :]), bgt(bcs), ALU.mult)
        unpin_all()

    pp = {"k": 0}

    hand = {}

    def r4(bk):
        return bk.v(lambda a: a.rearrange("p (c d) -> p c d", c=4))

    def trs(src3):
        pb_ = bf(bank())
        p3 = pb_.v(lambda a: a[:, 0:512].rearrange("p (c d) -> p c d", c=4))
        for hh in range(4):
            P.tr(p3[:, hh, :], src3[:, hh, :], identb)
        return p3

    def gdn_p1(blk, n):
        samp = blk["kind"] == "s"
        par = pp["k"] % 2
        pp["k"] += 1
        H = dict(Am=Am2[par], nAm=nAm2[par], B0=B02[par], qkT=qkT2[par], kbg=kbg2[par], kdec=kdec2[par],
                 vb=vb2[par], qdT=qdT2[par], gsc=gsc1[par])
        hand[n] = H
        gsc_ = H["gsc"]
        cs = slice(n * 128, (n + 1) * 128)
        qTr, kTr, vTr = qkvT3[:, 0:4, cs], qkvT3[:, 4:8, cs], qkvT3[:, 8:12, cs]
        U32 = c32(C_US) if samp else c32(C_U)
        SU32 = c32(C_SUS) if samp else c32(C_SU)
        ON32 = c32(C_ONES) if samp else c32(C_ONE)
        STb = cb(C_SUS) if samp else cb(C_SU)
        g_n, beta_n = gt4[:, 2, n, :], gt4[:, 4, n, :]
        sq3 = sqb.v(lambda a: a.rearrange("p (c d) -> p c d", c=8))
        P.tt("pool", sq3, qkvT3[:, 0:8, cs], qkvT3[:, 0:8, cs], ALU.mult)
        ssq, ssk = bank(), bank()
        P.mm(ssq, onesb, sqb[:, 0:512])
        P.mm(ssk, onesb, sqb[:, 512:1024])
        P.ts("dve", rn[:, 0:512], ssq, EPS, None, ALU.add)
        P.ts("dve", rn[:, 512:1024], ssk, EPS, None, ALU.add)
        P.act(rn, rn, AF.Ln)
        P.act(rn, rn, AF.Exp, scale=-0.5)
        rn3 = rn.v(lambda a: a.rearrange("p (c d) -> p c d", c=8))
        P.stt("dve", qTn, qTr, float(DK) ** -0.5, rn3[:, 0:4, :], ALU.mult, ALU.mult)
        P.tt("dve", kTn, kTr, rn3[:, 4:8, :], ALU.mult)
        gs = bank()
        P.mm(gs[:, 0:4], U32, g_n)
        P.mm(gs[:, 4:8], ON32, g_n)
        P.copy("dve", gsc_[:, 0:8], gs[:, 0:8])
        P.tt("dve", gsc_[:, 8:12], gsc_[:, 4:8], gsc_[:, 0:4], ALU.subtract)
        P.act(gsc_[:, 16:28], gsc_[:, 0:12], AF.Exp)
        e_gc, e_last, e_rem = gsc_[:, 16:20], gsc_[:, 20:24], gsc_[:, 24:28]
        H["e_last"] = e_last
        P.tt("dve", gsc_[:, 32:36], beta_n, e_gc, ALU.mult)
        c_kbg = gsc_[:, 32:36]
        P.tt("pool", dgb, bc_m(identb), bc_h(e_gc), ALU.mult)
        rb = bank()
        P.mm(rb, onesb, flat(dgb))
        P.tt("dve", H["qdT"], qTn, r4(rb), ALU.mult)
        P.tt("pool", Ug, bc_m(U32), bc_h(g_n), ALU.mult)
        dm = bank()
        dm3 = r4(dm)
        for hh in range(4):
            P.mm(dm3[:, hh, :], Ug[:, hh, :], SU32)
        P.act(Fb, dm3, AF.Exp)
        P.tt("pool", Fs, Fb, bc_m(STb), ALU.mult)
        P.tt("pool", Fm, Fs, bc_m(identb), ALU.add)
        P.tt("pool", Fsb, Fs, bc_h(beta_n), ALU.mult)
        kk, qk = bank(), bank()
        kk3, qk3 = r4(kk), r4(qk)
        for hh in range(4):
            P.mm(kk3[:, hh, :], kTn[:, hh, :], kTn[:, hh, :])
        for hh in range(4):
            P.mm(qk3[:, hh, :], qTn[:, hh, :], kTn[:, hh, :])
        P.tt("dve", H["Am"], kk3, Fsb, ALU.mult)
        P.tt("dve", QKm, qk3, Fm, ALU.mult)
        P.ts("pool", H["nAm"], H["Am"], -1.0, 0.0, ALU.mult, ALU.add)
        b_ps = trs(H["Am"])
        P.copy("act", flat(H["B0"]), flat(b_ps))
        qkT_ps = trs(QKm)
        P.copy("act", flat(H["qkT"]), flat(qkT_ps))
        ktm = trs(kTn)
        P.tt("dve", H["kbg"], ktm, bc_h(c_kbg), ALU.mult)
        P.tt("dve", H["kdec"], ktm, bc_h(e_rem), ALU.mult)
        vtm = trs(vTr)
        P.tt("dve", H["vb"], vtm, bc_h(beta_n), ALU.mult)

    def gdn_p2(blk, n):
        samp = blk["kind"] == "s"
        H = hand.pop(n)
        R_ps = bank(pin=True)
        R3 = r4(R_ps)
        for hh in range(4):
            P.mm(R3[:, hh, :], H["nAm"][:, hh, :], identb, start=(hh == 0), stop=False, skip_group_check=True)
            P.mm(R3[:, hh, :], identb, identb, start=False, stop=False, skip_group_check=True)
        P.copy("dve", flat(Rbf[0]), R_ps)
        cur_P, cur_PT, cur_R = H["B0"], H["Am"], Rbf[0]
        for k in range(1, 7):
            nP, nPT, nR = Pm[k % 2], PT[k % 2], Rbf[k % 2]
            if k < 6:
                p_ps = bank()
                p3 = r4(p_ps)
                for hh in range(4):
                    P.mm(p3[:, hh, :], cur_PT[:, hh, :], cur_P[:, hh, :])
            pt_ps = bank()
            pt3 = r4(pt_ps)
            for hh in range(4):
                P.mm(pt3[:, hh, :], cur_P[:, hh, :], cur_PT[:, hh, :])
            P.copy("dve", nPT, pt3)
            if k < 6:
                P.copy("act", flat(nP), flat(p3))
            for hh in range(4):
                P.mm(R3[:, hh, :], nPT[:, hh, :], cur_R[:, hh, :], start=False, stop=(k == 6), skip_group_check=True)
            P.copy("act", flat(nR), R_ps)
            cur_P, cur_PT, cur_R = nP, nPT, nR
        Rf = cur_R
        for i_, bk_ in enumerate(banks):
            if bk_.uid == R_ps.uid:
                pinned.discard(i_)
        w_ps = bank()
        w3 = r4(w_ps)
        for hh in range(4):
            P.mm(w3[:, hh, :], H["kbg"][:, hh, :], Rf[:, hh, :])
        P.ts("dve", negwT, w3, -1.0)
        if samp:
            gdn_sample_state(n, Rf, H)
            return
        if blk["first"] and n == 0:
            P.memset("pool", S32, 0.0)
            P.memset("pool", Sbf, 0.0)
        v_ps = bank()
        v3 = r4(v_ps)
        for hh in range(4):
            P.mm(v3[:, hh, :], Rf[:, hh, :], H["vb"][:, hh, :], start=True, stop=False)
            P.mm(v3[:, hh, :], negwT[:, hh, :], Sbf3[:, hh, :], start=False, stop=True)
        P.copy("act", flat(vnew), flat(v3))
        o_ps = bank()
        o3 = r4(o_ps)
        for hh in range(4):
            P.mm(o3[:, hh, :], H["qkT"][:, hh, :], vnew[:, hh, :], start=True, stop=False)
            P.mm(o3[:, hh, :], H["qdT"][:, hh, :], Sbf3[:, hh, :], start=False, stop=True)
        s_ps = bank()
        s3 = r4(s_ps)
        for hh in range(4):
            P.mm(s3[:, hh, :], H["kdec"][:, hh, :], vnew[:, hh, :])
        P.tt("pool", stmp, S32_3, bc_h(H["e_last"]), ALU.mult)
        P.tt("dve", S32_3, stmp, s3, ALU.add)
        P.copy("act", Sbf, S32)
        if blk["last"] and n == blk["nt"] - 1:
            P.dma("sp", sg_p[blk["seq"]].rearrange("h k v -> k h v"), S32_3)
        gdn_out(n, o3)

    def gdn_out(n, o3):
        gsc = gsc2
        P.act(osq, o3, AF.Square)
        P.add("dve", lambda e: e.tensor_reduce(gsc[:, 40:44].ap, osq.ap, AX.X, ALU.add), [osq], [gsc])
        P.ts("dve", gsc[:, 44:48], gsc[:, 40:44], 1.0 / 128, EPS, ALU.mult, ALU.add)
        P.act(gsc[:, 44:48], gsc[:, 44:48], AF.Ln)
        P.act(gsc[:, 44:48], gsc[:, 44:48], AF.Exp, scale=-0.5)
        P.tt("dve", otmp, o3, bc_h(gsc[:, 44:48]), ALU.mult)
        P.tt("pool", og, otmp, gz3[:, n, :].v(lambda a: a.rearrange("p (c d) -> p c d", c=4)), ALU.mult)
        pb_ = bf(bank())
        p3 = pb_.v(lambda a: a[:, 0:512].rearrange("p (c d) -> p c d", c=4))
        for hh in range(4):
            P.tr(p3[:, hh, :], og[:, hh, :], identb)
        P.copy("dve", ogT3[:, :, n * 128:(n + 1) * 128], p3)

    def gdn_sample_state(n, Rf, H):
        e_last = H["e_last"]
        vb, qdT, qkT, kdec = H["vb"], H["qdT"], H["qkT"], H["kdec"]
        P.tt("pool", dgb, bc_m(identb), bc_h(e_last), ALU.mult)
        rbl = bank()
        P.mm(rbl, onesb, flat(dgb))
        elrb = Ug
        P.copy("act", flat(elrb), rbl)
        vT_ps, oT_ps = bank(pin=True), bank(pin=True)
        vT3, oT3 = r4(vT_ps), r4(oT_ps)
        for hh in range(4):
            P.mm(vT3[:, hh, :], vb[:, hh, :], Rf[:, hh, :], start=(hh == 0), stop=False)
        sbufs = [Sbf3, og]
        first = True
        for b_ in range(16):
            Sb = sbufs[b_ % 2]
            P.dma("pool", Sb, st_gdn[b_].rearrange("h k v -> k h v"))
            cs_ = slice(b_ * 8, (b_ + 1) * 8)
            for hh in range(4):
                P.mm(vT3[:, hh, cs_], Sb[:, hh, :], negwT[:, hh, cs_], start=False, stop=(b_ == 15 and hh == 3))
            for hh in range(4):
                P.mm(oT3[:, hh, cs_], Sb[:, hh, :], qdT[:, hh, cs_], start=first, stop=False)
                first = False
        vT_sb = Pm[0]
        P.copy("act", flat(vT_sb), vT_ps)
        pb_ = bf(bank())
        p3 = pb_.v(lambda a: a[:, 0:512].rearrange("p (c d) -> p c d", c=4))
        for hh in range(4):
            P.tr(p3[:, hh, :], vT_sb[:, hh, :], identb)
        P.copy("act", flat(vnew), flat(p3))
        for hh in range(4):
            P.mm(oT3[:, hh, :], vnew[:, hh, :], qkT[:, hh, :], start=False, stop=(hh == 3))
        P.copy("act", flat(otmp), oT_ps)
        unpin_all()
        o_ps = bank()
        o3 = r4(o_ps)
        for hh in range(4):
            P.tr(o3[:, hh, :], otmp[:, hh, :], ident32)
        gdn_out(n, o3)
        vms = [Pm[1], PT[0]]
        sin = [S32_3, R32]
        sout = [stmp, osq]
        for b_ in range(16):
            vm = vms[b_ % 2]
            P.ts("dve", vm, vnew, c32(C_ONES)[:, b_ * 8:b_ * 8 + 1])
            s_ps = bank()
            s3 = r4(s_ps)
            for hh in range(4):
                P.mm(s3[:, hh, :], kdec[:, hh, :], vm[:, hh, :])
            si, so = sin[b_ % 2], sout[b_ % 2]
            P.dma("sp", si, st_gdn[b_].rearrange("h k v -> k h v"))
            P.tt("pool", so, si, elrb[:, :, b_ * 8:b_ * 8 + 1].v(lambda a: a.to_broadcast([128, 4, 128])), ALU.mult)
            P.tt("dve", so, so, s3, ALU.add)
            P.dma("sp", sg_s[b_].rearrange("h k v -> k h v"), so)

    def ple_final(blk, plan):
        nt = blk["nt"]
        t = nt * 128
        samp = blk["kind"] == "s"
        psrc = p_s if samp else p_p
        ydst = y_s if samp else y_p
        t0 = blk["t0"]
        pst3 = pst.v(lambda a: a.rearrange("p (n d) -> p n d", n=NT))
        pstb3 = pstb.v(lambda a: a.rearrange("p (n d) -> p n d", n=NT))
        pT3 = pT.v(lambda a: a.rearrange("p (c t) -> p c t", c=2))
        P.dma("sp", pst3[:, :nt, :], psrc[t0:t0 + t, :].rearrange("(n p) d -> p n d", p=128))
        P.copy("pool", pstb3[:, :nt, :], pst3[:, :nt, :])
        for c in range(2):
            pb_ = bf(bank())
            for n in range(nt):
                P.tr(pb_[:, n * 128:(n + 1) * 128], pstb3[:, n, c * 128:(c + 1) * 128], identb)
            P.copy("act", pT3[:, c, :t], pb_[:, :t])
        Wg = [k8(ws.get(plan["ple"][i], AHEAD, lo=plan["ple"][0]), 512) for i in range(2)]
        Wp = ws.get(plan["ple"][2], AHEAD, lo=plan["ple"][0]).v(lambda a: a[:, :2048].rearrange("p (c n) -> p c n", c=2))
        k = 0
        for half in range(2):
            for n in range(nt):
                pg_, pp_ = bank(), bank()
                for kc in range(8):
                    P.mm(pg_, xnT3[:, kc, n * 128:(n + 1) * 128], Wg[half][:, kc, :], start=(kc == 0), stop=(kc == 7))
                for c in range(2):
                    P.mm(pp_, pT3[:, c, n * 128:(n + 1) * 128], Wp[:, c, half * 512:(half + 1) * 512], start=(c == 0), stop=(c == 1))
                sb_ = sig[k % 2]
                k += 1
                P.act(sb_, pg_, AF.Sigmoid)
                P.tt("dve", sb_, sb_, pp_, ALU.mult)
                hv = h3[:, n, half * 512:(half + 1) * 512]
                P.tt("pool", hv, hv, sb_, ALU.add)

    def final_store(blk):
        nt = blk["nt"]
        samp = blk["kind"] == "s"
        ydst = y_s if samp else y_p
        t0 = blk["t0"]
        for n in range(nt):
            P.act(junk, h3[:, n, :], AF.Square, accum=ss[:, n:n + 1])
            P.ts("dve", rstd[:, n:n + 1], ss[:, n:n + 1], 1.0 / D, EPS, ALU.mult, ALU.add)
            P.act(rstd[:, n:n + 1], rstd[:, n:n + 1], AF.Ln)
            P.act(rstd[:, n:n + 1], rstd[:, n:n + 1], AF.Exp, scale=-0.5)
            yt = ytile[cnt["yt"] % 2]
            cnt["yt"] += 1
            P.stt("dve", yt, h3[:, n, :], rstd[:, n:n + 1], nfb, ALU.mult, ALU.mult)
            P.dma("sp", ydst[t0 + n * 128:t0 + (n + 1) * 128, :], yt)

    def load_x(blk):
        xsrc = x_s if blk["kind"] == "s" else x_p
        for n in range(blk["nt"]):
            P.dma("sp", hT[n], xsrc[blk["t0"] + n * 128:blk["t0"] + (n + 1) * 128, :])

    for bi, blk in enumerate(blocks):
        nt = blk["nt"]
        pl = plans[bi]
        if bi == 0:
            load_x(blk)
            rmsnorm_fm(gcol(0), nt)
        ffn_gu(gcol(0), pl["f1"], ffn1_down, nt, do_norm=False)
        if blk["kind"] == "s" or not cfg.get("ovl_pro", True):
            par2(lambda: ffn_down(nt), lambda: rmsnorm_fm(gcol(1), nt))
            P.fence()
            mix_prologue(blk, pl)
        else:
            def pro_():
                rmsnorm_fm(gcol(1), nt)
                mix_prologue(blk, pl)
            par2(lambda: ffn_down(nt), pro_)
            P.fence()
        mix_loop(blk, pl)
        par2(lambda: mix_wout(blk, pl), lambda: rmsnorm_fm(gcol(2), nt))
        if DBG:
            for n_ in range(NT):
                P.dma("sp", dbg_h[:, n_ * D:(n_ + 1) * D], hT[n_])
        P.fence()
        ffn_gu(gcol(2), pl["f2"], ffn2_down, nt, do_norm=False)
        par2(lambda: ffn_down(nt), lambda: rmsnorm_fm(gcol(3), nt))
        P.fence()
        ple_final(blk, pl)
        if bi + 1 < len(blocks):
            nb_ = blocks[bi + 1]

            def nxt():
                load_x(nb_)
                rmsnorm_fm(gcol(0), nb_["nt"])
            par2(lambda: final_store(blk), nxt)
        else:
            final_store(blk)
        P.fence()

    P.emit()
    es.close()
    return nc


_NC_CACHE = {}


def kernel(x_prompt, x_sample, state_gdn, state_conv, cache_swa_k, cache_swa_v, p_prompt, p_sample,
           norm_ffn1, ffn1_gu, ffn1_down, norm_mix, w_in, conv_w, a_log, dt_bias, gdn_norm, sinks,
           w_out, norm_ffn2, ffn2_gu, ffn2_down, norm_ple, ple_proj, ple_gate, norm_final):
    f = lambda a: np.ascontiguousarray(np.asarray(a), dtype=np.float32)
    x_prompt, x_sample, state_gdn, state_conv = f(x_prompt), f(x_sample), f(state_gdn), f(state_conv)
    cache_swa_k, cache_swa_v, p_prompt, p_sample = f(cache_swa_k), f(cache_swa_v), f(p_prompt), f(p_sample)
    nc = bass.Bass("TRN2", target_bir_lowering=False)
    build(nc, dict(nseq=2, seq=2048, sample=True))
    smalls = make_smalls(f(norm_ffn1)[0], f(norm_mix)[0], f(norm_ffn2)[0], f(norm_ple)[0], f(conv_w)[0], f(a_log)[0],
                         f(dt_bias)[0], f(gdn_norm)[0], f(sinks)[0], f(norm_final))
    consts = make_consts()
    shared = dict(ffn1_gu=f(ffn1_gu)[0], ffn1_down=f(ffn1_down)[0], w_in=f(w_in)[0], w_out=f(w_out)[0],
                  ffn2_gu=f(ffn2_gu)[0], ffn2_down=f(ffn2_down)[0], ple_proj=f(ple_proj)[0], ple_gate=f(ple_gate)[0],
                  smalls=smalls, consts=consts)
    in_maps = []
    for c in range(NCORES):
        sp, ss_ = slice(2 * c, 2 * c + 2), slice(16 * c, 16 * c + 16)
        m = dict(shared)
        m.update(x_p=f(x_prompt[sp].reshape(4096, 1024)), p_p=f(p_prompt[0, sp].reshape(4096, 256)),
                 x_s=f(x_sample[ss_].reshape(128, 1024)), p_s=f(p_sample[0, ss_].reshape(128, 256)),
                 st_gdn=f(state_gdn[0, ss_]), st_conv=f(state_conv[0, ss_].reshape(48, 1536)),
                 ck=f(cache_swa_k[0, ss_].reshape(16, 128, 128)), cv=f(cache_swa_v[0, ss_].reshape(16, 128, 128)))
        in_maps.append(m)
    res = run_bass_kernel_spmd(nc, in_maps, core_ids=list(range(NCORES)))
    R_ = res.results
    cat = lambda k, shp: np.concatenate([np.asarray(r[k], dtype=np.float32).reshape(shp) for r in R_], axis=0)
    y_prompt = cat("y_p", (2, 2048, 1024))
    y_sample = cat("y_s", (16, 8, 1024))
    sgp = cat("sg_p", (2, 4, 128, 128))[None]
    scp = cat("sc_p", (2, 3, 1536))[None]
    kkp = cat("kk_p", (2, 128, 2, 64))[None]
    vvp = cat("vv_p", (2, 128, 2, 64))[None]
    sgs = cat("sg_s", (16, 4, 128, 128))[None]
    scs = cat("sc_s", (16, 3, 1536))[None]
    kks = cat("kk_s", (16, 128, 2, 64))[None]
    vvs = cat("vv_s", (16, 128, 2, 64))[None]
    return (y_prompt, y_sample, sgp, scp, kkp, vvp, sgs, scs, kks, vvs)
```

```python
import contextlib
import numpy as np
import concourse.bass as bass
import concourse.mybir as mybir
from concourse.bass_utils import run_bass_kernel_spmd

F32 = mybir.dt.float32
BF16 = mybir.dt.bfloat16
AF = mybir.ActivationFunctionType
ALU = mybir.AluOpType
AX = mybir.AxisListType

NCORES = 8
D = 1024
DFF = 2816
NFF = 22
INW = 2824
EPS = 1e-6


class Buf:
    def __init__(self, ap, uid, reg=None, psum=False):
        self.ap = ap
        self.uid = uid
        self.reg = reg
        self.psum = psum

    def __getitem__(self, k):
        return Buf(self.ap[k], self.uid, self.reg, self.psum)

    def v(self, fn):
        return Buf(fn(self.ap), self.uid, self.reg, self.psum)


class Op:
    __slots__ = ("eng", "fn", "reads", "writes", "dma", "idx", "pos", "waits", "inc", "slot", "slotval", "pre", "cost")

    def __init__(self, eng, fn, reads, writes, dma):
        self.eng, self.fn, self.reads, self.writes, self.dma = eng, fn, reads, writes, dma
        self.waits = []
        self.inc = None
        self.pre = None


def _ap(x):
    return x.ap if isinstance(x, Buf) else x


class Prog:
    ENGS = ["pe", "act", "dve", "pool", "sp"]
    NSLOT = 12
    WINDOW = 5
    KEEP_STREAM_ORDER = True

    def __init__(self, nc, es):
        self.nc, self.es = nc, es
        self.ops = []
        self.nuid = 0

    def newbuf(self, ap):
        self.nuid += 1
        return Buf(ap, self.nuid)

    def sb(self, name, shape, dt):
        t = self.es.enter_context(self.nc.sbuf_tensor(name, list(shape), dt))
        return self.newbuf(t[:])

    def ps(self, name, shape, dt):
        t = self.es.enter_context(self.nc.psum_tensor(name, list(shape), dt))
        b = self.newbuf(t[:])
        b.psum = True
        return b

    def alias(self, buf):
        self.nuid += 1
        return Buf(buf.ap, self.nuid, buf.reg)

    def make_arena(self, name, nbytes):
        t = self.es.enter_context(self.nc.sbuf_tensor(name, [128, nbytes // 2], BF16))
        self.nuid += 1
        self.arena_ap = t[:]
        self.arena_reg = self.nuid
        self.arena_bytes = nbytes
        self.fence_t = self.sb("fence_t", [128, 2], F32)
        self.regbuf = Buf(self.fence_t.ap, self.arena_reg)
        self.aoff = 0

    def arena_reset(self):
        self.aoff = 0

    def take(self, nelem, dt, parts=128):
        esz = 4 if dt == F32 else 2
        nb = (nelem * esz + 31) // 32 * 32
        assert self.aoff + nb <= self.arena_bytes, ("arena overflow", self.aoff, nb, self.arena_bytes)
        ap = self.arena_ap[0:parts, self.aoff // 2:(self.aoff + nelem * esz) // 2]
        if dt == F32:
            ap = ap.bitcast(F32)
        self.aoff += nb
        self.nuid += 1
        return Buf(ap, self.nuid, self.arena_reg)

    def fence(self):
        o = self.fence_t.ap
        self.add("pool", lambda e: e.memset(o, 0.0), [], [self.regbuf])

    def add(self, eng, fn, reads, writes, dma=False):
        r = [b.uid for b in reads if isinstance(b, Buf)]
        w = [b.uid for b in writes if isinstance(b, Buf)]
        w += [b.uid for b in reads if isinstance(b, Buf) and b.psum and b.uid not in w]
        for b in list(reads) + list(writes):
            if isinstance(b, Buf) and b.reg is not None and b.reg not in r:
                r.append(b.reg)
        op = Op(eng, fn, r, w, dma)
        op.idx = len(self.ops)
        n = 0
        for b in writes:
            if isinstance(b, Buf):
                k = 1
                for d in b.ap.shape[1:]:
                    k *= int(d)
                n = max(n, k)
        if eng == "pe":
            op.cost = 60.0 + 0.47 * n
        elif eng == "act":
            op.cost = 230.0 + 0.85 * n
        elif eng == "dve":
            op.cost = 70.0 + 1.1 * n
        elif eng == "pool":
            op.cost = 120.0 + 2.0 * n
        else:
            op.cost = 100.0
        if dma:
            op.cost = 2000.0
        self.ops.append(op)
        return op

    def mm(self, out, lhsT, rhs, start=True, stop=True, **kw):
        o, l, r = out.ap, lhsT.ap, rhs.ap
        self.add("pe", lambda e: e.matmul(o, l, r, start=start, stop=stop, **kw), [lhsT, rhs], [out])

    def tr(self, out, in_, ident):
        o, i, d = out.ap, in_.ap, ident.ap
        self.add("pe", lambda e: e.transpose(o, i, d), [in_, ident], [out])

    def act(self, out, in_, func, bias=0.0, scale=1.0, accum=None, eng="act"):
        o, i, b, s = out.ap, in_.ap, _ap(bias), _ap(scale)
        a = _ap(accum) if accum is not None else None
        self.add(eng, lambda e: e.activation(o, i, func, bias=b, scale=s, accum_out=a),
                 [in_, bias, scale], [out] + ([accum] if accum is not None else []))

    def ts(self, eng, out, in0, s1, s2=None, op0=ALU.mult, op1=None, accum=None):
        o, i, a1, a2 = out.ap, in0.ap, _ap(s1), _ap(s2)
        ac = _ap(accum) if accum is not None else None
        if op1 is None:
            f = lambda e: e.tensor_scalar(o, i, a1, None, op0)
        elif ac is None:
            f = lambda e: e.tensor_scalar(o, i, a1, a2, op0, op1)
        else:
            f = lambda e: e.tensor_scalar(o, i, a1, a2, op0, op1, accum_out=ac)
        self.add(eng, f, [in0, s1, s2], [out] + ([accum] if accum is not None else []))

    def tt(self, eng, out, in0, in1, op):
        o, a, b = out.ap, in0.ap, in1.ap
        self.add(eng, lambda e: e.tensor_tensor(o, a, b, op), [in0, in1], [out])

    def stt(self, eng, out, in0, scalar, in1, op0, op1):
        o, a, s, b = out.ap, in0.ap, _ap(scalar), in1.ap
        self.add(eng, lambda e: e.scalar_tensor_tensor(o, a, s, b, op0, op1), [in0, scalar, in1], [out])

    def copy(self, eng, out, in_):
        o, i = out.ap, in_.ap
        if eng == "act":
            self.add(eng, lambda e: e.copy(o, i), [in_], [out])
        else:
            self.add(eng, lambda e: e.tensor_copy(o, i), [in_], [out])

    def memset(self, eng, out, val):
        o = out.ap
        self.add(eng, lambda e: e.memset(o, val), [], [out])

    def recip(self, out, in_):
        o, i = out.ap, in_.ap
        self.add("dve", lambda e: e.reciprocal(o, i), [in_], [out])

    def dma(self, q, out, in_):
        o, i = _ap(out), _ap(in_)
        self.add(q, lambda e: e.dma_start(out=o, in_=i), [in_], [out], dma=True)

    def capture(self):
        self._saved = self.ops
        self.ops = []

    def end_capture(self):
        L = self.ops
        self.ops = self._saved
        return L

    def merge(self, LA, LB):
        na, nb = len(LA), len(LB)
        ia = ib = 0
        while ia < na or ib < nb:
            if ib >= nb or (ia < na and ia * nb <= ib * na):
                self.ops.append(LA[ia])
                ia += 1
            else:
                self.ops.append(LB[ib])
                ib += 1

    def merge_n(self, Ls):
        Ls = [L for L in Ls if L]
        pos = [0] * len(Ls)
        total = sum(len(L) for L in Ls)
        for _ in range(total):
            best, bi = None, -1
            for i, L in enumerate(Ls):
                if pos[i] < len(L):
                    f = (pos[i] + 0.5) / len(L)
                    if best is None or f < best:
                        best, bi = f, i
            self.ops.append(Ls[bi][pos[bi]])
            pos[bi] += 1

    SCHED_LAT = 900.0
    SAME_LAT = 0.0

    def merge_sched(self, Ls, lat=None):
        lat = self.SCHED_LAT if lat is None else lat
        ops = [op for L in Ls for op in L]
        n = len(ops)
        lastw, readers = {}, {}
        deps = [set() for _ in range(n)]
        for i, op in enumerate(ops):
            for u in op.reads:
                if u in lastw:
                    deps[i].add(lastw[u])
            for u in op.writes:
                if u in lastw:
                    deps[i].add(lastw[u])
                for r in readers.get(u, ()):
                    deps[i].add(r)
            deps[i].discard(i)
            for u in op.reads:
                readers.setdefault(u, []).append(i)
            for u in op.writes:
                lastw[u] = i
                readers[u] = []
        sid = []
        for si, L in enumerate(Ls):
            sid += [si] * len(L)
        last_on = {}
        if self.KEEP_STREAM_ORDER:
            for i, op in enumerate(ops):
                key = (sid[i], op.eng)
                if key in last_on:
                    deps[i].add(last_on[key])
                last_on[key] = i
        nsucc_ready = [len(d) for d in deps]
        succ = [[] for _ in range(n)]
        for i, d in enumerate(deps):
            for j in d:
                succ[j].append(i)
        finish = [0.0] * n
        ready_t = [0.0] * n
        teng = {}
        avail = [i for i in range(n) if nsucc_ready[i] == 0]
        order = []
        while avail:
            best, bt = None, None
            for i in avail:
                t = max(ready_t[i], teng.get(ops[i].eng, 0.0))
                if bt is None or t < bt or (t == bt and i < best):
                    best, bt = i, t
            avail.remove(best)
            op = ops[best]
            finish[best] = bt + op.cost
            teng[op.eng] = finish[best] if not op.dma else bt + 100.0
            order.append(best)
            for j in succ[best]:
                rt = finish[best] + (lat if ops[j].eng != op.eng else self.SAME_LAT)
                if rt > ready_t[j]:
                    ready_t[j] = rt
                nsucc_ready[j] -= 1
                if nsucc_ready[j] == 0:
                    avail.append(j)
        assert len(order) == n
        for i in order:
            self.ops.append(ops[i])

    def finalize(self):
        ops = self.ops
        for i, op in enumerate(ops):
            op.idx = i
        per = {e: [] for e in self.ENGS}
        for op in ops:
            op.pos = len(per[op.eng])
            per[op.eng].append(op)
        lastw, readers = {}, {}
        deps_all = []
        for op in ops:
            deps = set()
            for u in op.reads:
                if u in lastw:
                    deps.add(lastw[u])
            for u in op.writes:
                if u in lastw:
                    deps.add(lastw[u])
                for r in readers.get(u, ()):
                    deps.add(r)
            deps.discard(op.idx)
            for u in op.reads:
                readers.setdefault(u, []).append(op.idx)
            for u in op.writes:
                lastw[u] = op.idx
                readers[u] = []
            deps_all.append(deps)
        nd = {e: 0 for e in self.ENGS}
        for op in ops:
            if op.dma:
                k = nd[op.eng]
                nd[op.eng] += 1
                op.slot = k % self.NSLOT
                op.slotval = 16 * (k // self.NSLOT + 1)
        known = {e: {f: -1 for f in self.ENGS} for e in self.ENGS}
        kdma = {e: {} for e in self.ENGS}
        needmark = set()
        for op in ops:
            X = op.eng
            best = {}
            dmadeps = []
            for di in deps_all[op.idx]:
                d = ops[di]
                if d.dma:
                    dmadeps.append(d)
                    continue
                if d.eng == X and not op.dma:
                    if X == "pe":
                        continue
                    if op.pos - d.pos > self.WINDOW:
                        continue
                if d.pos > best.get(d.eng, -1):
                    best[d.eng] = d.pos
            if op.dma:
                if op.slotval > 16:
                    key = (X, op.slot)
                    if kdma[X].get(key, 0) < op.slotval - 16:
                        op.waits.append(("dma", X, op.slot, op.slotval - 16))
                        kdma[X][key] = op.slotval - 16
            for E, p in best.items():
                if p > known[X][E]:
                    known[X][E] = p
                    op.waits.append(("eng", E, p))
                    needmark.add(per[E][p].idx)
            for d in dmadeps:
                key = (d.eng, d.slot)
                if kdma[X].get(key, 0) < d.slotval:
                    kdma[X][key] = d.slotval
                    op.waits.append(("dma", d.eng, d.slot, d.slotval))
        self.cnt = {}
        for e in self.ENGS:
            c = 0
            for op in per[e]:
                if op.idx in needmark and not op.dma:
                    c += 1
                    op.inc = c
        self.per = per
        self.nd = nd
        print('marks', {e: max([o.inc or 0 for o in per[e]] + [0]) for e in self.ENGS}, 'nops', {e: len(per[e]) for e in self.ENGS}, 'ndma', nd)

    def emit(self):
        nc, es = self.nc, self.es
        self.finalize()
        per = self.per
        esem = {e: es.enter_context(nc.semaphore("s_" + e)) for e in self.ENGS}
        dsem = {}
        for e in self.ENGS:
            if self.nd[e]:
                dsem[e] = [es.enter_context(nc.semaphore("d_%s_%d" % (e, i))) for i in range(min(self.NSLOT, self.nd[e]))]
        ops = self.ops

        def run(ename, eng):
            for op in per[ename]:
                for w in op.waits:
                    if w[0] == "eng":
                        eng.wait_ge(esem[w[1]], per[w[1]][w[2]].inc)
                    else:
                        eng.wait_ge(dsem[w[1]][w[2]], w[3])
                ins = op.fn(eng)
                if op.dma:
                    ins.then_inc(dsem[ename][op.slot], 16)
                elif op.inc is not None:
                    ins.then_inc(esem[ename], 1)
            if self.nd[ename]:
                last = {}
                for op in per[ename]:
                    if op.dma:
                        last[op.slot] = op.slotval
                for s, v in last.items():
                    eng.wait_ge(dsem[ename][s], v)

        with nc.Block() as block:
            @block.tensor
            def _(e):
                run("pe", e)

            @block.scalar
            def _(e):
                run("act", e)

            @block.vector
            def _(e):
                run("dve", e)

            @block.gpsimd
            def _(e):
                run("pool", e)

            @block.sync
            def _(e):
                run("sp", e)


class WStream:
    def __init__(self, P, nslot, cols, scratch, per_block):
        self.P = P
        self.slots = [P.sb("wslot%d" % i, [128, cols], BF16) for i in range(nslot)]
        self.pieces = []
        self.issued = 0
        self.scratch = scratch
        self.per_block = per_block

    def plan(self, parts):
        self.pieces.append(parts)
        return len(self.pieces) - 1

    def _issue(self, k):
        slot = self.slots[k % len(self.slots)]
        pos = k % self.per_block
        first = k < self.per_block
        for (dstfn, src) in self.pieces[k]:
            load_piece(self.P, dstfn(slot), src, dstfn(self.scratch[pos]) if self.scratch is not None else None, first)

    def get(self, k, ahead, lo=None):
        lo = k if lo is None else lo
        lim = min(len(self.pieces), k + 1 + ahead, lo + len(self.slots))
        while self.issued < lim:
            self._issue(self.issued)
            self.issued += 1
        return self.slots[k % len(self.slots)]


def load_piece(P, dst, src, scr, first):
    if scr is None:
        P.dma("pool", dst, src)
    elif first:
        P.dma("pool", dst, src)
        P.dma("sp", scr, dst)
    else:
        P.dma("sp", dst, scr)


OFF_G, OFF_CW, OFF_ALOG, OFF_DTB, OFF_GDN, OFF_SINK, OFF_NF, NSMALL = 0, 32, 80, 84, 88, 600, 608, 1632
C_ID, C_U, C_SU, C_ONE, C_US, C_SUS, C_ONES, C_MC, C_COS, C_SIN, NCONST = 0, 128, 256, 384, 512, 640, 768, 896, 1024, 1160, 1296
NMASK = 8
PAST = 16384
THETA = 500000.0


def make_consts():
    c = np.zeros((128, NCONST), np.float32)
    i = np.arange(128)
    m, n = i[:, None], i[None, :]
    same = (m // 8) == (n // 8)
    c[:, C_ID:C_ID + 128] = (m == n)
    c[:, C_U:C_U + 128] = (m <= n)
    c[:, C_SU:C_SU + 128] = (m > n)
    c[:, C_ONE:C_ONE + 128] = 1.0
    c[:, C_US:C_US + 128] = (m <= n) & same
    c[:, C_SUS:C_SUS + 128] = (m > n) & same
    c[:, C_ONES:C_ONES + 128] = same
    c[:, C_MC:C_MC + 128] = (m > (n % 8))
    inv = (np.float32(THETA) ** (-np.arange(8, dtype=np.float32) * np.float32(2.0) / np.float32(16))).astype(np.float32)
    for ti in range(17):
        pos = (ti * 128 + i) if ti < 16 else (PAST + (i % 8))
        ang = pos.astype(np.float32)[:, None] * inv[None, :]
        c[:, C_COS + ti * 8:C_COS + ti * 8 + 8] = np.cos(ang.astype(np.float32)).astype(np.float32)
        c[:, C_SIN + ti * 8:C_SIN + ti * 8 + 8] = np.sin(ang.astype(np.float32)).astype(np.float32)
    return c


def make_smalls(norm_ffn1, norm_mix, norm_ffn2, norm_ple, conv_w, a_log, dt_bias, gdn_norm, sinks, norm_final):
    s = np.zeros((128, NSMALL), np.float32)
    for k, g in enumerate([norm_ffn1, norm_mix, norm_ffn2, norm_ple]):
        s[:, OFF_G + 8 * k:OFF_G + 8 * k + 8] = np.asarray(g, np.float32).reshape(8, 128).T
    cw = np.asarray(conv_w, np.float32).reshape(4, 12, 128)
    s[:, OFF_CW:OFF_CW + 48] = cw.transpose(2, 0, 1).reshape(128, 48)
    s[:, OFF_ALOG:OFF_ALOG + 4] = np.asarray(a_log, np.float32).reshape(1, 4)
    s[:, OFF_DTB:OFF_DTB + 4] = np.asarray(dt_bias, np.float32).reshape(1, 4)
    s[:, OFF_GDN:OFF_GDN + 512] = np.tile(np.asarray(gdn_norm, np.float32).reshape(1, 128), (1, 4))
    s[:, OFF_SINK:OFF_SINK + 8] = np.asarray(sinks, np.float32).reshape(1, 8)
    s[:, OFF_NF:OFF_NF + 1024] = np.asarray(norm_final, np.float32).reshape(1, 1024)
    return s


def build(nc, cfg):
    es = contextlib.ExitStack()
    P = Prog(nc, es)
    NSEQ = cfg.get("nseq", 2)
    SEQ = cfg.get("seq", 2048)
    SAMPLE = cfg.get("sample", True)
    NT = 4
    T = NT * 128
    AHEAD = cfg.get("ahead", 2)
    DK = 128

    def din(name, shape):
        return nc.dram_tensor(name, list(shape), F32, kind="ExternalInput").ap()

    def dout(name, shape):
        return nc.dram_tensor(name, list(shape), F32, kind="ExternalOutput").ap()

    x_p = din("x_p", [NSEQ * SEQ, D])
    p_p = din("p_p", [NSEQ * SEQ, 256])
    x_s = din("x_s", [128, D])
    p_s = din("p_s", [128, 256])
    st_gdn = din("st_gdn", [16, 4, 128, 128])
    st_conv = din("st_conv", [48, 1536])
    ck_d = din("ck", [16, 128, 128])
    cv_d = din("cv", [16, 128, 128])
    ffn1_gu = din("ffn1_gu", [D, 2 * DFF])
    ffn1_down = din("ffn1_down", [DFF, D])
    w_in = din("w_in", [D, INW])
    w_out = din("w_out", [D, D])
    ffn2_gu = din("ffn2_gu", [D, 2 * DFF])
    ffn2_down = din("ffn2_down", [DFF, D])
    ple_proj = din("ple_proj", [256, D])
    ple_gate = din("ple_gate", [D, D])
    smalls_d = din("smalls", [128, NSMALL])
    consts_d = din("consts", [128, NCONST])
    y_p = dout("y_p", [NSEQ * SEQ, D])
    y_s = dout("y_s", [128, D])
    sg_p = dout("sg_p", [NSEQ, 4, 128, 128])
    sc_p = dout("sc_p", [NSEQ * 3, 1536])
    kk_p = dout("kk_p", [NSEQ, 128, 128])
    vv_p = dout("vv_p", [NSEQ, 128, 128])
    sg_s = dout("sg_s", [16, 4, 128, 128])
    sc_s = dout("sc_s", [48, 1536])
    kk_s = dout("kk_s", [16, 128, 128])
    vv_s = dout("vv_s", [16, 128, 128])

    DBG = cfg.get("debug", False)
    if DBG:
        dbg_og = dout("dbg_og", [128, 4 * 512])
        dbg_os = dout("dbg_os", [64, 8 * 512])
        dbg_h = dout("dbg_h", [128, 4 * D])
        dbg_h1 = dout("dbg_h1", [128, 4 * D])
    cst = P.sb("cst", [128, NCONST], F32)
    sml = P.sb("sml", [128, NSMALL], F32)
    P.dma("sp", cst, consts_d)
    P.dma("sp", sml, smalls_d)
    mb = P.sb("mb", [128, NMASK * 128], BF16)
    P.copy("dve", mb, cst[:, 0:NMASK * 128])

    def c32(off):
        return cst[:, off:off + 128]

    def cb(off):
        return mb[:, off:off + 128]

    ident32, identb, onesb = c32(C_ID), cb(C_ID), cb(C_ONE)
    neghalf = P.sb("neghalf", [128, 16], F32)
    P.memset("dve", neghalf, -0.5)
    negA = P.sb("negA", [128, 4], F32)
    P.act(negA, sml[:, OFF_ALOG:OFF_ALOG + 4], AF.Exp)
    P.ts("dve", negA, negA, -1.0)
    esk = P.sb("esk", [128, 8], F32)
    P.act(esk, sml[:, OFF_SINK:OFF_SINK + 8], AF.Exp)
    gnb = sml[:, OFF_GDN:OFF_GDN + 512]
    nfb = sml[:, OFF_NF:OFF_NF + 1024]
    dtb = sml[:, OFF_DTB:OFF_DTB + 4]

    def gcol(k):
        return sml[:, OFF_G + 8 * k:OFF_G + 8 * k + 8]

    NPB = 11 + 6 + 3 + 11 + 3
    USE_SCR = cfg.get("scratch", True)
    if USE_SCR:
        scr_d = nc.dram_tensor("wscr", [NPB + 12, 128, 8 * 520], BF16, kind="Internal").ap()
        scr = [P.newbuf(scr_d[i]) for i in range(NPB + 12)]
    else:
        scr = None
    ws = WStream(P, cfg.get("wslots", 4), 8 * 520, scr[:NPB] if USE_SCR else None, NPB)
    ffn_ctr = [0]
    P.make_arena("arena", cfg.get("arena", 116 * 1024))
    P.arena_reset()
    wd = P.take(NFF * 1024, BF16)
    wd_parts = [P.alias(wd) for _ in range(6)]
    wd3 = [w.v(lambda a: a.rearrange("p (c n) -> p c n", c=NFF)) for w in wd_parts]

    banks = [P.ps("pb%d" % i, [128, 512], F32) for i in range(8)]
    bctr = [0]

    pinned = set()
    bpool = {"cur": list(range(8))}
    pctr = {}

    def set_pool(idx):
        bpool["cur"] = list(idx)

    def bank(pin=False):
        pool = bpool["cur"]
        key = tuple(pool)
        c = pctr.get(key, 0)
        while pool[c % len(pool)] in pinned:
            c += 1
        i = pool[c % len(pool)]
        pctr[key] = c + 1
        if pin:
            pinned.add(i)
        return banks[i]

    def unpin_all():
        pinned.clear()

    def bf(b):
        return b.v(lambda a: a.bitcast(BF16))

    hT = [P.sb("h%d" % n, [128, D], F32) for n in range(NT)]

    class _H3:
        def __getitem__(self, k):
            _, n, sl = k
            return hT[n][:, sl]
    h3 = _H3()
    xs = [P.sb("xs%d" % i, [128, D], BF16) for i in range(2)]
    junk = P.sb("junk", [128, D], BF16)
    xnT = P.sb("xnT", [128, 8 * T], BF16)
    xnT3 = xnT.v(lambda a: a.rearrange("p (k t) -> p k t", k=8))
    aT = P.take(NFF * T, BF16)
    aT3 = aT.v(lambda a: a.rearrange("p (c t) -> p c t", c=NFF))
    sg = [P.take(T, F32) for i in range(2)]
    ss = P.sb("ss", [128, NT], F32)
    rstd = P.sb("rstd", [128, NT], F32)
    cnt = {"xs": 0, "sg": 0, "yt": 0}

    def rmsnorm_fm(gc_, nt):
        t = nt * 128
        xsl = []
        for n in range(nt):
            P.act(junk, h3[:, n, :], AF.Square, accum=ss[:, n:n + 1])
            P.ts("dve", rstd[:, n:n + 1], ss[:, n:n + 1], 1.0 / D, EPS, ALU.mult, ALU.add)
            P.act(rstd[:, n:n + 1], rstd[:, n:n + 1], AF.Ln)
            P.act(rstd[:, n:n + 1], rstd[:, n:n + 1], AF.Exp, scale=-0.5)
            xb = xs[cnt["xs"] % 2]
            cnt["xs"] += 1
            P.ts("dve", xb, h3[:, n, :], rstd[:, n:n + 1])
            xsl.append(xb)
            if n % 2 == 1 or n == nt - 1:
                n0 = n - (1 if n % 2 == 1 else 0)
                for kc in range(8):
                    pt = bf(bank())
                    for m in range(n0, n + 1):
                        P.tr(pt[:, (m - n0) * 128:(m - n0 + 1) * 128], xsl[m][:, kc * 128:(kc + 1) * 128], identb)
                    w = (n + 1 - n0) * 128
                    P.ts("dve", xnT3[:, kc, n0 * 128:n0 * 128 + w], pt[:, :w], gc_[:, kc:kc + 1])

    def ffn_gu(gc_, plan_gu, down, nt, do_norm=True):
        t = nt * 128
        if do_norm:
            rmsnorm_fm(gc_, nt)
        dv = down.rearrange("(c p) n -> p c n", p=128)
        fidx = ffn_ctr[0]
        ffn_ctr[0] += 1
        for i in range(6):
            c0, c1 = i * 4, min(NFF, i * 4 + 4)
            sc_ = None
            if USE_SCR:
                sc_ = scr[NPB + (fidx % 2) * 6 + i].v(lambda a: a[:, :(c1 - c0) * 1024].rearrange("p (c n) -> p c n", c=c1 - c0))
            load_piece(P, wd3[i][:, c0:c1, :], dv[:, c0:c1, :], sc_, fidx < 2)
        for cg in range(NFF // 2):
            W = ws.get(plan_gu[cg], AHEAD)
            W3 = W.v(lambda a: a[:, :8 * 512].rearrange("p (k n) -> p k n", k=8))
            for c in range(2):
                g_ps, u_ps = bank(), bank()
                sgb = sg[cnt["sg"] % 2]
                cnt["sg"] += 1
                for kc in range(8):
                    P.mm(g_ps[:, :t], W3[:, kc, c * 128:(c + 1) * 128], xnT3[:, kc, :t], start=(kc == 0), stop=(kc == 7))
                for kc in range(8):
                    P.mm(u_ps[:, :t], W3[:, kc, 256 + c * 128:256 + (c + 1) * 128], xnT3[:, kc, :t], start=(kc == 0), stop=(kc == 7))
                P.act(sgb[:, :t], g_ps[:, :t], AF.Silu)
                P.tt("dve", aT3[:, cg * 2 + c, :t], u_ps[:, :t], sgb[:, :t], ALU.mult)

    def ffn_down(nt):
        for n in range(nt):
            for half in range(2):
                d_ps = bank()
                for c in range(NFF):
                    P.mm(d_ps, aT3[:, c, n * 128:(n + 1) * 128], wd3[c // 4][:, c, half * 512:(half + 1) * 512],
                         start=(c == 0), stop=(c == NFF - 1))
                hv = h3[:, n, half * 512:(half + 1) * 512]
                P.stt("dve", hv, d_ps, 0.5, hv, ALU.mult, ALU.add)

    def cap(fn, pool):
        P.capture()
        set_pool(pool)
        fn()
        L = P.end_capture()
        set_pool(range(8))
        return L

    def par2(fa, fb):
        if not cfg.get("par2", True):
            fa()
            fb()
            return
        LA = cap(fa, [0, 1, 2, 3])
        LB = cap(fb, [4, 5, 6, 7])
        P.merge_sched([LA, LB])

    def k8(sl, w):
        return sl.v(lambda a: a[:, :8 * w].rearrange("p (k n) -> p k n", k=8))

    def plan_gu(gu):
        guv = gu.rearrange("(k p) n -> p k n", p=128)
        ids = []
        for cg in range(NFF // 2):
            ids.append(ws.plan([(lambda sl: k8(sl, 512)[:, :, 0:256], guv[:, :, cg * 256:(cg + 1) * 256]),
                                (lambda sl: k8(sl, 512)[:, :, 256:512], guv[:, :, DFF + cg * 256:DFF + (cg + 1) * 256])]))
        return ids

    WIN_COLS = [(0, 512), (512, 1024), (1024, 1536), (1536, 2056), (2056, 2568), (2568, 2824)]

    def plan_win():
        wv = w_in.rearrange("(k p) n -> p k n", p=128)
        ids = []
        for (a, b) in WIN_COLS:
            ids.append(ws.plan([((lambda w: (lambda sl: k8(sl, w)))(b - a), wv[:, :, a:b])]))
        return ids

    def plan_wout():
        ids = [ws.plan([(lambda sl: sl.v(lambda a: a[:, :4096].rearrange("p (c n) -> p c n", c=4)),
                         w_out[0:512, :].rearrange("(c p) n -> p c n", p=128))])]
        for i in range(2):
            ids.append(ws.plan([(lambda sl: sl.v(lambda a: a[0:64, :4096].rearrange("p (c n) -> p c n", c=4)),
                                 w_out[512 + i * 256:512 + (i + 1) * 256, :].rearrange("(c p) n -> p c n", p=64))]))
        return ids

    def plan_ple():
        gv = ple_gate.rearrange("(k p) n -> p k n", p=128)
        ids = [ws.plan([(lambda sl: k8(sl, 512), gv[:, :, hf * 512:(hf + 1) * 512])]) for hf in range(2)]
        ids.append(ws.plan([(lambda sl: sl.v(lambda a: a[:, :2048].rearrange("p (c n) -> p c n", c=2)),
                             ple_proj.rearrange("(c p) n -> p c n", p=128))]))
        return ids

    blocks = []
    for s in range(NSEQ):
        nb = SEQ // T
        for b in range(nb):
            blocks.append(dict(kind="p", seq=s, bi=b, t0=s * SEQ + b * T, nt=NT, first=(b == 0), last=(b == nb - 1)))
    if SAMPLE:
        blocks.append(dict(kind="s", seq=0, bi=0, t0=0, nt=1, first=True, last=True))
    plans = []
    for blk in blocks:
        plans.append(dict(f1=plan_gu(ffn1_gu), win=plan_win(), wout=plan_wout(), f2=plan_gu(ffn2_gu), ple=plan_ple()))

    P.arena_reset()
    EXTW = 3 + T
    ext = P.take(4 * EXTW, F32)
    ext3 = ext.v(lambda a: a.rearrange("p (c w) -> p c w", c=4))
    carry = P.sb("carry", [128, 12 * 48], F32)
    carry3 = carry.v(lambda a: a[:, :36].rearrange("p (c j) -> p c j", c=12))
    carry_s = carry.v(lambda a: a.rearrange("p (c b j) -> p c b j", c=12, b=16))
    acc = [P.take(T, F32) for i in range(2)]
    qkvT = P.take(12 * T, BF16)
    qkvT3 = qkvT.v(lambda a: a.rearrange("p (c t) -> p c t", c=12))
    gz = P.take(NT * 512, BF16)
    gz3 = gz.v(lambda a: a.rearrange("p (n d) -> p n d", n=NT))
    szt = [P.take(512, F32) for i in range(2)]
    ab = P.take(NT * 8, F32)
    ab3 = ab.v(lambda a: a.rearrange("p (n c) -> p n c", n=NT))
    gt = P.take(6 * NT * 4, F32)
    gt4 = gt.v(lambda a: a.rearrange("p (k n c) -> p k n c", k=6, n=NT))
    ogT = P.take(4 * T, BF16)
    ogT3 = ogT.v(lambda a: a.rearrange("p (c t) -> p c t", c=4))
    oTs = P.take(8 * T, BF16, parts=64)
    oTs3 = oTs.v(lambda a: a.rearrange("p (c t) -> p c t", c=8))
    S32 = P.sb("S32", [128, 512], F32)
    Sbf = P.sb("Sbf", [128, 512], BF16)
    S32_3 = S32.v(lambda a: a.rearrange("p (c d) -> p c d", c=4))
    Sbf3 = Sbf.v(lambda a: a.rearrange("p (c d) -> p c d", c=4))

    def t4(name, dt, n=1):
        l = []
        for i in range(n):
            b = P.take(512, dt)
            l.append(b.v(lambda a: a.rearrange("p (c d) -> p c d", c=4)))
        return l

    sqb = P.take(1024, BF16)
    rn = P.take(1024, F32)
    kTn, qTn = t4("kTn", BF16)[0], t4("qTn", BF16)[0]
    qdT2 = t4("qdT", BF16, 2)
    dgb = t4("dgb", BF16)[0]
    Ug = t4("Ug", F32)[0]
    Fb, Fs, Fm, Fsb = t4("Fb", BF16)[0], t4("Fs", BF16)[0], t4("Fm", BF16)[0], t4("Fsb", BF16)[0]
    QKm = t4("QKm", BF16)[0]
    Am2, nAm2, B02, qkT2 = t4("Am", BF16, 2), t4("nAm", BF16, 2), t4("B0", BF16, 2), t4("qkT", BF16, 2)
    Pm, PT, Rbf = t4("Pm", BF16, 2), t4("PT", BF16, 2), t4("Rbf", BF16, 2)
    R32 = t4("R32", F32)[0]
    kbg2, kdec2, vb2 = t4("kbg", BF16, 2), t4("kdec", BF16, 2), t4("vb", BF16, 2)
    negwT, vnew = t4("negwT", BF16)[0], t4("vnew", BF16)[0]
    osq, otmp = t4("osq", F32)[0], t4("otmp", F32)[0]
    og = t4("og", BF16)[0]
    stmp = t4("stmp", F32)[0]
    gsc1 = [P.take(64, F32) for i in range(2)]
    gsc2 = P.take(64, F32)
    qkv_tm = P.take(768, F32)
    rtmp = P.take(6 * 80, F32)
    qkb = P.take(640, BF16)
    Vaug = [P.sb("Vaug%d" % i, [128, 130], BF16) for i in range(2)]
    kTs = [P.sb("kTs%d" % i, [64, 256], BF16) for i in range(2)]
    qTs = P.take(1024, BF16, parts=64)
    PTs = [P.take(512, BF16) for i in range(2)]
    den = P.take(512, F32)
    bcs = P.take(512, F32, parts=64)
    sct = P.take(1536, F32, parts=48)
    print("arena mixing bytes", P.aoff)
    for v_ in Vaug:
        P.memset("pool", v_, 1.0)
    if SAMPLE:
        kcT = P.sb("kcT", [64, 16 * 128], BF16)
        Vc = P.sb("Vc", [128, 8 * 130], BF16)
        PTc = P.sb("PTc", [128, 256], BF16)
        P.memset("pool", Vc, 1.0)
    P.arena_reset()
    pst = P.take(NT * 256, F32)
    pstb = P.take(NT * 256, BF16)
    pT = P.take(2 * T, BF16)
    sig = [P.take(512, F32) for i in range(2)]
    ytile = [P.take(D, F32) for i in range(2)]

    def flat(x):
        return x.v(lambda a: a.rearrange("p c d -> p (c d)"))

    def bc_h(v4):
        return v4.v(lambda a: a.unsqueeze(2).to_broadcast([128, 4, 128]))

    def bc_m(m):
        return m.v(lambda a: a.unsqueeze(1).to_broadcast([128, 4, 128]))

    def mixing(blk, plan):
        nt = blk["nt"]
        t = nt * 128
        samp = blk["kind"] == "s"
        if samp:
            e4 = ext3.v(lambda a: a[:, :, :176].rearrange("p c (b j) -> p c b j", j=11))
            P.dma("sp", sct[0:48, :], st_conv)
            for g3 in range(3):
                ps = bank()
                for c in range(4):
                    cc = g3 * 4 + c
                    P.tr(ps[:, c * 48:(c + 1) * 48], sct[0:48, cc * 128:(cc + 1) * 128], ident32[0:48, 0:48])
                P.copy("dve", carry[:, g3 * 192:(g3 + 1) * 192], ps[:, 0:192])
        for grp in range(3):
            W = ws.get(plan["win"][grp], AHEAD)
            W3 = k8(W, 512)
            if samp:
                P.copy("pool", e4[:, :, :, 0:3], carry_s[:, grp * 4:(grp + 1) * 4, :, :])
            elif blk["first"]:
                P.memset("pool", ext3[:, :, 0:3], 0.0)
            else:
                P.copy("pool", ext3[:, :, 0:3], carry3[:, grp * 4:(grp + 1) * 4, :])
            for c in range(4):
                ps = bank()
                for kc in range(8):
                    P.mm(ps[:, :t], W3[:, kc, c * 128:(c + 1) * 128], xnT3[:, kc, :t], start=(kc == 0), stop=(kc == 7))
                if samp:
                    P.copy("act", e4[:, c, :, 3:11], ps[:, :128].v(lambda a: a.rearrange("p (b j) -> p b j", j=8)))
                else:
                    P.copy("act", ext3[:, c, 3:3 + t], ps[:, :t])
            for c in range(4):
                cc = grp * 4 + c
                a_ = acc[c % 2]
                eng = "dve"
                if samp:
                    av = a_[:, :128].v(lambda a: a.rearrange("p (b j) -> p b j", j=8))
                    src = lambda j: e4[:, c, :, j:j + 8]
                else:
                    av = a_[:, :t]
                    src = lambda j: ext3[:, c, j:j + t]
                P.ts(eng, av, src(3), sml[:, OFF_CW + 3 * 12 + cc:OFF_CW + 3 * 12 + cc + 1])
                for j in (2, 1, 0):
                    P.stt(eng, av, src(j), sml[:, OFF_CW + j * 12 + cc:OFF_CW + j * 12 + cc + 1], av, ALU.mult, ALU.add)
                P.act(qkvT3[:, cc, :t], a_[:, :t], AF.Silu)
            if samp:
                P.copy("pool", carry_s[:, grp * 4:(grp + 1) * 4, :, :], e4[:, :, :, 8:11])
            else:
                P.copy("pool", carry3[:, grp * 4:(grp + 1) * 4, :], ext3[:, :, t:t + 3])
        if blk["last"]:
            ncol = 48 if samp else 3
            for g3 in range(3):
                ps = bank()
                for c in range(4):
                    cc = g3 * 4 + c
                    src = carry[:, cc * 48:(cc + 1) * 48] if samp else carry3[:, cc, :]
                    P.tr(ps[0:ncol, c * 128:(c + 1) * 128], src, ident32)
                P.copy("dve", sct[0:ncol, g3 * 512:(g3 + 1) * 512], ps[0:ncol, :])
            if samp:
                P.dma("sp", sc_s, sct[0:48, :])
            else:
                P.dma("sp", sc_p[blk["seq"] * 3:blk["seq"] * 3 + 3, :], sct[0:3, :])
        W = ws.get(plan["win"][3], AHEAD)
        W3 = k8(W, 520)
        for n in range(nt):
            ps, ps2 = bank(), bank()
            for kc in range(8):
                P.mm(ps, xnT3[:, kc, n * 128:(n + 1) * 128], W3[:, kc, 0:512], start=(kc == 0), stop=(kc == 7))
            for kc in range(8):
                P.mm(ps2[:, 0:8], xnT3[:, kc, n * 128:(n + 1) * 128], W3[:, kc, 512:520], start=(kc == 0), stop=(kc == 7))
            sz = szt[n % 2]
            P.act(sz, ps, AF.Silu)
            P.tt("pool", gz3[:, n, :], sz, gnb, ALU.mult)
            P.copy("dve", ab3[:, n, :], ps2[:, 0:8])
        a_v, b_v = ab3[:, :nt, 0:4], ab3[:, :nt, 4:8]
        G = lambda k: gt4[:, k, :nt, :]
        P.tt("dve", G(0), a_v, dtb.v(lambda a: a.unsqueeze(1).to_broadcast([128, nt, 4])), ALU.add)
        P.act(G(1), G(0), AF.Exp)
        P.act(G(1), G(1), AF.Ln, bias=1.0)
        P.tt("dve", G(2), G(1), negA.v(lambda a: a.unsqueeze(1).to_broadcast([128, nt, 4])), ALU.mult)
        P.act(G(3), b_v, AF.Exp, scale=-1.0)
        P.ts("dve", G(3), G(3), 1.0, None, ALU.add)
        P.recip(G(4), G(3))
        MP = cfg.get("mparts", "sdo")
        W4 = k8(ws.get(plan["win"][4], AHEAD), 512)
        W5 = k8(ws.get(plan["win"][5], AHEAD, lo=plan["win"][4]), 256)
        if samp and cfg.get("merge", True):
            def gd_():
                gdn_p1(blk, 0)
                gdn_p2(blk, 0)
            par2(lambda: swa_tile(blk, 0, W4, W5), gd_)
        elif not cfg.get("merge", True):
            for n in range(nt):
                swa_tile(blk, n, W4, W5)
                gdn_p1(blk, n)
                gdn_p2(blk, n)
        else:
            POOL_S, POOL_1, POOL_2 = cfg.get("pools", ([0, 1, 2], [3, 4], [5, 6, 7]))

            LA = cap(lambda: (None if cfg.get('noswa') else swa_tile(blk, 0, W4, W5)), POOL_S)
            LB = cap(lambda: (None if cfg.get('nogdn') else gdn_p1(blk, 0)), POOL_1)
            (P.merge_sched if cfg.get('sched', True) else P.merge_n)([LA, LB])
            for n in range(nt):
                Ls = [cap(lambda: (None if cfg.get('nogdn') else gdn_p2(blk, n)), POOL_2)]
                if n + 1 < nt:
                    Ls.append(cap(lambda: (None if cfg.get('nogdn') else gdn_p1(blk, n + 1)), POOL_1))
                    Ls.append(cap(lambda: (None if cfg.get('noswa') else swa_tile(blk, n + 1, W4, W5)), POOL_S))
                (P.merge_sched if cfg.get('sched', True) else P.merge_n)(Ls)

    def mix_wout(blk, plan):
        nt = blk["nt"]
        if DBG:
            P.dma("pool", dbg_og, ogT)
            P.dma("pool", dbg_os, oTs)
            for n_ in range(NT):
                P.dma("sp", dbg_h1[:, n_ * D:(n_ + 1) * D], hT[n_])
        Wo = [ws.get(plan["wout"][i], AHEAD, lo=plan["wout"][0]) for i in range(3)]
        Wo1 = Wo[0].v(lambda a: a[:, :4096].rearrange("p (c n) -> p c n", c=4))
        Wo2 = [w.v(lambda a: a[0:64, :4096].rearrange("p (c n) -> p c n", c=4)) for w in Wo[1:]]
        for n in range(nt):
            for half in range(2):
                ps = bank()
                for hh in range(4):
                    P.mm(ps, ogT3[:, hh, n * 128:(n + 1) * 128], Wo1[:, hh, half * 512:(half + 1) * 512], start=(hh == 0), stop=False)
                for hh in range(8):
                    P.mm(ps, oTs3[:, hh, n * 128:(n + 1) * 128], Wo2[hh // 4][:, hh % 4, half * 512:(half + 1) * 512],
                         start=False, stop=(hh == 7))
                hv = h3[:, n, half * 512:(half + 1) * 512]
                P.tt("dve", hv, ps, hv, ALU.add)

    swa_state = {"par": 0}

    def swa_tile(blk, n, W4, W5):
        samp = blk["kind"] == "s"
        ti = 16 if samp else blk["bi"] * NT + n
        has_prev = samp or not (blk["first"] and n == 0)
        cur = swa_state["par"] % 2
        prev = 1 - cur
        swa_state["par"] += 1
        ps_q, ps_kv = bank(), bank()
        for kc in range(8):
            P.mm(ps_q, xnT3[:, kc, n * 128:(n + 1) * 128], W4[:, kc, 0:512], start=(kc == 0), stop=(kc == 7))
        for kc in range(8):
            P.mm(ps_kv[:, 0:256], xnT3[:, kc, n * 128:(n + 1) * 128], W5[:, kc, 0:256], start=(kc == 0), stop=(kc == 7))
        P.copy("act", qkv_tm[:, 0:512], ps_q)
        P.copy("dve", qkv_tm[:, 512:768], ps_kv[:, 0:256])
        X = qkv_tm.v(lambda a: a[:, 0:640].rearrange("p (h d) -> p h d", h=10))
        x1, x2 = X[:, :, 0:8], X[:, :, 8:16]
        cosb = cst[:, C_COS + ti * 8:C_COS + ti * 8 + 8].v(lambda a: a.unsqueeze(1).to_broadcast([128, 10, 8]))
        sinb = cst[:, C_SIN + ti * 8:C_SIN + ti * 8 + 8].v(lambda a: a.unsqueeze(1).to_broadcast([128, 10, 8]))
        R = lambda k: rtmp[:, k * 80:(k + 1) * 80].v(lambda a: a.rearrange("p (h d) -> p h d", h=10))
        P.tt("pool", R(0), x1, cosb, ALU.mult)
        P.tt("pool", R(1), x2, sinb, ALU.mult)
        P.tt("pool", R(2), x2, cosb, ALU.mult)
        P.tt("pool", R(3), x1, sinb, ALU.mult)
        P.tt("pool", x1, R(0), R(1), ALU.subtract)
        P.tt("pool", x2, R(2), R(3), ALU.add)
        P.copy("dve", qkb, qkv_tm[:, 0:640])
        Va = Vaug[cur].v(lambda a: a.rearrange("p (k d) -> p k d", k=2))
        P.copy("pool", Va[:, :, 0:64], qkv_tm[:, 640:768].v(lambda a: a.rearrange("p (k d) -> p k d", k=2)))
        if blk["last"] and not samp and n == blk["nt"] - 1:
            P.dma("sp", kk_p[blk["seq"]], qkv_tm[:, 512:640])
            P.dma("sp", vv_p[blk["seq"]], qkv_tm[:, 640:768])
        if samp:
            for b in range(16):
                P.dma("sp", kk_s[b, 120:128, :], qkv_tm[b * 8:(b + 1) * 8, 512:640])
                P.dma("sp", vv_s[b, 120:128, :], qkv_tm[b * 8:(b + 1) * 8, 640:768])
        pq = bf(bank())
        for hh in range(8):
            P.tr(pq[0:64, hh * 128:(hh + 1) * 128], qkb[:, hh * 64:(hh + 1) * 64], identb)
        pk = bf(bank())
        for hh in range(2):
            P.tr(pk[0:64, hh * 128:(hh + 1) * 128], qkb[:, 512 + hh * 64:512 + (hh + 1) * 64], identb)
        P.copy("act", qTs, pq[0:64, :])
        P.copy("dve", kTs[cur], pk[0:64, 0:256])
        if samp:
            swa_sample(n, cur)
            return
        for kvh in range(2):
            rhs_q = qTs[:, kvh * 512:(kvh + 1) * 512]
            sc = bank()
            P.mm(sc, kTs[cur][:, kvh * 128:(kvh + 1) * 128], rhs_q)
            P.act(PTs[0], sc, AF.Exp, scale=0.125)
            mcur = cb(C_US) if samp else cb(C_U)
            P.tt("pool", PTs[0].v(lambda a: a.rearrange("p (g q) -> p g q", g=4)),
                 PTs[0].v(lambda a: a.rearrange("p (g q) -> p g q", g=4)), bc_m(mcur), ALU.mult)
            o_ps = bank()
            if samp:
                swa_sample_cache(kvh, rhs_q, o_ps, cur)
            else:
                if has_prev:
                    sp_ = bank()
                    P.mm(sp_, kTs[prev][:, kvh * 128:(kvh + 1) * 128], rhs_q)
                    P.act(PTs[1], sp_, AF.Exp, scale=0.125)
                    P.tt("pool", PTs[1].v(lambda a: a.rearrange("p (g q) -> p g q", g=4)),
                         PTs[1].v(lambda a: a.rearrange("p (g q) -> p g q", g=4)), bc_m(cb(C_SU)), ALU.mult)
                P.mm(o_ps[0:65, :], Vaug[cur][:, kvh * 65:(kvh + 1) * 65], PTs[0], start=True, stop=not has_prev)
                if has_prev:
                    P.mm(o_ps[0:65, :], Vaug[prev][:, kvh * 65:(kvh + 1) * 65], PTs[1], start=False, stop=True)
            dv_ = den[64:65, :].v(lambda a: a.rearrange("p (g q) -> p g q", g=4))
            P.tt("dve", dv_, o_ps[64:65, :].v(lambda a: a.rearrange("p (g q) -> p g q", g=4)),
                 esk[64:65, kvh * 4:(kvh + 1) * 4].v(lambda a: a.unsqueeze(2).to_broadcast([1, 4, 128])), ALU.add)
            P.act(den[64:65, :], den[64:65, :], AF.Ln)
            P.act(den[64:65, :], den[64:65, :], AF.Exp, scale=-1.0)
            bcp = bank()
            P.mm(bcp[0:64, :], cst[64:65, C_ONE:C_ONE + 64], den[64:65, :])
            P.copy("act", bcs, bcp[0:64, :])
            P.tt("dve", oTs3[:, kvh * 4:(kvh + 1) * 4, n * 128:(n + 1) * 128],
                 o_ps[0:64, :].v(lambda a: a.rearrange("p (g q) -> p g q", g=4)),
                 bcs.v(lambda a: a.rearrange("p (g q) -> p g q", g=4)), ALU.mult)

    def g4(x, w=128):
        return x.v(lambda a: a.rearrange("p (g q) -> p g q", g=4))

    def swa_sample(n, cur):
        bgt = lambda x: x.v(lambda a: a.rearrange("p (b g t) -> p b g t", g=4, t=8))
        P.dma("sp", kk_s[:, 0:120, :], ck_d[:, 8:128, :])
        P.dma("sp", vv_s[:, 0:120, :], cv_d[:, 8:128, :])
        qre = qkv_tm.v(lambda a: a[0:64, 0:512].bitcast(BF16))
        for kvh in range(2):
            P.copy("pool", bgt(qre[:, kvh * 512:(kvh + 1) * 512]),
                   qTs[:, kvh * 512:(kvh + 1) * 512].v(lambda a: a.rearrange("p (g b t) -> p b g t", g=4, t=8)))
        o_ps = [bank(pin=True), bank(pin=True)]
        us4 = cb(C_US).v(lambda a: a.rearrange("p (b t) -> p b t", t=8).unsqueeze(2).to_broadcast([128, 16, 4, 8]))
        for kvh in range(2):
            rhs_q = qre[:, kvh * 512:(kvh + 1) * 512]
            sc = bank()
            P.mm(sc, kTs[cur][:, kvh * 128:(kvh + 1) * 128], rhs_q)
            P.act(PTs[kvh], sc, AF.Exp, scale=0.125)
            P.tt("pool", bgt(PTs[kvh]), bgt(PTs[kvh]), us4, ALU.mult)
            P.mm(o_ps[kvh][0:65, :], Vaug[cur][:, kvh * 65:(kvh + 1) * 65], PTs[kvh], start=True, stop=False)
        for half in range(2):
            b0 = half * 8
            ckv = xs[0].v(lambda a: a.rearrange("p (b d) -> p b d", b=8))
            P.dma("pool", ckv, ck_d[b0:b0 + 8].rearrange("b j d -> j b d"))
            for kvh in range(2):
                P.dma("pool", Vc.v(lambda a: a.rearrange("p (b d) -> p b d", b=8)[:, :, kvh * 65:kvh * 65 + 64]),
                      cv_d[b0:b0 + 8].rearrange("b j d -> j b d")[:, :, kvh * 64:(kvh + 1) * 64])
            for hb in range(2):
                pk = bf(bank())
                for i in range(8):
                    bl, kvh = (hb * 8 + i) // 2, (hb * 8 + i) % 2
                    P.tr(pk[0:64, i * 128:(i + 1) * 128], ckv[:, bl, kvh * 64:(kvh + 1) * 64], identb)
                P.copy("act" if hb == 0 else "dve", kcT[:, hb * 1024:(hb + 1) * 1024], pk[0:64, :])
            for kvh in range(2):
                scc = bank()
                for bl in range(8):
                    bb = b0 + bl
                    P.mm(scc[:, bl * 32:(bl + 1) * 32], kcT[:, (bl * 2 + kvh) * 128:(bl * 2 + kvh + 1) * 128],
                         qre[:, kvh * 512 + bb * 32:kvh * 512 + (bb + 1) * 32])
                P.act(PTc, scc[:, 0:256], AF.Exp, scale=0.125)
                pc3 = PTc.v(lambda a: a.rearrange("p (c t) -> p c t", t=8))
                P.tt("pool", pc3, pc3, cb(C_MC)[:, 0:8].v(lambda a: a.unsqueeze(1).to_broadcast([128, 32, 8])), ALU.mult)
                for bl in range(8):
                    bb = b0 + bl
                    P.mm(o_ps[kvh][0:65, bb * 32:(bb + 1) * 32], Vc[:, bl * 130 + kvh * 65:bl * 130 + (kvh + 1) * 65],
                         PTc[:, bl * 32:(bl + 1) * 32], start=False, stop=(half == 1 and bl == 7))
        for kvh in range(2):
            op_ = o_ps[kvh]
            P.tt("dve", bgt(den[64:65, :]), bgt(op_[64:65, :]),
                 esk[64:65, kvh * 4:(kvh + 1) * 4].v(lambda a: a.unsqueeze(1).unsqueeze(3).to_broadcast([1, 16, 4, 8])), ALU.add)
            P.act(den[64:65, :], den[64:65, :], AF.Ln)
            P.act(den[64:65, :], den[64:65, :], AF.Exp, scale=-1.0)
            bcp = bank()
            P.mm(bcp[0:64, :], cst[64:65, C_ONE:C_ONE + 64], den[64:65, :])
            P.copy("act", bcs, bcp[0:64, :])
            P.tt("dve", oTs3[:, kvh * 4:(kvh + 1) * 4, n * 128:(n + 1) * 128].v(lambda a: a.rearrange("p g (b t) -> p b g t", t=8)),
                 bgt(op_[0:64, :]), bgt(bcs), ALU.mult)
        unpin_all()

    pp = {"k": 0}

    hand = {}

    def r4(bk):
        return bk.v(lambda a: a.rearrange("p (c d) -> p c d", c=4))

    def trs(src3):
        pb_ = bf(bank())
        p3 = pb_.v(lambda a: a[:, 0:512].rearrange("p (c d) -> p c d", c=4))
        for hh in range(4):
            P.tr(p3[:, hh, :], src3[:, hh, :], identb)
        return p3

    def gdn_p1(blk, n):
        samp = blk["kind"] == "s"
        par = pp["k"] % 2
        pp["k"] += 1
        H = dict(Am=Am2[par], nAm=nAm2[par], B0=B02[par], qkT=qkT2[par], kbg=kbg2[par], kdec=kdec2[par],
                 vb=vb2[par], qdT=qdT2[par], gsc=gsc1[par])
        hand[n] = H
        gsc_ = H["gsc"]
        cs = slice(n * 128, (n + 1) * 128)
        qTr, kTr, vTr = qkvT3[:, 0:4, cs], qkvT3[:, 4:8, cs], qkvT3[:, 8:12, cs]
        U32 = c32(C_US) if samp else c32(C_U)
        SU32 = c32(C_SUS) if samp else c32(C_SU)
        ON32 = c32(C_ONES) if samp else c32(C_ONE)
        STb = cb(C_SUS) if samp else cb(C_SU)
        g_n, beta_n = gt4[:, 2, n, :], gt4[:, 4, n, :]
        sq3 = sqb.v(lambda a: a.rearrange("p (c d) -> p c d", c=8))
        P.tt("pool", sq3, qkvT3[:, 0:8, cs], qkvT3[:, 0:8, cs], ALU.mult)
        ssq, ssk = bank(), bank()
        P.mm(ssq, onesb, sqb[:, 0:512])
        P.mm(ssk, onesb, sqb[:, 512:1024])
        P.ts("dve", rn[:, 0:512], ssq, EPS, None, ALU.add)
        P.ts("dve", rn[:, 512:1024], ssk, EPS, None, ALU.add)
        P.act(rn, rn, AF.Ln)
        P.act(rn, rn, AF.Exp, scale=-0.5)
        rn3 = rn.v(lambda a: a.rearrange("p (c d) -> p c d", c=8))
        P.stt("dve", qTn, qTr, float(DK) ** -0.5, rn3[:, 0:4, :], ALU.mult, ALU.mult)
        P.tt("dve", kTn, kTr, rn3[:, 4:8, :], ALU.mult)
        gs = bank()
        P.mm(gs[:, 0:4], U32, g_n)
        P.mm(gs[:, 4:8], ON32, g_n)
        P.copy("dve", gsc_[:, 0:8], gs[:, 0:8])
        P.tt("dve", gsc_[:, 8:12], gsc_[:, 4:8], gsc_[:, 0:4], ALU.subtract)
        P.act(gsc_[:, 16:28], gsc_[:, 0:12], AF.Exp)
        e_gc, e_last, e_rem = gsc_[:, 16:20], gsc_[:, 20:24], gsc_[:, 24:28]
        H["e_last"] = e_last
        P.tt("dve", gsc_[:, 32:36], beta_n, e_gc, ALU.mult)
        c_kbg = gsc_[:, 32:36]
        P.tt("pool", dgb, bc_m(identb), bc_h(e_gc), ALU.mult)
        rb = bank()
        P.mm(rb, onesb, flat(dgb))
        P.tt("dve", H["qdT"], qTn, r4(rb), ALU.mult)
        P.tt("pool", Ug, bc_m(U32), bc_h(g_n), ALU.mult)
        dm = bank()
        dm3 = r4(dm)
        for hh in range(4):
            P.mm(dm3[:, hh, :], Ug[:, hh, :], SU32)
        P.act(Fb, dm3, AF.Exp)
        P.tt("pool", Fs, Fb, bc_m(STb), ALU.mult)
        P.tt("pool", Fm, Fs, bc_m(identb), ALU.add)
        P.tt("pool", Fsb, Fs, bc_h(beta_n), ALU.mult)
        kk, qk = bank(), bank()
        kk3, qk3 = r4(kk), r4(qk)
        for hh in range(4):
            P.mm(kk3[:, hh, :], kTn[:, hh, :], kTn[:, hh, :])
        for hh in range(4):
            P.mm(qk3[:, hh, :], qTn[:, hh, :], kTn[:, hh, :])
        P.tt("dve", H["Am"], kk3, Fsb, ALU.mult)
        P.tt("dve", QKm, qk3, Fm, ALU.mult)
        P.ts("pool", H["nAm"], H["Am"], -1.0, 0.0, ALU.mult, ALU.add)
        b_ps = trs(H["Am"])
        P.copy("act", flat(H["B0"]), flat(b_ps))
        qkT_ps = trs(QKm)
        P.copy("act", flat(H["qkT"]), flat(qkT_ps))
        ktm = trs(kTn)
        P.tt("dve", H["kbg"], ktm, bc_h(c_kbg), ALU.mult)
        P.tt("dve", H["kdec"], ktm, bc_h(e_rem), ALU.mult)
        vtm = trs(vTr)
        P.tt("dve", H["vb"], vtm, bc_h(beta_n), ALU.mult)

    def gdn_p2(blk, n):
        samp = blk["kind"] == "s"
        H = hand.pop(n)
        R_ps = bank(pin=True)
        R3 = r4(R_ps)
        for hh in range(4):
            P.mm(R3[:, hh, :], H["nAm"][:, hh, :], identb, start=(hh == 0), stop=False, skip_group_check=True)
            P.mm(R3[:, hh, :], identb, identb, start=False, stop=False, skip_group_check=True)
        P.copy("dve", flat(Rbf[0]), R_ps)
        cur_P, cur_PT, cur_R = H["B0"], H["Am"], Rbf[0]
        for k in range(1, 7):
            nP, nPT, nR = Pm[k % 2], PT[k % 2], Rbf[k % 2]
            if k < 6:
                p_ps = bank()
                p3 = r4(p_ps)
                for hh in range(4):
                    P.mm(p3[:, hh, :], cur_PT[:, hh, :], cur_P[:, hh, :])
            pt_ps = bank()
            pt3 = r4(pt_ps)
            for hh in range(4):
                P.mm(pt3[:, hh, :], cur_P[:, hh, :], cur_PT[:, hh, :])
            P.copy("dve", nPT, pt3)
            if k < 6:
                P.copy("act", flat(nP), flat(p3))
            for hh in range(4):
                P.mm(R3[:, hh, :], nPT[:, hh, :], cur_R[:, hh, :], start=False, stop=(k == 6), skip_group_check=True)
            P.copy("act", flat(nR), R_ps)
            cur_P, cur_PT, cur_R = nP, nPT, nR
        Rf = cur_R
        for i_, bk_ in enumerate(banks):
            if bk_.uid == R_ps.uid:
                pinned.discard(i_)
        w_ps = bank()
        w3 = r4(w_ps)
        for hh in range(4):
            P.mm(w3[:, hh, :], H["kbg"][:, hh, :], Rf[:, hh, :])
        P.ts("dve", negwT, w3, -1.0)
        if samp:
            gdn_sample_state(n, Rf, H)
            return
        if blk["first"] and n == 0:
            P.memset("pool", S32, 0.0)
            P.memset("pool", Sbf, 0.0)
        v_ps = bank()
        v3 = r4(v_ps)
        for hh in range(4):
            P.mm(v3[:, hh, :], Rf[:, hh, :], H["vb"][:, hh, :], start=True, stop=False)
            P.mm(v3[:, hh, :], negwT[:, hh, :], Sbf3[:, hh, :], start=False, stop=True)
        P.copy("act", flat(vnew), flat(v3))
        o_ps = bank()
        o3 = r4(o_ps)
        for hh in range(4):
            P.mm(o3[:, hh, :], H["qkT"][:, hh, :], vnew[:, hh, :], start=True, stop=False)
            P.mm(o3[:, hh, :], H["qdT"][:, hh, :], Sbf3[:, hh, :], start=False, stop=True)
        s_ps = bank()
        s3 = r4(s_ps)
        for hh in range(4):
            P.mm(s3[:, hh, :], H["kdec"][:, hh, :], vnew[:, hh, :])
        P.tt("pool", stmp, S32_3, bc_h(H["e_last"]), ALU.mult)
        P.tt("dve", S32_3, stmp, s3, ALU.add)
        P.copy("act", Sbf, S32)
        if blk["last"] and n == blk["nt"] - 1:
            P.dma("sp", sg_p[blk["seq"]].rearrange("h k v -> k h v"), S32_3)
        gdn_out(n, o3)

    def gdn_out(n, o3):
        gsc = gsc2
        P.act(osq, o3, AF.Square)
        P.add("dve", lambda e: e.tensor_reduce(gsc[:, 40:44].ap, osq.ap, AX.X, ALU.add), [osq], [gsc])
        P.ts("dve", gsc[:, 44:48], gsc[:, 40:44], 1.0 / 128, EPS, ALU.mult, ALU.add)
        P.act(gsc[:, 44:48], gsc[:, 44:48], AF.Ln)
        P.act(gsc[:, 44:48], gsc[:, 44:48], AF.Exp, scale=-0.5)
        P.tt("dve", otmp, o3, bc_h(gsc[:, 44:48]), ALU.mult)
        P.tt("pool", og, otmp, gz3[:, n, :].v(lambda a: a.rearrange("p (c d) -> p c d", c=4)), ALU.mult)
        pb_ = bf(bank())
        p3 = pb_.v(lambda a: a[:, 0:512].rearrange("p (c d) -> p c d", c=4))
        for hh in range(4):
            P.tr(p3[:, hh, :], og[:, hh, :], identb)
        P.copy("dve", ogT3[:, :, n * 128:(n + 1) * 128], p3)

    def gdn_sample_state(n, Rf, H):
        e_last = H["e_last"]
        vb, qdT, qkT, kdec = H["vb"], H["qdT"], H["qkT"], H["kdec"]
        P.tt("pool", dgb, bc_m(identb), bc_h(e_last), ALU.mult)
        rbl = bank()
        P.mm(rbl, onesb, flat(dgb))
        elrb = Ug
        P.copy("act", flat(elrb), rbl)
        vT_ps, oT_ps = bank(pin=True), bank(pin=True)
        vT3, oT3 = r4(vT_ps), r4(oT_ps)
        for hh in range(4):
            P.mm(vT3[:, hh, :], vb[:, hh, :], Rf[:, hh, :], start=(hh == 0), stop=False)
        sbufs = [Sbf3, og]
        first = True
        for b_ in range(16):
            Sb = sbufs[b_ % 2]
            P.dma("pool", Sb, st_gdn[b_].rearrange("h k v -> k h v"))
            cs_ = slice(b_ * 8, (b_ + 1) * 8)
            for hh in range(4):
                P.mm(vT3[:, hh, cs_], Sb[:, hh, :], negwT[:, hh, cs_], start=False, stop=(b_ == 15 and hh == 3))
            for hh in range(4):
                P.mm(oT3[:, hh, cs_], Sb[:, hh, :], qdT[:, hh, cs_], start=first, stop=False)
                first = False
        vT_sb = Pm[0]
        P.copy("act", flat(vT_sb), vT_ps)
        pb_ = bf(bank())
        p3 = pb_.v(lambda a: a[:, 0:512].rearrange("p (c d) -> p c d", c=4))
        for hh in range(4):
            P.tr(p3[:, hh, :], vT_sb[:, hh, :], identb)
        P.copy("act", flat(vnew), flat(p3))
        for hh in range(4):
            P.mm(oT3[:, hh, :], vnew[:, hh, :], qkT[:, hh, :], start=False, stop=(hh == 3))
        P.copy("act", flat(otmp), oT_ps)
        unpin_all()
        o_ps = bank()
        o3 = r4(o_ps)
        for hh in range(4):
            P.tr(o3[:, hh, :], otmp[:, hh, :], ident32)
        gdn_out(n, o3)
        vms = [Pm[1], PT[0]]
        sin = [S32_3, R32]
        sout = [stmp, osq]
        for b_ in range(16):
            vm = vms[b_ % 2]
            P.ts("dve", vm, vnew, c32(C_ONES)[:, b_ * 8:b_ * 8 + 1])
            s_ps = bank()
            s3 = r4(s_ps)
            for hh in range(4):
                P.mm(s3[:, hh, :], kdec[:, hh, :], vm[:, hh, :])
            si, so = sin[b_ % 2], sout[b_ % 2]
            P.dma("sp", si, st_gdn[b_].rearrange("h k v -> k h v"))
            P.tt("pool", so, si, elrb[:, :, b_ * 8:b_ * 8 + 1].v(lambda a: a.to_broadcast([128, 4, 128])), ALU.mult)
            P.tt("dve", so, so, s3, ALU.add)
            P.dma("sp", sg_s[b_].rearrange("h k v -> k h v"), so)

    def ple_final(blk, plan):
        nt = blk["nt"]
        t = nt * 128
        samp = blk["kind"] == "s"
        psrc = p_s if samp else p_p
        ydst = y_s if samp else y_p
        t0 = blk["t0"]
        pst3 = pst.v(lambda a: a.rearrange("p (n d) -> p n d", n=NT))
        pstb3 = pstb.v(lambda a: a.rearrange("p (n d) -> p n d", n=NT))
        pT3 = pT.v(lambda a: a.rearrange("p (c t) -> p c t", c=2))
        P.dma("sp", pst3[:, :nt, :], psrc[t0:t0 + t, :].rearrange("(n p) d -> p n d", p=128))
        P.copy("pool", pstb3[:, :nt, :], pst3[:, :nt, :])
        for c in range(2):
            pb_ = bf(bank())
            for n in range(nt):
                P.tr(pb_[:, n * 128:(n + 1) * 128], pstb3[:, n, c * 128:(c + 1) * 128], identb)
            P.copy("act", pT3[:, c, :t], pb_[:, :t])
        Wg = [k8(ws.get(plan["ple"][i], AHEAD, lo=plan["ple"][0]), 512) for i in range(2)]
        Wp = ws.get(plan["ple"][2], AHEAD, lo=plan["ple"][0]).v(lambda a: a[:, :2048].rearrange("p (c n) -> p c n", c=2))
        k = 0
        for half in range(2):
            for n in range(nt):
                pg_, pp_ = bank(), bank()
                for kc in range(8):
                    P.mm(pg_, xnT3[:, kc, n * 128:(n + 1) * 128], Wg[half][:, kc, :], start=(kc == 0), stop=(kc == 7))
                for c in range(2):
                    P.mm(pp_, pT3[:, c, n * 128:(n + 1) * 128], Wp[:, c, half * 512:(half + 1) * 512], start=(c == 0), stop=(c == 1))
                sb_ = sig[k % 2]
                k += 1
                P.act(sb_, pg_, AF.Sigmoid)
                P.tt("dve", sb_, sb_, pp_, ALU.mult)
                hv = h3[:, n, half * 512:(half + 1) * 512]
                P.tt("pool", hv, hv, sb_, ALU.add)

    def final_store(blk):
        nt = blk["nt"]
        samp = blk["kind"] == "s"
        ydst = y_s if samp else y_p
        t0 = blk["t0"]
        for n in range(nt):
            P.act(junk, h3[:, n, :], AF.Square, accum=ss[:, n:n + 1])
            P.ts("dve", rstd[:, n:n + 1], ss[:, n:n + 1], 1.0 / D, EPS, ALU.mult, ALU.add)
            P.act(rstd[:, n:n + 1], rstd[:, n:n + 1], AF.Ln)
            P.act(rstd[:, n:n + 1], rstd[:, n:n + 1], AF.Exp, scale=-0.5)
            yt = ytile[cnt["yt"] % 2]
            cnt["yt"] += 1
            P.stt("dve", yt, h3[:, n, :], rstd[:, n:n + 1], nfb, ALU.mult, ALU.mult)
            P.dma("sp", ydst[t0 + n * 128:t0 + (n + 1) * 128, :], yt)

    def load_x(blk):
        xsrc = x_s if blk["kind"] == "s" else x_p
        for n in range(blk["nt"]):
            P.dma("sp", hT[n], xsrc[blk["t0"] + n * 128:blk["t0"] + (n + 1) * 128, :])

    for bi, blk in enumerate(blocks):
        nt = blk["nt"]
        pl = plans[bi]
        if bi == 0:
            load_x(blk)
            rmsnorm_fm(gcol(0), nt)
        ffn_gu(gcol(0), pl["f1"], ffn1_down, nt, do_norm=False)
        par2(lambda: ffn_down(nt), lambda: rmsnorm_fm(gcol(1), nt))
        P.fence()
        mixing(blk, pl)
        par2(lambda: mix_wout(blk, pl), lambda: rmsnorm_fm(gcol(2), nt))
        if DBG:
            for n_ in range(NT):
                P.dma("sp", dbg_h[:, n_ * D:(n_ + 1) * D], hT[n_])
        P.fence()
        ffn_gu(gcol(2), pl["f2"], ffn2_down, nt, do_norm=False)
        par2(lambda: ffn_down(nt), lambda: rmsnorm_fm(gcol(3), nt))
        P.fence()
        ple_final(blk, pl)
        if bi + 1 < len(blocks):
            nb_ = blocks[bi + 1]

            def nxt():
                load_x(nb_)
                rmsnorm_fm(gcol(0), nb_["nt"])
            par2(lambda: final_store(blk), nxt)
        else:
            final_store(blk)
        P.fence()

    P.emit()
    es.close()
    return nc


_NC_CACHE = {}


def kernel(x_prompt, x_sample, state_gdn, state_conv, cache_swa_k, cache_swa_v, p_prompt, p_sample,
           norm_ffn1, ffn1_gu, ffn1_down, norm_mix, w_in, conv_w, a_log, dt_bias, gdn_norm, sinks,
           w_out, norm_ffn2, ffn2_gu, ffn2_down, norm_ple, ple_proj, ple_gate, norm_final):
    f = lambda a: np.ascontiguousarray(np.asarray(a), dtype=np.float32)
    x_prompt, x_sample, state_gdn, state_conv = f(x_prompt), f(x_sample), f(state_gdn), f(state_conv)
    cache_swa_k, cache_swa_v, p_prompt, p_sample = f(cache_swa_k), f(cache_swa_v), f(p_prompt), f(p_sample)
    nc = bass.Bass("TRN2", target_bir_lowering=False)
    build(nc, dict(nseq=2, seq=2048, sample=True))
    smalls = make_smalls(f(norm_ffn1)[0], f(norm_mix)[0], f(norm_ffn2)[0], f(norm_ple)[0], f(conv_w)[0], f(a_log)[0],
                         f(dt_bias)[0], f(gdn_norm)[0], f(sinks)[0], f(norm_final))
    consts = make_consts()
    shared = dict(ffn1_gu=f(ffn1_gu)[0], ffn1_down=f(ffn1_down)[0], w_in=f(w_in)[0], w_out=f(w_out)[0],
                  ffn2_gu=f(ffn2_gu)[0], ffn2_down=f(ffn2_down)[0], ple_proj=f(ple_proj)[0], ple_gate=f(ple_gate)[0],
                  smalls=smalls, consts=consts)
    in_maps = []
    for c in range(NCORES):
        sp, ss_ = slice(2 * c, 2 * c + 2), slice(16 * c, 16 * c + 16)
        m = dict(shared)
        m.update(x_p=f(x_prompt[sp].reshape(4096, 1024)), p_p=f(p_prompt[0, sp].reshape(4096, 256)),
                 x_s=f(x_sample[ss_].reshape(128, 1024)), p_s=f(p_sample[0, ss_].reshape(128, 256)),
                 st_gdn=f(state_gdn[0, ss_]), st_conv=f(state_conv[0, ss_].reshape(48, 1536)),
                 ck=f(cache_swa_k[0, ss_].reshape(16, 128, 128)), cv=f(cache_swa_v[0, ss_].reshape(16, 128, 128)))
        in_maps.append(m)
    res = run_bass_kernel_spmd(nc, in_maps, core_ids=list(range(NCORES)))
    R_ = res.results
    cat = lambda k, shp: np.concatenate([np.asarray(r[k], dtype=np.float32).reshape(shp) for r in R_], axis=0)
    y_prompt = cat("y_p", (2, 2048, 1024))
    y_sample = cat("y_s", (16, 8, 1024))
    sgp = cat("sg_p", (2, 4, 128, 128))[None]
    scp = cat("sc_p", (2, 3, 1536))[None]
    kkp = cat("kk_p", (2, 128, 2, 64))[None]
    vvp = cat("vv_p", (2, 128, 2, 64))[None]
    sgs = cat("sg_s", (16, 4, 128, 128))[None]
    scs = cat("sc_s", (16, 3, 1536))[None]
    kks = cat("kk_s", (16, 128, 2, 64))[None]
    vvs = cat("vv_s", (16, 128, 2, 64))[None]
    return (y_prompt, y_sample, sgp, scp, kkp, vvp, sgs, scs, kks, vvs)
```
